# Optimizing a Trainium2 kernel written in Bass

```python
import jax
import jax.numpy as jnp
from jax import lax
import numpy as np

D_MODEL = 2048
BATCH = 32
SEQ = 256
DEPTH = 4
DEC_BATCH = 8
DEC_SEQ = 4096
PAST_LEN = 256

GRID_W = 64
N_EVEN = (DEPTH + 1) // 2
N_ODD = DEPTH // 2
MIX_A = D_MODEL // 2
MIX_B = D_MODEL // 2
A_KEY_HEAD = 128
A_HEADS = MIX_A // A_KEY_HEAD
A_VAL_HEAD = MIX_A // A_HEADS
B_HEADS = 8
B_KEY_HEAD = MIX_B // B_HEADS
B_VAL_HEAD = MIX_B // B_HEADS
A_CHUNK = 16
B_CHUNK = 64
EVEN_IN = 5 * MIX_A + 4 * MIX_B
RET_GAMMA_EXP0 = 5
ROPE_BASE = 10000.0
LB_FLOOR = 1e-30
LRU_WIDTH = D_MODEL
LRU_BLOCKS = 16
LRU_BLOCK = LRU_WIDTH // LRU_BLOCKS
LRU_C = 8.0
CONV_W = 4
CONV_LEFT = 2
D_FF = -(-(8 * D_MODEL) // (3 * 256)) * 256
NORM_EPS = 1e-6

kernel_name = "hybrid_hgrn2_retention_rglru_prefix_step"


def rmsnorm(x, gain):
    xf = x.astype(jnp.float32)
    y = xf * lax.rsqrt(jnp.mean(xf * xf, axis=-1, keepdims=True) + NORM_EPS)
    return (y * gain.astype(jnp.float32)).astype(x.dtype)


def head_rmsnorm(o):
    return o * lax.rsqrt(jnp.mean(o * o, axis=-1, keepdims=True) + NORM_EPS)


def adaln_modulation(cond, w, b):
    m = jax.nn.silu(cond) @ w + b
    return jnp.split(m[..., None, :], 6, axis=-1)


def swiglu(h, wg, wu, wd):
    return (jax.nn.silu(h @ wg) * (h @ wu)) @ wd


def grid_positions(length):
    n_rows = length // GRID_W
    rows = jnp.repeat(jnp.arange(n_rows, dtype=jnp.float32), GRID_W)
    cols = jnp.tile(jnp.arange(GRID_W, dtype=jnp.float32), n_rows)
    return rows, cols


def rope_2d(x, rows, cols):
    k_dim = x.shape[-1]
    half, quarter = k_dim // 2, k_dim // 4
    inv_freq = ROPE_BASE ** (-jnp.arange(quarter, dtype=jnp.float32) / quarter)

    def rotate(xp, pos):
        ang = pos[:, None] * inv_freq
        cos = jnp.cos(ang)[None, :, None, :]
        sin = jnp.sin(ang)[None, :, None, :]
        x1, x2 = xp[..., :quarter], xp[..., quarter:]
        return jnp.concatenate([x1 * cos - x2 * sin, x1 * sin + x2 * cos], axis=-1)

    return jnp.concatenate([rotate(x[..., :half], rows), rotate(x[..., half:], cols)], axis=-1)


def chunk_gated_linear(q, k, v, log_f, s0, chunk):
    f32 = jnp.float32
    bsz, length, heads, kd = q.shape
    vd = v.shape[-1]
    n = length // chunk
    qc = q.astype(f32).reshape(bsz, n, chunk, heads, kd)
    kc = k.astype(f32).reshape(bsz, n, chunk, heads, kd)
    vc = v.astype(f32).reshape(bsz, n, chunk, heads, vd)
    gc = log_f.astype(f32).reshape(log_f.shape[0], n, chunk, heads, log_f.shape[-1])
    b = jnp.cumsum(gc, axis=2)
    b_last = b[:, :, -1:]
    q_in = qc * jnp.exp(b)
    k_st = kc * jnp.exp(b_last - b)
    causal = jnp.tril(jnp.ones((chunk, chunk), dtype=bool))
    diff = b[:, :, :, None] - b[:, :, None, :]
    decay = jnp.where(causal[:, :, None, None], jnp.exp(jnp.minimum(diff, 0.0)), 0.0)
    if log_f.shape[-1] == 1:
        scores = jnp.einsum('bnthk,bnshk->bnhts', qc, kc) * jnp.moveaxis(decay[..., 0], 4, 2)
    else:
        scores = jnp.einsum('bnthk,bnshk,bntshk->bnhts', qc, kc, decay)
    o_intra = jnp.einsum('bnhts,bnshv->bnthv', scores, vc)
    chunk_decay = jnp.exp(b_last[:, :, 0])

    def step(state, inp):
        qi, ks, vv, dec = inp
        o_inter = jnp.einsum('bthk,bhkv->bthv', qi, state)
        state = dec[..., None] * state + jnp.einsum('bshk,bshv->bhkv', ks, vv)
        return state, o_inter

    xs = (jnp.moveaxis(q_in, 1, 0), jnp.moveaxis(k_st, 1, 0),
          jnp.moveaxis(vc, 1, 0), jnp.moveaxis(chunk_decay, 1, 0))
    s_last, o_inter = lax.scan(step, s0.astype(f32), xs)
    o = o_intra + jnp.moveaxis(o_inter, 0, 1)
    return o.reshape(bsz, length, heads, vd), s_last


def directional_chunk_scan(q, k, v, log_f, s0, chunk, reverse):
    if reverse:
        q, k, v, log_f = [jnp.flip(t, axis=1) for t in (q, k, v, log_f)]
    o, s = chunk_gated_linear(q, k, v, log_f, s0, chunk)
    return (jnp.flip(o, axis=1) if reverse else o), s


def hgrn_lower_bounds(lb_logits):
    p = jax.nn.softmax(lb_logits.astype(jnp.float32), axis=0)
    return jnp.cumsum(p, axis=0) - p[0:1]


def even_mixer(h, w_in, w_out, lb, decay_logit, s_a0, s_b0, pos):
    f32 = jnp.float32
    bsz, length, _ = h.shape
    proj = jnp.einsum('bld,de->ble', h, w_in)
    bounds = [MIX_A * j for j in range(1, 6)] + [5 * MIX_A + MIX_B * j for j in range(1, 4)]
    q_a, fz_f, fz_b, i_a, g_a, q_b, k_b, v_b, g_b = jnp.split(proj, bounds, axis=-1)

    qa = (jax.nn.silu(q_a.astype(f32)) * A_KEY_HEAD ** -0.5).reshape(bsz, length, A_HEADS, A_KEY_HEAD)
    va = i_a.astype(f32).reshape(bsz, length, A_HEADS, A_VAL_HEAD)
    outs_a, fin_a = [], []
    for d, fz in enumerate((fz_f, fz_b)):
        z = fz.astype(f32).reshape(bsz, length, A_HEADS, A_KEY_HEAD)
        lbd = lb[d].reshape(A_HEADS, A_KEY_HEAD)
        log_f = jnp.logaddexp(jnp.log(jnp.maximum(lbd, LB_FLOOR)),
                              jnp.log1p(-lbd) + jax.nn.log_sigmoid(z))
        ka = (1.0 - lbd) * jax.nn.sigmoid(-z)
        o, s = directional_chunk_scan(qa, ka, va, log_f, s_a0[:, d], A_CHUNK, d == 1)
        outs_a.append(o)
        fin_a.append(s)
    o_a = head_rmsnorm(outs_a[0] + outs_a[1]).reshape(bsz, length, MIX_A) * jax.nn.silu(g_a.astype(f32))

    qb = q_b.astype(f32).reshape(bsz, length, B_HEADS, B_KEY_HEAD)
    kb = k_b.astype(f32).reshape(bsz, length, B_HEADS, B_KEY_HEAD)
    if pos is not None:
        qb = rope_2d(qb, pos[0], pos[1])
        kb = rope_2d(kb, pos[0], pos[1])
    qb = qb * B_KEY_HEAD ** -0.5
    vb = v_b.astype(f32).reshape(bsz, length, B_HEADS, B_VAL_HEAD)
    outs_b, fin_b = [], []
    for d in range(2):
        log_g = jax.nn.log_sigmoid(decay_logit[d].astype(f32))[None, None, :, None]
        log_g = jnp.broadcast_to(log_g, (1, length, B_HEADS, 1))
        o, s = directional_chunk_scan(qb, kb, vb, log_g, s_b0[:, d], B_CHUNK, d == 1)
        outs_b.append(o)
        fin_b.append(s)
    o_b = head_rmsnorm(outs_b[0] + outs_b[1]).reshape(bsz, length, MIX_B) * jax.nn.silu(g_b.astype(f32))

    y = jnp.concatenate([o_a, o_b], axis=-1).astype(h.dtype) @ w_out
    return y, jnp.stack(fin_a, axis=1), jnp.stack(fin_b, axis=1)


def centred_dwconv(x, w, b):
    length = x.shape[1]
    xp = jnp.pad(x, ((0, 0), (CONV_LEFT, CONV_W - 1 - CONV_LEFT), (0, 0)))
    y = b + xp[:, 0:length] * w[0]
    for j in range(1, CONV_W):
        y = y + xp[:, j:j + length] * w[j]
    return y


def block_diag_linear(x, w, b):
    xb = x.reshape(x.shape[0], x.shape[1], LRU_BLOCKS, LRU_BLOCK)
    y = jnp.einsum('blni,nij->blnj', xb, w) + b
    return y.reshape(x.shape)


def linear_scan(a, u, h0):
    def combine(e1, e2):
        a1, b1 = e1
        a2, b2 = e2
        return a1 * a2, a2 * b1 + b2

    a_cum, b_cum = lax.associative_scan(combine, (a, u), axis=1)
    h = a_cum * h0[:, None, :] + b_cum
    return h, h[:, -1]


def odd_mixer(h, w_in, conv_w, conv_b, w_a, b_a, w_i, b_i, a_param, w_out, h0):
    f32 = jnp.float32
    proj = h @ w_in
    gate_br, x_br = jnp.split(proj, 2, axis=-1)
    xc = centred_dwconv(x_br, conv_w, conv_b).astype(f32)
    outs, fins = [], []
    for d in range(2):
        xd = jnp.flip(xc, axis=1) if d == 1 else xc
        r = jax.nn.sigmoid(block_diag_linear(xd, w_a[d], b_a[d]))
        gi = jax.nn.sigmoid(block_diag_linear(xd, w_i[d], b_i[d]))
        log_a = -LRU_C * r * jax.nn.softplus(-a_param[d].astype(f32))
        mult = jnp.sqrt(jnp.maximum(-jnp.expm1(2.0 * log_a), 0.0))
        hs, hl = linear_scan(jnp.exp(log_a), mult * gi * xd, h0[:, d].astype(f32))
        outs.append(jnp.flip(hs, axis=1) if d == 1 else hs)
        fins.append(hl)
    y = (outs[0] + outs[1]) * jax.nn.gelu(gate_br.astype(f32))
    return y.astype(h.dtype) @ w_out, jnp.stack(fins, axis=1)


def setup_inputs(seed: int = 0) -> dict:
    key = jax.random.key(seed)
    ks = jax.random.split(key, 32)
    f32 = jnp.float32

    def nrm(k, shape, s):
        return s * jax.random.normal(k, shape, f32)

    gam = 1.0 - 2.0 ** (-RET_GAMMA_EXP0 - np.arange(B_HEADS, dtype=np.float32))
    gam_logit = jnp.asarray(np.log(gam) - np.log1p(-gam), f32)
    u = jax.random.uniform(ks[20], (N_ODD, 2, LRU_WIDTH), f32, minval=0.81, maxval=0.998)
    a0 = u ** (1.0 / LRU_C)
    return {
        "x_prompt": nrm(ks[0], (BATCH, SEQ, D_MODEL), 1.0),
        "x_sample": nrm(ks[1], (DEC_BATCH, DEC_SEQ, D_MODEL), 1.0),
        "state_hgrn": nrm(ks[2], (DEC_BATCH, N_EVEN, 2, A_HEADS, A_KEY_HEAD, A_VAL_HEAD), 1.0),
        "state_ret": nrm(ks[3], (DEC_BATCH, N_EVEN, 2, B_HEADS, B_KEY_HEAD, B_VAL_HEAD), 1.0),
        "state_rglru": nrm(ks[4], (DEC_BATCH, N_ODD, 2, LRU_WIDTH), 0.5),
        "c": nrm(ks[5], (DEC_BATCH, D_MODEL), 1.0),
        "c_ctx": nrm(ks[6], (D_MODEL,), 1.0),
        "w_mod": nrm(ks[7], (DEPTH, D_MODEL, 6 * D_MODEL), 0.5 * D_MODEL ** -0.5),
        "b_mod": nrm(ks[8], (DEPTH, 6 * D_MODEL), 0.01),
        "norm_mix": 1.0 + nrm(ks[9], (DEPTH, D_MODEL), 0.01),
        "norm_ffn": 1.0 + nrm(ks[10], (DEPTH, D_MODEL), 0.01),
        "w_in_even": nrm(ks[11], (N_EVEN, D_MODEL, EVEN_IN), D_MODEL ** -0.5),
        "hgrn_lb_logits": nrm(ks[12], (N_EVEN, 2, MIX_A), 1.0),
        "ret_decay_logit": gam_logit[None, None, :] + nrm(ks[13], (N_EVEN, 2, B_HEADS), 0.01),
        "w_out_even": nrm(ks[14], (N_EVEN, MIX_A + MIX_B, D_MODEL), (MIX_A + MIX_B) ** -0.5),
        "w_in_odd": nrm(ks[15], (N_ODD, D_MODEL, 2 * LRU_WIDTH), D_MODEL ** -0.5),
        "conv_w": nrm(ks[16], (N_ODD, CONV_W, LRU_WIDTH), CONV_W ** -0.5),
        "conv_b": nrm(ks[17], (N_ODD, LRU_WIDTH), 0.01),
        "rg_w_a": nrm(ks[18], (N_ODD, 2, LRU_BLOCKS, LRU_BLOCK, LRU_BLOCK), LRU_BLOCK ** -0.5),
        "rg_b_a": nrm(ks[19], (N_ODD, 2, LRU_BLOCKS, LRU_BLOCK), 0.01),
        "rg_w_i": nrm(ks[21], (N_ODD, 2, LRU_BLOCKS, LRU_BLOCK, LRU_BLOCK), LRU_BLOCK ** -0.5),
        "rg_b_i": nrm(ks[22], (N_ODD, 2, LRU_BLOCKS, LRU_BLOCK), 0.01),
        "rg_a_param": jnp.log(a0) - jnp.log1p(-a0),
        "w_out_odd": nrm(ks[23], (N_ODD, LRU_WIDTH, D_MODEL), LRU_WIDTH ** -0.5),
        "w_ffn_gate": nrm(ks[24], (DEPTH, D_MODEL, D_FF), D_MODEL ** -0.5),
        "w_ffn_up": nrm(ks[25], (DEPTH, D_MODEL, D_FF), D_MODEL ** -0.5),
        "w_ffn_down": nrm(ks[26], (DEPTH, D_FF, D_MODEL), D_FF ** -0.5),
        "final_norm": 1.0 + nrm(ks[27], (D_MODEL,), 0.01),
    }


def reference(x_prompt, x_sample, state_hgrn, state_ret, state_rglru, c, c_ctx,
              w_mod, b_mod, norm_mix, norm_ffn, w_in_even, hgrn_lb_logits, ret_decay_logit,
              w_out_even, w_in_odd, conv_w, conv_b, rg_w_a, rg_b_a, rg_w_i, rg_b_i,
              rg_a_param, w_out_odd, w_ffn_gate, w_ffn_up, w_ffn_down, final_norm):
    f32 = jnp.float32
    lower_bounds = hgrn_lower_bounds(hgrn_lb_logits)

    def trunk(x, cond, s_hgrn, s_ret, s_lru, pos):
        fin_hgrn, fin_ret, fin_lru = [], [], []
        for l in range(DEPTH):
            sh1, sc1, g1, sh2, sc2, g2 = adaln_modulation(cond, w_mod[l], b_mod[l])
            h = rmsnorm(x, norm_mix[l]) * (1.0 + sc1) + sh1
            if l % 2 == 0:
                le = l // 2
                y, sa, sb = even_mixer(h, w_in_even[le], w_out_even[le], lower_bounds[le],
                                       ret_decay_logit[le], s_hgrn[:, le], s_ret[:, le], pos)
                fin_hgrn.append(sa)
                fin_ret.append(sb)
            else:
                lo = l // 2
                y, sc = odd_mixer(h, w_in_odd[lo], conv_w[lo], conv_b[lo], rg_w_a[lo], rg_b_a[lo],
                                  rg_w_i[lo], rg_b_i[lo], rg_a_param[lo], w_out_odd[lo], s_lru[:, lo])
                fin_lru.append(sc)
            x = x + (g1 * y).astype(x.dtype)
            h = rmsnorm(x, norm_ffn[l]) * (1.0 + sc2) + sh2
            x = x + (g2 * swiglu(h, w_ffn_gate[l], w_ffn_up[l], w_ffn_down[l])).astype(x.dtype)
        return (rmsnorm(x, final_norm), jnp.stack(fin_hgrn, axis=1),
                jnp.stack(fin_ret, axis=1), jnp.stack(fin_lru, axis=1))

    b_ctx = x_prompt.shape[0]
    zero_hgrn = jnp.zeros((b_ctx, N_EVEN, 2, A_HEADS, A_KEY_HEAD, A_VAL_HEAD), f32)
    zero_ret = jnp.zeros((b_ctx, N_EVEN, 2, B_HEADS, B_KEY_HEAD, B_VAL_HEAD), f32)
    zero_lru = jnp.zeros((b_ctx, N_ODD, 2, LRU_WIDTH), f32)
    y_prompt, new_state_hgrn, new_state_ret, new_state_rglru = trunk(
        x_prompt, c_ctx, zero_hgrn, zero_ret, zero_lru, None)

    pos = grid_positions(x_sample.shape[1])
    y_sample, _, _, _ = trunk(x_sample, c, state_hgrn, state_ret, state_rglru, pos)

    return (y_prompt, y_sample, new_state_hgrn, new_state_ret, new_state_rglru)
```

```python
import numpy as np
import ml_dtypes
from contextlib import ExitStack
import concourse.bass as bass
import concourse.mybir as mybir
from concourse.bass_utils import run_bass_kernel_spmd

F32 = mybir.dt.float32
BF16 = mybir.dt.bfloat16
AF = mybir.ActivationFunctionType
ALU = mybir.AluOpType
D = 2048
NCH = 16
DFF = 5632
NFC = 44
TT = 512
EPS = 1e-6
KS = 128 ** -0.5


class Buf:
    def __init__(self, name, arena=None, lo=0, hi=0, const=False):
        self.name, self.arena, self.lo, self.hi, self.const = name, arena, lo, hi, const
        self.lw = None
        self.rd = {}
        self.ov = [self]


class Op:
    __slots__ = ("eng", "fn", "reads", "writes", "dma", "ndma", "sig", "sigval", "waits", "dval")


class Prog:
    def __init__(self):
        self.ops = []
        self.arena_bufs = {}

    def buf(self, name, arena=None, lo=0, hi=0, const=False):
        b = Buf(name, arena, lo, hi, const)
        if arena is not None:
            lst = self.arena_bufs.setdefault(arena, [])
            for o in lst:
                if o.lo < hi and lo < o.hi:
                    o.ov.append(b)
                    b.ov.append(o)
            lst.append(b)
        return b

    def op(self, eng, fn, reads=(), writes=(), dma=None, ndma=1):
        if fn.__defaults__ is not None and fn.__code__.co_argcount == len(fn.__defaults__):
            fn = fn()
        o = Op()
        o.eng, o.fn, o.reads, o.writes, o.dma, o.ndma = eng, fn, list(reads), list(writes), dma, ndma
        o.sig, o.sigval, o.waits, o.dval = False, 0, [], 0
        self.ops.append(o)
        return o

    NSD = 96

    def gid(self, slot, eng):
        d = self.gids.setdefault(eng, {})
        n = 20 if eng == "sp" else 4
        g = d.setdefault(id(slot), len(d) % n)
        return (eng, g)

    def resolve(self):
        self.gids = {}
        for o in self.ops:
            if o.dma is not None:
                o.dma = self.gid(o.dma, o.eng)
        dcnt = {}
        for o in self.ops:
            deps = {}
            for b in o.reads:
                for ob in b.ov:
                    if ob.lw is not None:
                        deps[id(ob.lw)] = ob.lw
            for b in o.writes:
                for ob in b.ov:
                    if ob.lw is not None:
                        deps[id(ob.lw)] = ob.lw
                    for r in ob.rd.values():
                        deps[id(r)] = r
            deps.pop(id(o), None)
            for p in deps.values():
                if p.dma is not None:
                    o.waits.append((("d", p.dma), dcnt[p.dma] * 16))
                else:
                    if p.eng == o.eng == "pe" and o.dma is None:
                        continue
                    p.sig = True
                    o.waits.append((("e", p.eng), p))
            if o.dma is not None:
                dcnt[o.dma] = dcnt.get(o.dma, 0) + o.ndma
            key = o.eng if o.dma is None else ("d", o.dma)
            for b in o.reads:
                if not b.const:
                    b.rd[key] = o
            for b in o.writes:
                for ob in b.ov:
                    ob.lw = o
                    ob.rd = {}
        cnt = {}
        for o in self.ops:
            if o.dma is None and o.sig:
                cnt[o.eng] = cnt.get(o.eng, 0) + 1
                o.sigval = cnt[o.eng]
        for o in self.ops:
            o.waits = [(k, (v.sigval if isinstance(v, Op) else v)) for k, v in o.waits]

    def dma_slots(self):
        return sorted(set(o.dma for o in self.ops if o.dma is not None))

    def emit(self, eng_name, e, sems):
        waited = {}
        for o in self.ops:
            if o.eng != eng_name:
                continue
            for k, v in o.waits:
                if waited.get(k, 0) < v:
                    e.wait_ge(sems[k], v)
                    waited[k] = v
            r = o.fn(e)
            if o.dma is not None:
                rl = r if isinstance(r, (list, tuple)) else [r]
                assert len(rl) == o.ndma, (len(rl), o.ndma)
                for ins in rl:
                    ins.then_inc(sems[("d", o.dma)], 16)
            elif o.sig:
                last = r[-1] if isinstance(r, (list, tuple)) else r
                last.then_inc(sems[("e", o.eng)], 1)


def build(LS, NP, LP=256, stop=None, dump=(), lite=False, skip=()):
    T = LS + NP * LP
    NT = T // TT
    NB = T // 128
    NTS = LS // TT
    seqs = [(0, LS, None)] + [(LS + i * LP, LP, i) for i in range(NP)]
    nc = bass.Bass("TRN2", target_bir_lowering=False)
    P = Prog()
    es = ExitStack()

    def din(name, shape, dt=F32):
        return nc.dram_tensor(name, list(shape), dt, kind="ExternalInput").ap()

    def dout(name, shape, dt=F32):
        return nc.dram_tensor(name, list(shape), dt, kind="ExternalOutput").ap()

    def dscr(name, shape, dt):
        kind = "ExternalOutput" if name in dump else "Internal"
        return nc.dram_tensor(name, list(shape), dt, kind=kind).ap()

    xT_in = din("xT", [128, NCH, T])
    condT_in = din("condT", [128, NCH, 2])
    st_h_in = din("st_h", [2, 2, 8, 128, 128])
    st_r_in = din("st_r", [2, 2, 8, 128, 128])
    h0T_in = din("h0T", [128, 2, 2, NCH])
    _din = din

    def din(name, shape, dt=F32):
        if lite and name.startswith("w_") and name != "w_in_even":
            return None
        return _din(name, shape, dt)
    w_mod = din("w_mod", [4, D, 6 * D])
    w_in_even = din("w_in_even", [2, D, 9216])
    w_out_even = din("w_out_even", [2, D, D])
    w_in_odd = din("w_in_odd", [2, D, 2 * D])
    w_out_odd = din("w_out_odd", [2, D, D])
    w_g = din("w_ffn_gate", [4, D, DFF])
    w_u = din("w_ffn_up", [4, D, DFF])
    w_d = din("w_ffn_down", [4, DFF, D])
    din = _din
    rg_w_a = din("rg_w_a", [2, 2, 16, 128, 128])
    rg_w_i = din("rg_w_i", [2, 2, 16, 128, 128])
    bmodT_in = din("bmodT", [128, 4, 96])
    nmixT_in = din("nmixT", [128, 4, NCH])
    nffnT_in = din("nffnT", [128, 4, NCH])
    fnormT_in = din("fnormT", [128, NCH])
    lblT_in = din("lblT", [128, 2, 2, 8])
    decl_in = din("decl", [1, 32])
    convwT_in = din("convwT", [128, 2, 4, NCH])
    convbT_in = din("convbT", [128, 2, NCH])
    rgbaT_in = din("rgbaT", [128, 2, 2, NCH])
    rgbiT_in = din("rgbiT", [128, 2, 2, NCH])
    aparT_in = din("aparT", [128, 2, 2, NCH])
    c_ident = din("c_ident", [128, 128], BF16)
    c_ones = din("c_ones", [128, 128], BF16)
    c_perm = din("c_perm", [128, 128], BF16)
    c_maskf = din("c_maskf", [128, 128], BF16)
    c_maskb = din("c_maskb", [128, 128], BF16)
    c_reset = din("c_reset", [128, TT])
    c_pos = din("c_pos", [128, 2, 32])
    c_cos = din("c_cos", [128, LS])
    c_sin = din("c_sin", [128, LS])
    yT_out = dout("yT", [128, NCH, T])
    nsh_out = dout("nsh", [NP, 2, 2, 8, 128, 128])
    nsr_out = dout("nsr", [NP, 2, 2, 8, 128, 128])
    nsl_out = dout("nsl", [128, NP, 2, 2, NCH])
    XT = dscr("XT", [128, NCH, T], F32)
    QF = dscr("QF", [NB, 128, 16, 128], BF16)
    QB = dscr("QB", [NB, 128, 16, 128], BF16)
    KF = dscr("KF", [NB, 128, 16, 128], BF16)
    KB = dscr("KB", [NB, 128, 16, 128], BF16)
    SF = dscr("SF", [NB, 128, 16, 128], BF16)
    SB = dscr("SB", [NB, 128, 16, 128], BF16)
    VV = dscr("VV", [NB, 128, 16, 128], BF16)
    DEC = dscr("DEC", [NT, 128, 16, 2, 16], F32)
    GG = dscr("GG", [128, NCH, T], BF16)
    OF = dscr("OF", [128, NCH, T], F32)
    OB = dscr("OB", [128, NCH, T], F32)
    XBR = dscr("XBR", [128, NCH, T], F32)
    HF = dscr("HF", [128, NCH, T], F32)
    YO = dscr("YO", [128, NCH, T], BF16)

    dbufs = {}

    def db(name, idx=0):
        k = (name, idx)
        if k not in dbufs:
            dbufs[k] = P.buf("%s_%s" % (name, idx))
        return dbufs[k]

    def sb(name, shape, dt, const=False):
        t = es.enter_context(nc.sbuf_tensor(name, list(shape), dt))
        return t, P.buf(name, const=const)

    ident, b_ident = sb("ident", [128, 128], BF16, True)
    ones, b_ones = sb("ones", [128, 128], BF16, True)
    perm, b_perm = sb("perm", [128, 128], BF16, True)
    maskf, b_maskf = sb("maskf", [128, 128], BF16, True)
    maskb, b_maskb = sb("maskb", [128, 128], BF16, True)
    resetm, b_resetm = sb("resetm", [128, TT], F32, True)
    posc, b_posc = sb("posc", [128, 2, 32], F32, True)
    epsc, b_epsc = sb("epsc", [128, 4], F32, True)
    condT, b_condT = sb("condT_s", [128, NCH, 2], F32)
    scond, b_scond = sb("scond", [128, NCH, 2], BF16)
    modT, b_modT = sb("modT", [128, 4, 2, 96], F32)
    bmodT, b_bmodT = sb("bmodT_s", [128, 4, 96], F32)
    nmix, b_nmix = sb("nmix", [128, 4, NCH], F32)
    nffn, b_nffn = sb("nffn", [128, 4, NCH], F32)
    fnorm, b_fnorm = sb("fnorm", [128, NCH], F32)
    A1, b_A1 = sb("A1", [128, 4, 2, NCH], F32)
    A2, b_A2 = sb("A2", [128, 4, 2, NCH], F32)
    lbl, b_lbl = sb("lbl", [128, 2, 2, 8], F32)
    LB, b_LB = sb("LB", [128, 2, 2, 8], F32)
    OML, b_OML = sb("OML", [128, 2, 2, 8], F32)
    LBM, b_LBM = sb("LBM", [128, 2, 2, 8], F32)
    lng, b_lng = sb("lng", [128, 32], F32)
    nlng, b_nlng = sb("nlng", [128, 32], F32)
    EB32, b_EB32 = sb("EB32", [128, 32, 32], F32)
    ENB32, b_ENB32 = sb("ENB32", [128, 32, 32], F32)
    convw, b_convw = sb("convw", [128, 2, 4, NCH], F32)
    convb, b_convb = sb("convb", [128, 2, NCH], F32)
    rgba, b_rgba = sb("rgba", [128, 2, 2, NCH], F32)
    rgbi, b_rgbi = sb("rgbi", [128, 2, 2, NCH], F32)
    apar, b_apar = sb("apar", [128, 2, 2, NCH], F32)
    cA, b_cA = sb("cA", [128, 2, 2, NCH], F32)
    cA2, b_cA2 = sb("cA2", [128, 2, 2, NCH], F32)
    h0T, b_h0T = sb("h0T_s", [128, 2, 2, NCH], F32)
    nsl, b_nsl = sb("nsl_s", [128, NP, 2, 2, NCH], F32)
    hcar, b_hcar = sb("hcar", [128, NCH], F32)
    NW = 3
    wbufs = [sb("wbuf%d" % i, [128, 16, 512], BF16) for i in range(NW)]
    wctr = [0]
    psums = []
    for i in range(7):
        t = es.enter_context(nc.psum_tensor("ps%d" % i, [128, 512], F32))
        psums.append((t, P.buf("ps%d" % i)))
    psT = es.enter_context(nc.psum_tensor("psT", [128, 1024], BF16))
    b_psT = P.buf("psT")
    pctr = [0]

    def newps():
        r = psums[pctr[0] % 7]
        pctr[0] += 1
        return r

    ARENA = 120 * 1024
    arena = es.enter_context(nc.sbuf_tensor("arena", [128, ARENA // 4], F32))
    acur = [0]

    def areset():
        acur[0] = 0

    def aal(name, shape, dt):
        n = 1
        for s_ in shape[1:]:
            n *= s_
        nb = n * (4 if dt == F32 else 2)
        nb = (nb + 63) // 64 * 64
        lo = acur[0]
        acur[0] += nb
        assert acur[0] <= ARENA, ("arena overflow", name, acur[0])
        v = arena[:, lo // 4:(lo + nb) // 4]
        if dt != F32:
            v = v.bitcast(dt)
        v = v[:, 0:n]
        if len(shape) == 3:
            v = v.rearrange("p (a b) -> p a b", b=shape[2])
        elif len(shape) == 4:
            v = v.rearrange("p (a b c) -> p a b c", b=shape[2], c=shape[3])
        return v, P.buf(name, "arena", lo, lo + nb)

    def dma(eng, out, in_, slot, reads, writes):
        P.op(eng, lambda e: e.dma_start(out=out, in_=in_), reads, writes, dma=slot)

    def act(out, in_, func, reads, writes, scale=None, bias=None):
        kw = {}
        if scale is not None:
            kw["scale"] = scale
        if bias is not None:
            kw["bias"] = bias
        P.op("act", lambda e: e.activation(out=out, in_=in_, func=func, **kw), reads, writes)

    def tt(eng, out, in0, in1, op, reads, writes):
        P.op(eng, lambda e: e.tensor_tensor(out=out, in0=in0, in1=in1, op=op), reads, writes)

    def ts(eng, out, in0, s1, s2, op0, op1, reads, writes):
        if op1 is None:
            P.op(eng, lambda e: e.tensor_scalar(out=out, in0=in0, scalar1=s1, scalar2=None, op0=op0), reads, writes)
        else:
            P.op(eng, lambda e: e.tensor_scalar(out=out, in0=in0, scalar1=s1, scalar2=s2, op0=op0, op1=op1), reads, writes)

    def stt(out, in0, scalar, in1, op0, op1, reads, writes):
        P.op("dve", lambda e: e.scalar_tensor_tensor(out=out, in0=in0, scalar=scalar, in1=in1, op0=op0, op1=op1), reads, writes)

    def cp(eng, out, in_, reads, writes):
        if eng == "act":
            P.op("act", lambda e: e.copy(out=out, in_=in_), reads, writes)
        else:
            P.op(eng, lambda e: e.tensor_copy(out=out, in_=in_), reads, writes)

    def mm(out, lhsT, rhs, reads, writes, start=True, stop=True, tp=None):
        if tp is None:
            P.op("pe", lambda e: e.matmul(out, lhsT=lhsT, rhs=rhs, start=start, stop=stop), reads, writes)
        else:
            P.op("pe", lambda e: e.matmul(out, lhsT=lhsT, rhs=rhs, start=start, stop=stop, tile_position=tp), reads, writes)

    def mmk(out, wt, wcols, rhs3, nk, reads, writes, start=True, stop=True, k0=0):
        def fn(e):
            r = None
            for kc in range(nk):
                r = e.matmul(out, lhsT=wt[:, kc, wcols], rhs=rhs3[:, k0 + kc, :],
                             start=(start and kc == 0), stop=(stop and kc == nk - 1))
            return r
        P.op("pe", fn, reads, writes)

    for (t, b, src) in [(ident, b_ident, c_ident), (ones, b_ones, c_ones), (perm, b_perm, c_perm),
                        (maskf, b_maskf, c_maskf), (maskb, b_maskb, c_maskb), (resetm, b_resetm, c_reset),
                        (posc, b_posc, c_pos), (condT, b_condT, condT_in), (bmodT, b_bmodT, bmodT_in),
                        (nmix, b_nmix, nmixT_in), (nffn, b_nffn, nffnT_in), (fnorm, b_fnorm, fnormT_in),
                        (lbl, b_lbl, lblT_in), (convw, b_convw, convwT_in), (convb, b_convb, convbT_in),
                        (rgba, b_rgba, rgbaT_in), (rgbi, b_rgbi, rgbiT_in), (apar, b_apar, aparT_in),
                        (h0T, b_h0T, h0T_in)]:
        dma("sp", t[:], src, b, [], [b])
    dma("sp", lng[:], decl_in.partition_broadcast(128) if False else decl_in[0:1, :].to_broadcast([128, 32]), b_lng, [], [b_lng])
    P.op("dve", lambda e: e.memset(epsc[:], EPS), [], [b_epsc])
    P.op("dve", lambda e: e.memset(nsl[:], 0.0), [], [b_nsl])
    act(scond[:], condT[:], AF.Silu, [b_condT], [b_scond])
    P.op("dve", lambda e: e.memset(LB[:, 0], 0.0), [], [b_LB])
    tt("dve", lbl[:, 1], lbl[:, 1], lbl[:, 0], ALU.subtract, [b_lbl], [b_lbl])
    act(LB[:, 1], lbl[:, 1], AF.Sigmoid, [b_lbl, b_LB], [b_LB])
    ts("dve", OML[:], LB[:], -1.0, 1.0, ALU.mult, ALU.add, [b_LB], [b_OML])
    ts("dve", LBM[:], LB[:], -1.0, None, ALU.add, None, [b_LB], [b_LBM])
    act(lng[:], lng[:], AF.Sigmoid, [b_lng], [b_lng])
    act(lng[:], lng[:], AF.Ln, [b_lng], [b_lng])
    ts("dve", nlng[:], lng[:], -1.0, None, ALU.mult, None, [b_lng], [b_nlng])
    for i in range(32):
        d = (i // 8) % 2
        act(EB32[:, i, :], posc[:, d, :], AF.Exp, [b_posc, b_lng], [b_EB32], scale=lng[:, i:i + 1])
        act(ENB32[:, i, :], posc[:, d, :], AF.Exp, [b_posc, b_nlng], [b_ENB32], scale=nlng[:, i:i + 1])
    act(cA[:], apar[:], AF.Exp, [b_apar], [b_cA], scale=-1.0)
    ts("dve", cA[:], cA[:], 1.0, None, ALU.add, None, [b_cA], [b_cA])
    act(cA[:], cA[:], AF.Ln, [b_cA], [b_cA])
    ts("dve", cA2[:], cA[:], -16.0, None, ALU.mult, None, [b_cA], [b_cA2])
    ts("dve", cA[:], cA[:], -8.0, None, ALU.mult, None, [b_cA, b_cA2], [b_cA])

    wconv_slot = P.buf("wconv_slot")
    wtok = [P.buf("wtok%d" % i) for i in range(3)]
    wtokc = [0]

    class WB:
        pass

    def wprep(name, src, K, N):
        w = WB()
        w.ncg, w.nkt, w.K, w.N = N // 512, (K + 2047) // 2048, K, N
        w.t = dscr("wb_" + name, [w.ncg, w.nkt, 128, 16, 512], BF16)
        w.name, w.src, w.done = name, src, False
        return w

    def wconvert(w):
        if w.done or "conv" in skip:
            return
        w.done = True
        for cg in range(w.ncg):
            for kt in range(w.nkt):
                nk = min(16, (w.K - kt * 2048) // 128)
                s = w.src[kt * 2048:kt * 2048 + nk * 128, cg * 512:(cg + 1) * 512].rearrange("(kc p) e -> p kc e", p=128)
                o = w.t[cg, kt, :, 0:nk, :]
                tk = wtok[wtokc[0] % len(wtok)]
                wtokc[0] += 1
                dma("pool", o, s, wconv_slot, [], [db("wb_" + w.name, (cg, kt)), tk])

    def wload(w, cg, kt):
        i = wctr[0] % NW
        wctr[0] += 1
        t, b = wbufs[i]
        nk = min(16, (w.K - kt * 2048) // 128)
        dma("sp", t[:, 0:nk, :], w.t[cg, kt, :, 0:nk, :], b, [db("wb_" + w.name, (cg, kt))], [b])
        return t, b, nk

    W = {}
    for l in range(1 if lite else 4):
        if lite:
            W[("in", l)] = wprep("in%d" % l, w_in_even[l // 2], D, 9216)
            continue
        if l % 2 == 0:
            W[("in", l)] = wprep("in%d" % l, w_in_even[l // 2], D, 9216)
            W[("out", l)] = wprep("out%d" % l, w_out_even[l // 2], D, D)
        else:
            W[("in", l)] = wprep("in%d" % l, w_in_odd[l // 2], D, 2 * D)
            W[("out", l)] = wprep("out%d" % l, w_out_odd[l // 2], D, D)
        W[("g", l)] = wprep("g%d" % l, w_g[l], D, DFF)
        W[("u", l)] = wprep("u%d" % l, w_u[l], D, DFF)
        W[("d", l)] = wprep("d%d" % l, w_d[l], DFF, D)

    if lite:
        P.op("dve", lambda e: e.memset(modT[:], 0.05), [], [b_modT])
        P.op("dve", lambda e: e.memset(A1[:], 1.05), [], [b_A1])
        P.op("dve", lambda e: e.memset(A2[:], 1.05), [], [b_A2])
    for l in range(0 if lite else 4):
        pt, pb = newps()
        for cg in range(24):
            i = wctr[0] % NW
            wctr[0] += 1
            t, b = wbufs[i]
            s = w_mod[l][:, cg * 512:(cg + 1) * 512].rearrange("(kc p) e -> p kc e", p=128)
            dma("pool", t[:], s, b, [], [b])
            for j in range(4):
                ch = cg * 4 + j
                mmk(pt[:, ch * 2:ch * 2 + 2], t, slice(j * 128, (j + 1) * 128), scond, 16, [b, b_scond], [pb])
        tt("dve", modT[:, l].rearrange("p c j -> p j c"), pt[:, 0:192].rearrange("p (j c) -> p j c", c=2),
           bmodT[:, l, :].unsqueeze(2).to_broadcast([128, 96, 2]), ALU.add, [pb, b_bmodT], [b_modT])
        for c in range(2):
            stt(A1[:, l, c, :], modT[:, l, c, 16:32], 1.0, nmix[:, l, :], ALU.add, ALU.mult, [b_modT, b_nmix], [b_A1])
            stt(A2[:, l, c, :], modT[:, l, c, 64:80], 1.0, nffn[:, l, :], ALU.add, ALU.mult, [b_modT, b_nffn], [b_A2])

    def cond_of_tile(ti):
        return 0 if ti < NTS else 1

    def rmsnorm_mod(xt, b_xt, hT, b_hT, Acol, Bcol, tmp):
        (sq, b_sq), (rs, b_rs), (tm, b_tm) = tmp
        pt, pb = newps()
        for c in range(NCH):
            s_, bs_ = sq[c % 2], b_sq[c % 2]
            act(s_, xt[:, c, :], AF.Square, [b_xt], [bs_])
            mm(pt[:, :], ones[:], s_, [b_ones, bs_], [pb], start=(c == 0), stop=(c == NCH - 1))
        act(rs, pt[:, :], AF.Sqrt, [pb, b_epsc], [b_rs], scale=1.0 / D, bias=epsc[:, 0:1])
        P.op("dve", lambda e: e.reciprocal(out=rs, in_=rs), [b_rs], [b_rs])
        for c in range(NCH):
            t_, bt_ = tm[c % 2], b_tm[c % 2]
            a_, b_ = Acol(c), Bcol(c)
            stt(t_, xt[:, c, :], a_[0], rs, ALU.mult, ALU.mult, [b_xt, b_rs, a_[1]], [bt_])
            if b_ is None:
                cp("act", hT[:, c, :], t_, [bt_], [b_hT])
            else:
                act(hT[:, c, :], t_, AF.Identity, [bt_, b_[1]], [b_hT], bias=b_[0])

    def stream_T(x_src, ti):
        t0 = ti * TT
        return x_src[:, :, t0:t0 + TT]

    def phaseA_even(l, first):
        le = l // 2
        areset()
        xt, b_xt = aal("xt", [128, NCH, TT], F32)
        hT, b_hT = aal("hT", [128, NCH, TT], BF16)
        sq = [aal("sq%d" % i, [128, TT], BF16) for i in range(2)]
        rs, b_rs = aal("rs", [128, TT], F32)
        tm = [aal("tm%d" % i, [128, TT], F32) for i in range(2)]
        tmp = (([s[0] for s in sq], [s[1] for s in sq]), (rs, b_rs), ([s[0] for s in tm], [s[1] for s in tm]))
        q4, b_q4 = aal("q4", [128, 4, TT], F32)
        k4, b_k4 = aal("k4", [128, 4, TT], F32)
        R = 2
        sig = [aal("sig%d" % i, [128, TT], F32) for i in range(R)]
        lf = [aal("lf%d" % i, [128, TT], F32) for i in range(R)]
        ka = [aal("ka%d" % i, [128, TT], F32) for i in range(R)]
        bc = [aal("bc%d" % i, [128, TT], F32) for i in range(R)]
        eb = [aal("eb%d" % i, [128, TT], F32) for i in range(R)]
        enb = [aal("enb%d" % i, [128, TT], F32) for i in range(R)]
        qst = [aal("qst%d" % i, [128, TT], BF16) for i in range(R)]
        kst = [aal("kst%d" % i, [128, TT], BF16) for i in range(R)]
        kss = [aal("kss%d" % i, [128, TT], BF16) for i in range(R)]
        sst = [aal("sst%d" % i, [128, 4, 128], BF16) for i in range(R)]
        vst = [aal("vst%d" % i, [128, TT], BF16) for i in range(R)]
        gst = [aal("gst0", [128, 4, TT], BF16)] * 2
        decst, b_decst = aal("decst", [128, 16, 2, 16], F32)
        cosT, b_cosT = aal("cosT", [128, TT], F32)
        sinT, b_sinT = aal("sinT", [128, TT], F32)
        rb, r1, r2 = vst, sig, lf
        rot = [0]
        w = W[("in", l)]
        x_src = xT_in if first else XT

        def emit_hd(ti, hh, d, qsrc, bq, ksrc, bk, ebv, benb_r, enbv, decv, decr):
            if "hd" in skip:
                return
            i = rot[0] % R
            rot[0] += 1
            blk0 = ti * 4
            tgtQ, tgtK, tgtS = (QF, KF, SF) if d == 0 else (QB, KB, SB)
            q_, bq_ = qst[i]
            k_, bk_ = kst[i]
            s_, bs_ = kss[i]
            st_, bst_ = sst[i]
            v3 = lambda a: a.rearrange("p (c t) -> p c t", t=32)
            stt(v3(q_), v3(qsrc), KS, ebv, ALU.mult, ALU.mult, [bq] + benb_r, [bq_])
            tt("dve", v3(k_), v3(ksrc), enbv, ALU.mult, [bk] + benb_r, [bk_])
            tt("dve", s_.rearrange("p (c t) -> p c t", t=32), k_.rearrange("p (c t) -> p c t", t=32),
               decv, ALU.mult, [bk_] + decr, [bs_])
            dma("sp", tgtQ[blk0:blk0 + 4, :, hh, :].rearrange("b p t -> p b t"), q_.rearrange("p (b t) -> p b t", t=128),
                bq_, [bq_], [db(tgtQ.tensor.name, ti)])
            dma("sp", tgtK[blk0:blk0 + 4, :, hh, :].rearrange("b p t -> p b t"), k_.rearrange("p (b t) -> p b t", t=128),
                bk_, [bk_], [db(tgtK.tensor.name, ti)])
            for j in range(4):
                P.op("pe", (lambda j=j, s_=s_: (lambda e: e.transpose(psT[:, j * 128:(j + 1) * 128], s_[:, j * 128:(j + 1) * 128], ident[:]))),
                     [bs_, b_ident], [b_psT])
            cp("act", st_.rearrange("p b k -> p (b k)"), psT[:, 0:512], [b_psT], [bst_])
            dma("sp", tgtS[blk0:blk0 + 4, :, hh, :].rearrange("b p k -> p b k"), st_, bst_, [bst_], [db(tgtS.tensor.name, ti)])

        for ti in range(NT):
            cnd = cond_of_tile(ti)
            sample = ti < NTS
            dma("sp", xt, stream_T(x_src, ti), b_xt, [db(x_src.tensor.name, ti)], [b_xt])
            if first and "xst" not in skip:
                dma("sp", stream_T(XT, ti), xt, b_xt, [b_xt], [db("XT", ti)])
            if "norm" not in skip:
                rmsnorm_mod(xt, b_xt, hT, b_hT, lambda c: (A1[:, l, cnd, c:c + 1], b_A1),
                            lambda c: (modT[:, l, cnd, c:c + 1], b_modT), tmp)
            if sample and "cs" not in skip:
                dma("sp", cosT, c_cos[:, ti * TT:(ti + 1) * TT], b_cosT, [], [b_cosT])
                dma("sp", sinT, c_sin[:, ti * TT:(ti + 1) * TT], b_sinT, [], [b_sinT])

            def proj4(cg):
                wt, wb_, nk = wload(w, cg, 0)
                outs = []
                for j in range(4):
                    pt, pb = newps()
                    mmk(pt[:, :], wt, slice(j * 128, (j + 1) * 128), hT, 16, [wb_, b_hT], [pb])
                    outs.append((pt, pb))
                return outs

            def projV(cg, hbase):
                if "V" in skip:
                    return
                wt, wb_, nk = wload(w, cg, 0)
                for tb in range(4):
                    pt, pb = newps()

                    def fn(e, pt=pt, wt=wt, tb=tb):
                        r = None
                        for kc in range(16):
                            r = e.matmul(pt[:, :], lhsT=hT[:, kc, tb * 128:(tb + 1) * 128], rhs=wt[:, kc, :],
                                         start=(kc == 0), stop=(kc == 15))
                        return r
                    P.op("pe", fn, [wb_, b_hT], [pb])
                    i = rot[0] % R
                    rot[0] += 1
                    v_, bv_ = vst[i]
                    cp("act", v_, pt[:, :], [pb], [bv_])
                    dma("sp", VV[ti * 4 + tb, :, hbase:hbase + 4, :], v_.rearrange("p (h v) -> p h v", v=128), bv_, [bv_],
                        [db("VV", ti)])

            def projG(cg, chbase):
                if "G" in skip:
                    return
                outs = proj4(cg)
                i = rot[0] % R
                rot[0] += 1
                g_, bg_ = gst[i]
                for j, (pt, pb) in enumerate(outs):
                    act(g_[:, j, :], pt[:, :], AF.Silu, [pb], [bg_])
                dma("sp", GG[:, chbase:chbase + 4, ti * TT:(ti + 1) * TT], g_, bg_, [bg_], [db("GG", ti)])

            for hq in range(0 if "A" in skip else 2):
                outs = proj4(0 + hq)
                for j, (pt, pb) in enumerate(outs):
                    act(q4[:, j, :], pt[:, :], AF.Silu, [pb], [b_q4])
                for d in range(2):
                    outs = proj4(2 + 2 * d + hq)
                    for j, (pt, pb) in enumerate(outs):
                        h = hq * 4 + j
                        i = rot[0] % R
                        sg, bsg = sig[i]
                        l_, bl_ = lf[i]
                        a_, ba_ = ka[i]
                        c_, bc_ = bc[i]
                        e1, be1 = eb[i]
                        e2, be2 = enb[i]
                        act(sg, pt[:, :], AF.Sigmoid, [pb], [bsg])
                        act(l_, sg, AF.Ln, [bsg, b_OML, b_LB], [bl_], scale=OML[:, le, d, h:h + 1], bias=LB[:, le, d, h:h + 1])
                        ts("dve", a_, sg, LBM[:, le, d, h:h + 1], OML[:, le, d, h:h + 1], ALU.mult, ALU.add,
                           [bsg, b_LBM, b_OML], [ba_])
                        P.op("dve", (lambda c_=c_, l_=l_: (lambda e: e.tensor_tensor_scan(out=c_, data0=resetm[:], data1=l_, initial=0.0,
                                                                                          op0=ALU.mult, op1=ALU.add))),
                             [bl_, b_resetm], [bc_])
                        c3 = c_.rearrange("p (c t) -> p c t", t=32)
                        if d == 1:
                            tt("dve", l_, l_, c_, ALU.subtract, [bl_, bc_], [bl_])
                            tt("dve", c3, l_.rearrange("p (c t) -> p c t", t=32), c3[:, :, 31:32].to_broadcast([128, 16, 32]),
                               ALU.add, [bl_, bc_], [bc_])
                        ts("dve", c_, c_, -80.0, None, ALU.max, None, [bc_], [bc_])
                        act(e1, c_, AF.Exp, [bc_], [be1])
                        act(e2, c_, AF.Exp, [bc_], [be2], scale=-1.0)
                        e13 = e1.rearrange("p (c t) -> p c t", t=32)
                        dcol = e13[:, :, 31:32] if d == 0 else e13[:, :, 0:1]
                        cp("dve", decst[:, h, d, :].unsqueeze(2), dcol, [be1], [b_decst])
                        emit_hd(ti, h, d, q4[:, j, :], b_q4, a_, ba_, e13, [be1, be2],
                                e2.rearrange("p (c t) -> p c t", t=32), dcol.to_broadcast([128, 16, 32]), [be1])
                projV(6 + hq, hq * 4)
                projG(8 + hq, hq * 4)
            for hq in range(0 if "B" in skip else 2):
                for (cgb, dst, bdst) in ((10, q4, b_q4), (12, k4, b_k4)):
                    outs = proj4(cgb + hq)
                    for j, (pt, pb) in enumerate(outs):
                        if not sample or "rope" in skip:
                            cp("act", dst[:, j, :], pt[:, :], [pb], [bdst])
                            continue
                        i = rot[0] % R
                        rot[0] += 1
                        rb_, brb = rb[i]
                        a1, ba1 = r1[i]
                        a2, ba2 = r2[i]
                        cp("act", a1, pt[:, :], [pb], [ba1])
                        cp("dve", rb_, a1, [ba1], [brb])
                        p2, pb2 = newps()
                        mm(p2[:, :], perm[:], rb_, [b_perm, brb], [pb2])
                        tt("dve", a2, p2[:, :], sinT, ALU.mult, [pb2, b_sinT], [ba2])
                        tt("dve", a1, a1, cosT, ALU.mult, [ba1, b_cosT], [ba1])
                        tt("dve", dst[:, j, :], a1, a2, ALU.add, [ba1, ba2], [bdst])
                for j in range(4):
                    hb = hq * 4 + j
                    for d in range(2):
                        ix = (le * 2 + d) * 8 + hb
                        ebv = EB32[:, ix, :].unsqueeze(1).to_broadcast([128, 16, 32])
                        enbv = ENB32[:, ix, :].unsqueeze(1).to_broadcast([128, 16, 32])
                        dcol = EB32[:, ix, 31:32] if d == 0 else EB32[:, ix, 0:1]
                        if "decb" not in skip:
                            P.op("dve", (lambda hb=hb, d=d, dcol=dcol: (lambda e: e.tensor_copy(out=decst[:, 8 + hb, d, :],
                                                                                                 in_=dcol.to_broadcast([128, 16])))),
                                 [b_EB32], [b_decst])
                        emit_hd(ti, 8 + hb, d, q4[:, j, :], b_q4, k4[:, j, :], b_k4, ebv, [b_EB32, b_ENB32], enbv,
                                dcol.unsqueeze(2).to_broadcast([128, 16, 32]), [b_EB32])
                projV(14 + hq, 8 + hq * 4)
                projG(16 + hq, 8 + hq * 4)
            if "dec" not in skip:
                dma("sp", DEC[ti], decst, b_decst, [b_decst], [db("DEC", ti)])


    def phaseS_even(l):
        le = l // 2
        areset()
        S32, b_S = aal("S32", [128, 32, 128], F32)
        Sb, b_Sb = aal("Sb", [128, 32, 128], BF16)
        NBF = 2
        lds = [[aal("ld%d_%d" % (i, k), [128, 16, 128], BF16) for k in range(8)] for i in range(NBF)]
        decs = [aal("decs%d" % i, [128, 16, 2, 16], F32) for i in range(2)]
        ost = [[aal("ost_%d" % d, [128, 16, 128], F32) for d in range(2)]] * 2
        pm = [aal("pm%d" % i, [128, 128], BF16) for i in range(4)]
        rot = [0]
        for (t0, L, pi) in seqs:
            nblk = L // 128
            b0 = t0 // 128
            if pi is None:
                for d in range(2):
                    dma("sp", S32[:, d * 16:d * 16 + 8, :], st_h_in[le, d].rearrange("h k v -> k h v"), b_S, [], [b_S])
                    dma("sp", S32[:, d * 16 + 8:d * 16 + 16, :], st_r_in[le, d].rearrange("h k v -> k h v"), b_S, [], [b_S])
            else:
                P.op("dve", lambda e: e.memset(S32[:], 0.0), [], [b_S])
            cp("act", Sb[:], S32[:], [b_S], [b_Sb])
            for i in range(nblk):
                fb, bb = b0 + i, b0 + nblk - 1 - i
                L_ = lds[i % NBF]
                srcs = [(QF, fb), (KF, fb), (SF, fb), (VV, fb), (QB, bb), (KB, bb), (SB, bb), (VV, bb)]
                for k, (src, blk) in enumerate(srcs):
                    dma("sp", L_[k][0], src[blk], L_[k][1], [db(src.tensor.name, blk // 4)], [L_[k][1]])
                dcs = []
                for d, blk in ((0, fb), (1, bb)):
                    dc_, bdc_ = decs[d]
                    if i == 0 or (blk % 4 == (0 if d == 0 else 3)):
                        dma("sp", dc_, DEC[blk // 4], bdc_, [db("DEC", blk // 4)], [bdc_])
                    dcs.append((dc_, bdc_))
                for d in range(2):
                    q_, bq_ = L_[4 * d + 0]
                    k_, bk_ = L_[4 * d + 1]
                    s_, bs_ = L_[4 * d + 2]
                    v_, bv_ = L_[4 * d + 3]
                    blk = fb if d == 0 else bb
                    o_, bo_ = ost[i % 2][d]
                    msk, bmsk = (maskf, b_maskf) if d == 0 else (maskb, b_maskb)
                    dc_, bdc_ = dcs[d]
                    for h in range(16):
                        sp_, bsp_ = newps()
                        op_, bop_ = newps()
                        p_, bp_ = pm[rot[0] % 4]
                        rot[0] += 1
                        mm(sp_[:, 0:128], k_[:, h, :], q_[:, h, :], [bk_, bq_], [bsp_])
                        tt("dve", p_, sp_[:, 0:128], msk[:], ALU.mult, [bsp_, bmsk], [bp_])
                        mm(op_[:, 0:128], v_[:, h, :], p_, [bv_, bp_], [bop_], start=True, stop=False)
                        si = d * 16 + h
                        corder = range(4) if d == 0 else range(3, -1, -1)
                        for n_, c in enumerate(corder):
                            mm(op_[:, c * 32:(c + 1) * 32], Sb[:, si, :], q_[:, h, c * 32:(c + 1) * 32], [b_Sb, bq_], [bop_],
                               start=False, stop=(n_ == 3))
                            ds_, bds_ = newps()
                            tp = (96, 0) if c == 3 else None
                            mm(ds_[:, 0:128], s_[c * 32:(c + 1) * 32, h, :], v_[c * 32:(c + 1) * 32, h, :], [bs_, bv_], [bds_], tp=tp)
                            cidx = (blk % 4) * 4 + c
                            stt(S32[:, si, :], S32[:, si, :], dc_[:, h, d, cidx:cidx + 1], ds_[:, 0:128], ALU.mult, ALU.add,
                                [b_S, bdc_, bds_], [b_S])
                            cp("act", Sb[:, si, :], S32[:, si, :], [b_S], [b_Sb])
                        cp("act", o_[:, h, :], op_[:, 0:128], [bop_], [bo_])
                    tgt = OF if d == 0 else OB
                    dma("sp", tgt[:, :, blk * 128:(blk + 1) * 128], o_, bo_, [bo_], [db(tgt.tensor.name, blk // 4)])
            if pi is not None:
                for d in range(2):
                    dma("sp", nsh_out[pi, le, d].rearrange("h k v -> k h v"), S32[:, d * 16:d * 16 + 8, :], b_S, [b_S],
                        [db("nsh", (pi, le, d))])
                    dma("sp", nsr_out[pi, le, d].rearrange("h k v -> k h v"), S32[:, d * 16 + 8:d * 16 + 16, :], b_S, [b_S],
                        [db("nsr", (pi, le, d))])

    def phaseB(l, last):
        even = (l % 2 == 0)
        areset()
        xt, b_xt = aal("xt", [128, NCH, TT], F32)
        hT, b_hT = aal("hT", [128, NCH, TT], BF16)
        actT, b_actT = aal("actT", [128, NFC, TT], BF16)
        sq = [aal("sq%d" % i, [128, TT], BF16) for i in range(2)]
        rs, b_rs = aal("rs", [128, TT], F32)
        tm = [aal("tm%d" % i, [128, TT], F32) for i in range(2)]
        tmp = (([s[0] for s in sq], [s[1] for s in sq]), (rs, b_rs), ([s[0] for s in tm], [s[1] for s in tm]))
        of_ = [aal("of%d" % i, [128, TT], F32) for i in range(2)]
        ob_ = [aal("ob%d" % i, [128, TT], F32) for i in range(2)]
        gg_ = [aal("gg%d" % i, [128, TT], BF16) for i in range(2)]
        oo_ = [aal("oo%d" % i, [128, TT], F32) for i in range(2)]
        r2_ = [aal("r2%d" % i, [128, TT], F32) for i in range(2)]
        sg_ = r2_
        wo, wg_, wu_, wd_ = W[("out", l)], W[("g", l)], W[("u", l)], W[("d", l)]
        for ti in range(NT):
            cnd = cond_of_tile(ti)
            tsl = slice(ti * TT, (ti + 1) * TT)
            dma("sp", xt, XT[:, :, tsl], b_xt, [db("XT", ti)], [b_xt])
            if even:
                for c in range(NCH):
                    i = c % 2
                    dma("sp", of_[i][0], OF[:, c, tsl], of_[i][1], [db("OF", ti)], [of_[i][1]])
                    dma("sp", ob_[i][0], OB[:, c, tsl], ob_[i][1], [db("OB", ti)], [ob_[i][1]])
                    dma("sp", gg_[i][0], GG[:, c, tsl], gg_[i][1], [db("GG", ti)], [gg_[i][1]])
                    o, bo = oo_[i]
                    tt("dve", o, of_[i][0], ob_[i][0], ALU.add, [of_[i][1], ob_[i][1]], [bo])
                    act(sq[i][0], o, AF.Square, [bo], [sq[i][1]])
                    pt, pb = newps()
                    mm(pt[:, :], ones[:], sq[i][0], [b_ones, sq[i][1]], [pb])
                    r_, br_ = r2_[i]
                    act(r_, pt[:, :], AF.Sqrt, [pb, b_epsc], [br_], scale=1.0 / 128, bias=epsc[:, 0:1])
                    P.op("dve", (lambda r_=r_: (lambda e: e.reciprocal(out=r_, in_=r_))), [br_], [br_])
                    tt("dve", o, o, r_, ALU.mult, [bo, br_], [bo])
                    tt("dve", hT[:, c, :], o, gg_[i][0], ALU.mult, [bo, gg_[i][1]], [b_hT])
            else:
                dma("sp", hT, YO[:, :, tsl], b_hT, [db("YO", ti)], [b_hT])
            for cg in range(4):
                wt, wb_, nk = wload(wo, cg, 0)
                for j in range(4):
                    ch = cg * 4 + j
                    pt, pb = newps()
                    mmk(pt[:, :], wt, slice(j * 128, (j + 1) * 128), hT, 16, [wb_, b_hT], [pb])
                    stt(xt[:, ch, :], pt[:, :], modT[:, l, cnd, 32 + ch:33 + ch], xt[:, ch, :], ALU.mult, ALU.add,
                        [pb, b_modT, b_xt], [b_xt])
            rmsnorm_mod(xt, b_xt, hT, b_hT, lambda c: (A2[:, l, cnd, c:c + 1], b_A2),
                        lambda c: (modT[:, l, cnd, 48 + c:49 + c], b_modT), tmp)
            for jg in range(11):
                wtg, wbg, _ = wload(wg_, jg, 0)
                wtu, wbu, _ = wload(wu_, jg, 0)
                for j in range(4):
                    fc = jg * 4 + j
                    pg, pbg = newps()
                    pu, pbu = newps()
                    mmk(pg[:, :], wtg, slice(j * 128, (j + 1) * 128), hT, 16, [wbg, b_hT], [pbg])
                    mmk(pu[:, :], wtu, slice(j * 128, (j + 1) * 128), hT, 16, [wbu, b_hT], [pbu])
                    s_, bs_ = sg_[fc % 2]
                    act(s_, pg[:, :], AF.Silu, [pbg], [bs_])
                    tt("dve", actT[:, fc, :], pu[:, :], s_, ALU.mult, [pbu, bs_], [b_actT])
            for cg in range(4):
                pts = [newps() for _ in range(4)]
                for kt in range(3):
                    wt, wb_, nk = wload(wd_, cg, kt)
                    for j in range(4):
                        mmk(pts[j][0][:, :], wt, slice(j * 128, (j + 1) * 128), actT, nk, [wb_, b_actT], [pts[j][1]],
                            start=(kt == 0), stop=(kt == 2), k0=kt * 16)
                for j in range(4):
                    ch = cg * 4 + j
                    stt(xt[:, ch, :], pts[j][0][:, :], modT[:, l, cnd, 80 + ch:81 + ch], xt[:, ch, :], ALU.mult, ALU.add,
                        [pts[j][1], b_modT, b_xt], [b_xt])
            if not last:
                dma("sp", XT[:, :, tsl], xt, b_xt, [b_xt], [db("XT", ti)])
            else:
                yt, b_yt = actT[:, 0:32, :].bitcast(F32) if False else (None, None)
                (sqv, b_sqv), (rsv, b_rsv), (tmv, b_tmv) = tmp
                pt, pb = newps()
                for c in range(NCH):
                    act(sqv[c % 2], xt[:, c, :], AF.Square, [b_xt], [b_sqv[c % 2]])
                    mm(pt[:, :], ones[:], sqv[c % 2], [b_ones, b_sqv[c % 2]], [pb], start=(c == 0), stop=(c == NCH - 1))
                act(rsv, pt[:, :], AF.Sqrt, [pb, b_epsc], [b_rsv], scale=1.0 / D, bias=epsc[:, 0:1])
                P.op("dve", lambda e: e.reciprocal(out=rsv, in_=rsv), [b_rsv], [b_rsv])
                for c in range(NCH):
                    stt(xt[:, c, :], xt[:, c, :], fnorm[:, c:c + 1], rsv, ALU.mult, ALU.mult, [b_xt, b_fnorm, b_rsv], [b_xt])
                dma("sp", yT_out[:, :, tsl], xt, b_xt, [b_xt], [db("yT", ti)])

    def phaseA_odd(l):
        lo = l // 2
        areset()
        xt, b_xt = aal("xt", [128, NCH, TT], F32)
        hT, b_hT = aal("hT", [128, NCH, TT], BF16)
        sq = [aal("sq%d" % i, [128, TT], BF16) for i in range(2)]
        rs, b_rs = aal("rs", [128, TT], F32)
        tm = [aal("tm%d" % i, [128, TT], F32) for i in range(2)]
        tmp = (([s[0] for s in sq], [s[1] for s in sq]), (rs, b_rs), ([s[0] for s in tm], [s[1] for s in tm]))
        gx = [aal("gx%d" % i, [128, TT], F32) for i in range(2)]
        g2 = [aal("g2%d" % i, [128, TT], F32) for i in range(2)]
        gs = [aal("gs%d" % i, [128, 4, TT], BF16) for i in range(2)]
        xs = [aal("xs%d" % i, [128, 4, TT], F32) for i in range(2)]
        w = W[("in", l)]
        for ti in range(NT):
            cnd = cond_of_tile(ti)
            tsl = slice(ti * TT, (ti + 1) * TT)
            dma("sp", xt, XT[:, :, tsl], b_xt, [db("XT", ti)], [b_xt])
            rmsnorm_mod(xt, b_xt, hT, b_hT, lambda c: (A1[:, l, cnd, c:c + 1], b_A1),
                        lambda c: (modT[:, l, cnd, c:c + 1], b_modT), tmp)
            for cg in range(8):
                wt, wb_, nk = wload(w, cg, 0)
                st_, bst_ = (gs if cg < 4 else xs)[cg % 2]
                for j in range(4):
                    pt, pb = newps()
                    mmk(pt[:, :], wt, slice(j * 128, (j + 1) * 128), hT, 16, [wb_, b_hT], [pb])
                    if cg < 4:
                        x_, bx_ = gx[j % 2]
                        y_, by_ = g2[j % 2]
                        cp("act", x_, pt[:, :], [pb], [bx_])
                        act(y_, pt[:, :], AF.Square, [pb], [by_])
                        ts("dve", y_, y_, 0.044715, 1.0, ALU.mult, ALU.add, [by_], [by_])
                        tt("dve", y_, y_, x_, ALU.mult, [by_, bx_], [by_])
                        act(y_, y_, AF.Sigmoid, [by_], [by_], scale=1.5957691216057308)
                        tt("dve", st_[:, j, :], y_, x_, ALU.mult, [by_, bx_], [bst_])
                    else:
                        cp("act", st_[:, j, :], pt[:, :], [pb], [bst_])
                if cg < 4:
                    dma("sp", GG[:, cg * 4:cg * 4 + 4, tsl], st_, bst_, [bst_], [db("GG", ti)])
                else:
                    dma("sp", XBR[:, (cg - 4) * 4:(cg - 4) * 4 + 4, tsl], st_, bst_, [bst_], [db("XBR", ti)])

    def phaseS_odd(l):
        lo = l // 2
        areset()
        rw = [[aal("rw%d_%d" % (d, g), [128, 16, 128], BF16) for g in range(2)] for d in range(2)]
        for d in range(2):
            for g, src in enumerate((rg_w_a, rg_w_i)):
                dma("pool", rw[d][g][0], src[lo, d].rearrange("n i j -> i n j"), rw[d][g][1], [], [rw[d][g][1]])
        xh = [aal("xh%d" % i, [128, TT + 4], F32) for i in range(2)]
        xc = [aal("xc%d" % i, [128, TT], F32) for i in range(2)]
        xcb = [aal("xcb%d" % i, [128, TT], BF16) for i in range(2)]
        rr = [aal("rr%d" % i, [128, TT], F32) for i in range(2)]
        gi = [aal("gi%d" % i, [128, TT], F32) for i in range(2)]
        aa = [aal("aa%d" % i, [128, TT], F32) for i in range(2)]
        a2 = [aal("a2%d" % i, [128, TT], F32) for i in range(2)]
        uu = [aal("uu%d" % i, [128, TT], F32) for i in range(2)]
        ar = [aal("ar%d" % i, [128, TT], F32) for i in range(2)]
        ur = [aal("ur%d" % i, [128, TT], F32) for i in range(2)]
        hh = [aal("hh%d" % i, [128, TT], F32) for i in range(2)]
        hf = [aal("hf%d" % i, [128, TT], F32) for i in range(2)]
        gg = [aal("gg%d" % i, [128, TT], BF16) for i in range(2)]
        yo = [aal("yo%d" % i, [128, TT], BF16) for i in range(2)]
        rot = [0]
        for d in range(2):
            for (t0, L, pi) in seqs:
                SEG = min(TT, L)
                nseg = L // SEG
                order = range(nseg) if d == 0 else range(nseg - 1, -1, -1)
                for n_, sgi in enumerate(order):
                    s0 = t0 + sgi * SEG
                    ti = s0 // TT
                    for c in range(NCH):
                        i = rot[0] % 2
                        rot[0] += 1
                        xh_, bxh_ = xh[i]
                        lh = 2 if sgi > 0 else 0
                        rh = 1 if sgi < nseg - 1 else 0
                        P.op("dve", (lambda xh_=xh_: (lambda e: e.memset(xh_, 0.0))), [], [bxh_])
                        dma("sp", xh_[:, 2 - lh:2 + SEG + rh], XBR[:, c, s0 - lh:s0 + SEG + rh], bxh_,
                            [db("XBR", ti), db("XBR", max(ti - 1, 0)), db("XBR", min(ti + 1, NT - 1))], [bxh_])
                        xc_, bxc_ = xc[i]
                        xc_ = xc_[:, 0:SEG]
                        ts("dve", xc_, xh_[:, 0:SEG], convw[:, lo, 0, c:c + 1], convb[:, lo, c:c + 1], ALU.mult, ALU.add,
                           [bxh_, b_convw, b_convb], [bxc_])
                        for j in range(1, 4):
                            stt(xc_, xh_[:, j:j + SEG], convw[:, lo, j, c:c + 1], xc_, ALU.mult, ALU.add, [bxh_, b_convw, bxc_], [bxc_])
                        xb_, bxb_ = xcb[i]
                        xb_ = xb_[:, 0:SEG]
                        cp("act", xb_, xc_, [bxc_], [bxb_])
                        pr, pbr = newps()
                        pg, pbg = newps()
                        mm(pr[:, 0:SEG], rw[d][0][0][:, c, :], xb_, [rw[d][0][1], bxb_], [pbr])
                        mm(pg[:, 0:SEG], rw[d][1][0][:, c, :], xb_, [rw[d][1][1], bxb_], [pbg])
                        r_, br_ = rr[i]
                        g_, bg_ = gi[i]
                        a_, ba_ = aa[i]
                        q_, bq_ = a2[i]
                        u_, bu_ = uu[i]
                        r_, g_, a_, q_, u_ = [z[:, 0:SEG] for z in (r_, g_, a_, q_, u_)]
                        act(r_, pr[:, 0:SEG], AF.Sigmoid, [pbr, b_rgba], [br_], bias=rgba[:, lo, d, c:c + 1])
                        act(g_, pg[:, 0:SEG], AF.Sigmoid, [pbg, b_rgbi], [bg_], bias=rgbi[:, lo, d, c:c + 1])
                        act(a_, r_, AF.Exp, [br_, b_cA], [ba_], scale=cA[:, lo, d, c:c + 1])
                        act(q_, r_, AF.Exp, [br_, b_cA2], [bq_], scale=cA2[:, lo, d, c:c + 1])
                        ts("dve", q_, q_, -1.0, 1.0, ALU.mult, ALU.add, [bq_], [bq_])
                        ts("dve", q_, q_, 0.0, None, ALU.max, None, [bq_], [bq_])
                        act(q_, q_, AF.Sqrt, [bq_], [bq_])
                        tt("dve", u_, g_, xc_, ALU.mult, [bg_, bxc_], [bu_])
                        tt("dve", u_, u_, q_, ALU.mult, [bu_, bq_], [bu_])
                        h_, bh_ = hh[i]
                        h_ = h_[:, 0:SEG]
                        if n_ == 0:
                            if pi is None:
                                init, binit = h0T[:, lo, d, c:c + 1], b_h0T
                            else:
                                init, binit = 0.0, None
                        else:
                            init, binit = hcar[:, c:c + 1], b_hcar
                        rds = [ba_, bu_] + ([binit] if binit is not None else [])
                        if d == 0:
                            P.op("dve", (lambda h_=h_, a_=a_, u_=u_, init=init: (lambda e: e.tensor_tensor_scan(
                                out=h_, data0=a_, data1=u_, initial=init, op0=ALU.mult, op1=ALU.add))), rds, [bh_])
                            cp("dve", hcar[:, c:c + 1], h_[:, SEG - 1:SEG], [bh_], [b_hcar])
                            dma("sp", HF[:, c, s0:s0 + SEG], h_, bh_, [bh_], [db("HF", ti)])
                            if pi is not None and n_ == nseg - 1:
                                cp("dve", nsl[:, pi, lo, 0, c:c + 1], h_[:, SEG - 1:SEG], [bh_], [b_nsl])
                        else:
                            ar_, bar_ = ar[i]
                            ur_, bur_ = ur[i]
                            ar_, ur_ = ar_[:, 0:SEG], ur_[:, 0:SEG]
                            cp("dve", ar_, a_[:, ::-1], [ba_], [bar_])
                            cp("dve", ur_, u_[:, ::-1], [bu_], [bur_])
                            rds = [bar_, bur_] + ([binit] if binit is not None else [])
                            P.op("dve", (lambda h_=h_, ar_=ar_, ur_=ur_, init=init: (lambda e: e.tensor_tensor_scan(
                                out=h_, data0=ar_, data1=ur_, initial=init, op0=ALU.mult, op1=ALU.add))), rds, [bh_])
                            cp("dve", hcar[:, c:c + 1], h_[:, SEG - 1:SEG], [bh_], [b_hcar])
                            if pi is not None and n_ == nseg - 1:
                                cp("dve", nsl[:, pi, lo, 1, c:c + 1], h_[:, SEG - 1:SEG], [bh_], [b_nsl])
                            f_, bf_ = hf[i]
                            f_ = f_[:, 0:SEG]
                            gg_, bgg_ = gg[i]
                            gg_ = gg_[:, 0:SEG]
                            y_, by_ = yo[i]
                            y_ = y_[:, 0:SEG]
                            dma("sp", f_, HF[:, c, s0:s0 + SEG], bf_, [db("HF", ti)], [bf_])
                            dma("sp", gg_, GG[:, c, s0:s0 + SEG], bgg_, [db("GG", ti)], [bgg_])
                            tt("dve", f_, f_, h_[:, ::-1], ALU.add, [bf_, bh_], [bf_])
                            tt("dve", y_, f_, gg_, ALU.mult, [bf_, bgg_], [by_])
                            dma("sp", YO[:, c, s0:s0 + SEG], y_, by_, [by_], [db("YO", ti)])

    class Stop(Exception):
        pass

    def chk(tag):
        if stop == tag:
            raise Stop()
    try:
        chk("pre")
        for l in range(4):
            for k in (("in",) if lite else ("in", "out", "g", "u", "d")):
                wconvert(W[(k, l)])
            if l % 2 == 0:
                phaseA_even(l, l == 0)
                chk("A%d" % l)
                phaseS_even(l)
            else:
                phaseA_odd(l)
                chk("A%d" % l)
                phaseS_odd(l)
            chk("S%d" % l)
            phaseB(l, l == 3)
            chk("B%d" % l)
    except Stop:
        pass
    if "modT" in dump:
        dmod = dout("modT_o", [128, 4, 2, 96])
        dma("sp", dmod, modT[:], b_modT, [b_modT], [db("yT", "modT")])
    dma("sp", nsl_out, nsl[:], b_nsl, [b_nsl], [db("nsl_out", 0)])
    outs = [b for (k, b) in dbufs.items() if k[0] in ("yT", "nsh", "nsr", "nsl_out")]
    P.op("sp", lambda e: e.nop(), outs, [])

    P.resolve()
    slots = P.dma_slots()
    sems = {}
    for k in ("pe", "act", "dve", "pool", "sp"):
        sems[("e", k)] = es.enter_context(nc.semaphore("s_" + k))
    for n_, sid in enumerate(slots):
        sems[("d", sid)] = es.enter_context(nc.semaphore("d%s%d" % sid))
    with nc.Block() as block:
        @block.sync
        def _(e):
            P.emit("sp", e, sems)

        @block.tensor
        def _(e):
            P.emit("pe", e, sems)

        @block.scalar
        def _(e):
            P.emit("act", e, sems)

        @block.vector
        def _(e):
            P.emit("dve", e, sems)

        @block.gpsimd
        def _(e):
            P.emit("pool", e, sems)
    es.close()
    return nc


def _fm(v):
    v = np.asarray(v, np.float32)
    lead = v.shape[:-1]
    v = v.reshape(lead + (16, 128))
    return np.ascontiguousarray(np.moveaxis(v, -1, 0))


def _consts(LS):
    bf = ml_dtypes.bfloat16
    c = {}
    c["c_ident"] = np.eye(128, dtype=np.float32).astype(bf)
    c["c_ones"] = np.ones((128, 128), np.float32).astype(bf)
    i = np.arange(128)
    partner = np.where((i % 64) < 32, i + 32, i - 32)
    pm = np.zeros((128, 128), np.float32)
    pm[partner, i] = 1.0
    c["c_perm"] = pm.astype(bf)
    s, t = np.meshgrid(i, i, indexing="ij")
    same = (s // 32) == (t // 32)
    c["c_maskf"] = (same & (s <= t)).astype(np.float32).astype(bf)
    c["c_maskb"] = (same & (s >= t)).astype(np.float32).astype(bf)
    r = np.ones((128, TT), np.float32)
    r[:, 0::32] = 0.0
    c["c_reset"] = r
    pos = np.zeros((128, 2, 32), np.float32)
    pos[:, 0, :] = np.arange(1, 33)
    pos[:, 1, :] = 32 - np.arange(32)
    c["c_pos"] = pos
    tt_ = np.arange(LS)
    rows = (tt_ // 64).astype(np.float32)
    cols = (tt_ % 64).astype(np.float32)
    inv = (10000.0 ** (-np.arange(32, dtype=np.float32) / 32)).astype(np.float32)
    cosT = np.zeros((128, LS), np.float32)
    sinT = np.zeros((128, LS), np.float32)
    for f in range(128):
        p = rows if f < 64 else cols
        ang = (p * inv[(f % 64) % 32]).astype(np.float32)
        cosT[f] = np.cos(ang)
        sinT[f] = np.sin(ang) * (-1.0 if (f % 64) < 32 else 1.0)
    c["c_cos"] = cosT
    c["c_sin"] = sinT
    return c


_NC_CACHE = {}


def kernel(**inp):
    nc, in_maps, post = _prep(inp)
    res = run_bass_kernel_spmd(nc, in_maps, core_ids=list(range(len(in_maps))))
    return post(res)


def _prep(inp, **bkw):
    x_prompt = np.asarray(inp["x_prompt"], np.float32)
    x_sample = np.asarray(inp["x_sample"], np.float32)
    NS, LS = x_sample.shape[0], x_sample.shape[1]
    NPT, LP = x_prompt.shape[0], x_prompt.shape[1]
    n = NS
    NP = NPT // n
    key = (LS, NP, LP)
    if bkw:
        nc = build(LS, NP, LP, **bkw)
    else:
        if key not in _NC_CACHE:
            _NC_CACHE[key] = build(LS, NP, LP)
        nc = _NC_CACHE[key]
    consts = _consts(LS)
    shared = {}
    for k in ("w_mod", "w_in_even", "w_out_even", "w_in_odd", "w_out_odd", "w_ffn_gate", "w_ffn_up", "w_ffn_down",
              "rg_w_a", "rg_w_i"):
        shared[k] = np.ascontiguousarray(np.asarray(inp[k], np.float32))
    shared["bmodT"] = np.ascontiguousarray(np.asarray(inp["b_mod"], np.float32).reshape(4, 96, 128).transpose(2, 0, 1))
    shared["nmixT"] = _fm(inp["norm_mix"])
    shared["nffnT"] = _fm(inp["norm_ffn"])
    shared["fnormT"] = _fm(inp["final_norm"])
    shared["lblT"] = np.ascontiguousarray(np.asarray(inp["hgrn_lb_logits"], np.float32).reshape(2, 2, 8, 128).transpose(3, 0, 1, 2))
    shared["decl"] = np.ascontiguousarray(np.asarray(inp["ret_decay_logit"], np.float32).reshape(1, 32))
    shared["convwT"] = _fm(inp["conv_w"])
    shared["convbT"] = _fm(inp["conv_b"])
    shared["rgbaT"] = _fm(np.asarray(inp["rg_b_a"], np.float32).reshape(2, 2, 2048))
    shared["rgbiT"] = _fm(np.asarray(inp["rg_b_i"], np.float32).reshape(2, 2, 2048))
    shared["aparT"] = _fm(inp["rg_a_param"])
    shared.update(consts)
    c = np.asarray(inp["c"], np.float32)
    c_ctx = np.asarray(inp["c_ctx"], np.float32)
    in_maps = []
    for b in range(n):
        X = np.concatenate([x_sample[b]] + [x_prompt[b * NP + i] for i in range(NP)], axis=0)
        m = dict(shared)
        m["xT"] = np.ascontiguousarray(X.reshape(-1, 16, 128).transpose(2, 1, 0))
        m["condT"] = _fm(np.stack([c[b], c_ctx], axis=0)).transpose(0, 2, 1).copy()
        m["st_h"] = np.ascontiguousarray(np.asarray(inp["state_hgrn"], np.float32)[b])
        m["st_r"] = np.ascontiguousarray(np.asarray(inp["state_ret"], np.float32)[b])
        m["h0T"] = _fm(np.asarray(inp["state_rglru"], np.float32)[b])
        in_maps.append(m)
    def post(res):
        return _post(res, n, NP, NPT, NS, LS, LP)
    return nc, in_maps, post


def _post(res, n, NP, NPT, NS, LS, LP):
    y_prompt = np.zeros((NPT, LP, D), np.float32)
    y_sample = np.zeros((NS, LS, D), np.float32)
    nsh = np.zeros((NPT, 2, 2, 8, 128, 128), np.float32)
    nsr = np.zeros((NPT, 2, 2, 8, 128, 128), np.float32)
    nsl = np.zeros((NPT, 2, 2, D), np.float32)
    for b in range(n):
        r = res.results[b]
        Y = np.asarray(r["yT"]).transpose(2, 1, 0).reshape(-1, D)
        y_sample[b] = Y[:LS]
        for i in range(NP):
            y_prompt[b * NP + i] = Y[LS + i * LP:LS + (i + 1) * LP]
        nsh[b * NP:(b + 1) * NP] = np.asarray(r["nsh"])
        nsr[b * NP:(b + 1) * NP] = np.asarray(r["nsr"])
        nsl[b * NP:(b + 1) * NP] = np.asarray(r["nsl"]).transpose(1, 2, 3, 4, 0).reshape(NP, 2, 2, D)
    return (y_prompt, y_sample, nsh, nsr, nsl)
```

```python
import numpy as np
import ml_dtypes
from contextlib import ExitStack
import concourse.bass as bass
import concourse.mybir as mybir
from concourse.bass_utils import run_bass_kernel_spmd

F32 = mybir.dt.float32
BF16 = mybir.dt.bfloat16
AF = mybir.ActivationFunctionType
ALU = mybir.AluOpType
D = 2048
NCH = 16
DFF = 5632
NFC = 44
TT = 512
EPS = 1e-6
KS = 128 ** -0.5


class Buf:
    def __init__(self, name, arena=None, lo=0, hi=0, const=False):
        self.name, self.arena, self.lo, self.hi, self.const = name, arena, lo, hi, const
        self.lw = None
        self.rd = {}
        self.ov = [self]


class Op:
    __slots__ = ("eng", "fn", "reads", "writes", "dma", "ndma", "sig", "sigval", "waits", "dval")


class Prog:
    def __init__(self):
        self.ops = []
        self.arena_bufs = {}

    def buf(self, name, arena=None, lo=0, hi=0, const=False):
        b = Buf(name, arena, lo, hi, const)
        if arena is not None:
            lst = self.arena_bufs.setdefault(arena, [])
            for o in lst:
                if o.lo < hi and lo < o.hi:
                    o.ov.append(b)
                    b.ov.append(o)
            lst.append(b)
        return b

    def op(self, eng, fn, reads=(), writes=(), dma=None, ndma=1):
        if fn.__defaults__ is not None and fn.__code__.co_argcount == len(fn.__defaults__):
            fn = fn()
        o = Op()
        o.eng, o.fn, o.reads, o.writes, o.dma, o.ndma = eng, fn, list(reads), list(writes), dma, ndma
        o.sig, o.sigval, o.waits, o.dval = False, 0, [], 0
        self.ops.append(o)
        return o

    NSD = 96

    def gid(self, slot, eng):
        d = self.gids.setdefault(eng, {})
        n = 40 if eng == "sp" else 8
        g = d.setdefault(id(slot), len(d) % n)
        return (eng, g)

    def resolve(self):
        self.gids = {}
        for o in self.ops:
            if o.dma is not None:
                o.dma = self.gid(o.dma, o.eng)
        dcnt = {}
        for o in self.ops:
            deps = {}
            for b in o.reads:
                for ob in b.ov:
                    if ob.lw is not None:
                        deps[id(ob.lw)] = ob.lw
            for b in o.writes:
                for ob in b.ov:
                    if ob.lw is not None:
                        deps[id(ob.lw)] = ob.lw
                    for r in ob.rd.values():
                        deps[id(r)] = r
            deps.pop(id(o), None)
            for p in deps.values():
                if p.dma is not None:
                    o.waits.append((("d", p.dma), dcnt[p.dma] * 16))
                else:
                    if p.eng == o.eng == "pe" and o.dma is None:
                        continue
                    p.sig = True
                    o.waits.append((("e", p.eng), p))
            if o.dma is not None:
                if dcnt.get(o.dma, 0) > 0:
                    o.waits.append((("d", o.dma), dcnt[o.dma] * 16))
                dcnt[o.dma] = dcnt.get(o.dma, 0) + o.ndma
            key = o.eng if o.dma is None else ("d", o.dma)
            for b in o.reads:
                if not b.const:
                    b.rd[key] = o
            for b in o.writes:
                for ob in b.ov:
                    ob.lw = o
                    ob.rd = {}
        cnt = {}
        for o in self.ops:
            if o.dma is None and o.sig:
                cnt[o.eng] = cnt.get(o.eng, 0) + 1
                o.sigval = cnt[o.eng]
        for o in self.ops:
            o.waits = [(k, (v.sigval if isinstance(v, Op) else v)) for k, v in o.waits]

    def dma_slots(self):
        return sorted(set(o.dma for o in self.ops if o.dma is not None))

    def emit(self, eng_name, e, sems):
        waited = {}
        for o in self.ops:
            if o.eng != eng_name:
                continue
            for k, v in o.waits:
                if waited.get(k, 0) < v:
                    e.wait_ge(sems[k], v)
                    waited[k] = v
            r = o.fn(e)
            if o.dma is not None:
                rl = r if isinstance(r, (list, tuple)) else [r]
                assert len(rl) == o.ndma, (len(rl), o.ndma)
                for ins in rl:
                    ins.then_inc(sems[("d", o.dma)], 16)
            elif o.sig:
                last = r[-1] if isinstance(r, (list, tuple)) else r
                last.then_inc(sems[("e", o.eng)], 1)


def lockstep(makers, K):
    active, free, it, done = [], list(range(K)), iter(makers), False
    while True:
        while free and not done:
            try:
                mk = next(it)
            except StopIteration:
                done = True
                break
            sl = free.pop(0)
            active.append((sl, mk(sl)))
        if not active:
            break
        for item in list(active):
            try:
                next(item[1])
            except StopIteration:
                active.remove(item)
                free.append(item[0])


def build(LS, NP, LP=256, stop=None, dump=(), lite=False, skip=()):
    T = LS + NP * LP
    NT = T // TT
    NB = T // 128
    NTS = LS // TT
    seqs = [(0, LS, None)] + [(LS + i * LP, LP, i) for i in range(NP)]
    nc = bass.Bass("TRN2", target_bir_lowering=False)
    P = Prog()
    es = ExitStack()

    def din(name, shape, dt=F32):
        return nc.dram_tensor(name, list(shape), dt, kind="ExternalInput").ap()

    def dout(name, shape, dt=F32):
        return nc.dram_tensor(name, list(shape), dt, kind="ExternalOutput").ap()

    def dscr(name, shape, dt):
        kind = "ExternalOutput" if name in dump else "Internal"
        return nc.dram_tensor(name, list(shape), dt, kind=kind).ap()

    xT_in = din("xT", [128, NCH, T])
    condT_in = din("condT", [128, NCH, 2])
    st_h_in = din("st_h", [2, 2, 8, 128, 128])
    st_r_in = din("st_r", [2, 2, 8, 128, 128])
    h0T_in = din("h0T", [128, 2, 2, NCH])
    _din = din

    def din(name, shape, dt=F32):
        if lite and name.startswith("w_") and name != "w_in_even":
            return None
        return _din(name, shape, dt)
    w_mod = din("w_mod", [4, D, 6 * D])
    w_in_even = din("w_in_even", [2, D, 9216])
    w_out_even = din("w_out_even", [2, D, D])
    w_in_odd = din("w_in_odd", [2, D, 2 * D])
    w_out_odd = din("w_out_odd", [2, D, D])
    w_g = din("w_ffn_gate", [4, D, DFF])
    w_u = din("w_ffn_up", [4, D, DFF])
    w_d = din("w_ffn_down", [4, DFF, D])
    din = _din
    rg_w_a = din("rg_w_a", [2, 2, 16, 128, 128])
    rg_w_i = din("rg_w_i", [2, 2, 16, 128, 128])
    bmodT_in = din("bmodT", [128, 4, 96])
    nmixT_in = din("nmixT", [128, 4, NCH])
    nffnT_in = din("nffnT", [128, 4, NCH])
    fnormT_in = din("fnormT", [128, NCH])
    lblT_in = din("lblT", [128, 2, 2, 8])
    decl_in = din("decl", [1, 32])
    convwT_in = din("convwT", [128, 2, 4, NCH])
    convbT_in = din("convbT", [128, 2, NCH])
    rgbaT_in = din("rgbaT", [128, 2, 2, NCH])
    rgbiT_in = din("rgbiT", [128, 2, 2, NCH])
    aparT_in = din("aparT", [128, 2, 2, NCH])
    c_ident = din("c_ident", [128, 128], BF16)
    c_ones = din("c_ones", [128, 128], BF16)
    c_perm = din("c_perm", [128, 128], BF16)
    c_maskf = din("c_maskf", [128, 128], BF16)
    c_maskb = din("c_maskb", [128, 128], BF16)
    c_reset = din("c_reset", [128, TT])
    c_pos = din("c_pos", [128, 2, 32])
    c_cos = din("c_cos", [128, LS])
    c_sin = din("c_sin", [128, LS])
    yT_out = dout("yT", [128, NCH, T])
    nsh_out = dout("nsh", [NP, 2, 2, 8, 128, 128])
    nsr_out = dout("nsr", [NP, 2, 2, 8, 128, 128])
    nsl_out = dout("nsl", [128, NP, 2, 2, NCH])
    XT = dscr("XT", [128, NCH, T], F32)
    QF = dscr("QF", [NB, 128, 16, 128], BF16)
    QB = dscr("QB", [NB, 128, 16, 128], BF16)
    KF = dscr("KF", [NB, 128, 16, 128], BF16)
    KB = dscr("KB", [NB, 128, 16, 128], BF16)
    SF = dscr("SF", [NB, 128, 16, 128], BF16)
    SB = dscr("SB", [NB, 128, 16, 128], BF16)
    VV = dscr("VV", [NB, 128, 16, 128], BF16)
    DEC = dscr("DEC", [NT, 128, 16, 2, 16], F32)
    GG = dscr("GG", [128, NCH, T], BF16)
    OF = dscr("OF", [128, NCH, T], F32)
    OB = dscr("OB", [128, NCH, T], F32)
    XBR = dscr("XBR", [128, NCH, T], F32)
    HF = dscr("HF", [128, NCH, T], F32)
    YO = dscr("YO", [128, NCH, T], BF16)

    dbufs = {}

    def db(name, idx=0):
        k = (name, idx)
        if k not in dbufs:
            dbufs[k] = P.buf("%s_%s" % (name, idx))
        return dbufs[k]

    def sb(name, shape, dt, const=False):
        t = es.enter_context(nc.sbuf_tensor(name, list(shape), dt))
        return t, P.buf(name, const=const)

    ident, b_ident = sb("ident", [128, 128], BF16, True)
    ones, b_ones = sb("ones", [128, 128], BF16, True)
    perm, b_perm = sb("perm", [128, 128], BF16, True)
    maskf, b_maskf = sb("maskf", [128, 128], BF16, True)
    maskb, b_maskb = sb("maskb", [128, 128], BF16, True)
    resetm, b_resetm = sb("resetm", [128, TT], F32, True)
    posc, b_posc = sb("posc", [128, 2, 32], F32, True)
    epsc, b_epsc = sb("epsc", [128, 4], F32, True)
    condT, b_condT = sb("condT_s", [128, NCH, 2], F32)
    scond, b_scond = sb("scond", [128, NCH, 2], BF16)
    modT, b_modT = sb("modT", [128, 4, 2, 96], F32)
    bmodT, b_bmodT = sb("bmodT_s", [128, 4, 96], F32)
    nmix, b_nmix = sb("nmix", [128, 4, NCH], F32)
    nffn, b_nffn = sb("nffn", [128, 4, NCH], F32)
    fnorm, b_fnorm = sb("fnorm", [128, NCH], F32)
    A1, b_A1 = sb("A1", [128, 4, 2, NCH], F32)
    A2, b_A2 = sb("A2", [128, 4, 2, NCH], F32)
    lbl, b_lbl = sb("lbl", [128, 2, 2, 8], F32)
    LB, b_LB = sb("LB", [128, 2, 2, 8], F32)
    OML, b_OML = sb("OML", [128, 2, 2, 8], F32)
    LBM, b_LBM = sb("LBM", [128, 2, 2, 8], F32)
    lng, b_lng = sb("lng", [128, 32], F32)
    nlng, b_nlng = sb("nlng", [128, 32], F32)
    EB32, b_EB32 = sb("EB32", [128, 32, 32], F32)
    ENB32, b_ENB32 = sb("ENB32", [128, 32, 32], F32)
    convw, b_convw = sb("convw", [128, 2, 4, NCH], F32)
    convb, b_convb = sb("convb", [128, 2, NCH], F32)
    rgba, b_rgba = sb("rgba", [128, 2, 2, NCH], F32)
    rgbi, b_rgbi = sb("rgbi", [128, 2, 2, NCH], F32)
    apar, b_apar = sb("apar", [128, 2, 2, NCH], F32)
    cA, b_cA = sb("cA", [128, 2, 2, NCH], F32)
    cA2, b_cA2 = sb("cA2", [128, 2, 2, NCH], F32)
    h0T, b_h0T = sb("h0T_s", [128, 2, 2, NCH], F32)
    nsl, b_nsl = sb("nsl_s", [128, NP, 2, 2, NCH], F32)
    hcar, b_hcar = sb("hcar", [128, NCH], F32)
    NW = 3
    wbufs = [sb("wbuf%d" % i, [128, 16, 512], BF16) for i in range(NW)]
    wctr = [0]
    psums = []
    for i in range(7):
        t = es.enter_context(nc.psum_tensor("ps%d" % i, [128, 512], F32))
        psums.append((t, P.buf("ps%d" % i)))
    psT = es.enter_context(nc.psum_tensor("psT", [128, 1024], BF16))
    b_psT = P.buf("psT")
    pctr = [0]

    def newps():
        r = psums[pctr[0] % 7]
        pctr[0] += 1
        return r

    ARENA = 120 * 1024
    arena = es.enter_context(nc.sbuf_tensor("arena", [128, ARENA // 4], F32))
    acur = [0]

    def areset():
        acur[0] = 0

    def aal(name, shape, dt):
        n = 1
        for s_ in shape[1:]:
            n *= s_
        nb = n * (4 if dt == F32 else 2)
        nb = (nb + 63) // 64 * 64
        lo = acur[0]
        acur[0] += nb
        assert acur[0] <= ARENA, ("arena overflow", name, acur[0])
        v = arena[:, lo // 4:(lo + nb) // 4]
        if dt != F32:
            v = v.bitcast(dt)
        v = v[:, 0:n]
        if len(shape) == 3:
            v = v.rearrange("p (a b) -> p a b", b=shape[2])
        elif len(shape) == 4:
            v = v.rearrange("p (a b c) -> p a b c", b=shape[2], c=shape[3])
        return v, P.buf(name, "arena", lo, lo + nb)

    def dma(eng, out, in_, slot, reads, writes):
        P.op(eng, lambda e: e.dma_start(out=out, in_=in_), reads, writes, dma=slot)

    def act(out, in_, func, reads, writes, scale=None, bias=None):
        kw = {}
        if scale is not None:
            kw["scale"] = scale
        if bias is not None:
            kw["bias"] = bias
        P.op("act", lambda e: e.activation(out=out, in_=in_, func=func, **kw), reads, writes)

    def tt(eng, out, in0, in1, op, reads, writes):
        P.op(eng, lambda e: e.tensor_tensor(out=out, in0=in0, in1=in1, op=op), reads, writes)

    def ts(eng, out, in0, s1, s2, op0, op1, reads, writes):
        if op1 is None:
            P.op(eng, lambda e: e.tensor_scalar(out=out, in0=in0, scalar1=s1, scalar2=None, op0=op0), reads, writes)
        else:
            P.op(eng, lambda e: e.tensor_scalar(out=out, in0=in0, scalar1=s1, scalar2=s2, op0=op0, op1=op1), reads, writes)

    def stt(out, in0, scalar, in1, op0, op1, reads, writes):
        P.op("dve", lambda e: e.scalar_tensor_tensor(out=out, in0=in0, scalar=scalar, in1=in1, op0=op0, op1=op1), reads, writes)

    def cp(eng, out, in_, reads, writes):
        if eng == "act":
            P.op("act", lambda e: e.copy(out=out, in_=in_), reads, writes)
        else:
            P.op(eng, lambda e: e.tensor_copy(out=out, in_=in_), reads, writes)

    def mm(out, lhsT, rhs, reads, writes, start=True, stop=True, tp=None):
        if tp is None:
            P.op("pe", lambda e: e.matmul(out, lhsT=lhsT, rhs=rhs, start=start, stop=stop), reads, writes)
        else:
            P.op("pe", lambda e: e.matmul(out, lhsT=lhsT, rhs=rhs, start=start, stop=stop, tile_position=tp), reads, writes)

    def mmk(out, wt, wcols, rhs3, nk, reads, writes, start=True, stop=True, k0=0):
        def fn(e):
            r = None
            for kc in range(nk):
                r = e.matmul(out, lhsT=wt[:, kc, wcols], rhs=rhs3[:, k0 + kc, :],
                             start=(start and kc == 0), stop=(stop and kc == nk - 1))
            return r
        P.op("pe", fn, reads, writes)

    for (t, b, src) in [(ident, b_ident, c_ident), (ones, b_ones, c_ones), (perm, b_perm, c_perm),
                        (maskf, b_maskf, c_maskf), (maskb, b_maskb, c_maskb), (resetm, b_resetm, c_reset),
                        (posc, b_posc, c_pos), (condT, b_condT, condT_in), (bmodT, b_bmodT, bmodT_in),
                        (nmix, b_nmix, nmixT_in), (nffn, b_nffn, nffnT_in), (fnorm, b_fnorm, fnormT_in),
                        (lbl, b_lbl, lblT_in), (convw, b_convw, convwT_in), (convb, b_convb, convbT_in),
                        (rgba, b_rgba, rgbaT_in), (rgbi, b_rgbi, rgbiT_in), (apar, b_apar, aparT_in),
                        (h0T, b_h0T, h0T_in)]:
        dma("sp", t[:], src, b, [], [b])
    dma("sp", lng[:], decl_in.partition_broadcast(128) if False else decl_in[0:1, :].to_broadcast([128, 32]), b_lng, [], [b_lng])
    P.op("dve", lambda e: e.memset(epsc[:], EPS), [], [b_epsc])
    P.op("dve", lambda e: e.memset(nsl[:], 0.0), [], [b_nsl])
    act(scond[:], condT[:], AF.Silu, [b_condT], [b_scond])
    P.op("dve", lambda e: e.memset(LB[:, 0], 0.0), [], [b_LB])
    tt("dve", lbl[:, 1], lbl[:, 1], lbl[:, 0], ALU.subtract, [b_lbl], [b_lbl])
    act(LB[:, 1], lbl[:, 1], AF.Sigmoid, [b_lbl, b_LB], [b_LB])
    ts("dve", OML[:], LB[:], -1.0, 1.0, ALU.mult, ALU.add, [b_LB], [b_OML])
    ts("dve", LBM[:], LB[:], -1.0, None, ALU.add, None, [b_LB], [b_LBM])
    act(lng[:], lng[:], AF.Sigmoid, [b_lng], [b_lng])
    act(lng[:], lng[:], AF.Ln, [b_lng], [b_lng])
    ts("dve", nlng[:], lng[:], -1.0, None, ALU.mult, None, [b_lng], [b_nlng])
    for i in range(32):
        d = (i // 8) % 2
        act(EB32[:, i, :], posc[:, d, :], AF.Exp, [b_posc, b_lng], [b_EB32], scale=lng[:, i:i + 1])
        act(ENB32[:, i, :], posc[:, d, :], AF.Exp, [b_posc, b_nlng], [b_ENB32], scale=nlng[:, i:i + 1])
    act(cA[:], apar[:], AF.Exp, [b_apar], [b_cA], scale=-1.0)
    ts("dve", cA[:], cA[:], 1.0, None, ALU.add, None, [b_cA], [b_cA])
    act(cA[:], cA[:], AF.Ln, [b_cA], [b_cA])
    ts("dve", cA2[:], cA[:], -16.0, None, ALU.mult, None, [b_cA], [b_cA2])
    ts("dve", cA[:], cA[:], -8.0, None, ALU.mult, None, [b_cA, b_cA2], [b_cA])

    wconv_slot = P.buf("wconv_slot")
    wtok = [P.buf("wtok%d" % i) for i in range(3)]
    wtokc = [0]

    class WB:
        pass

    def wprep(name, src, K, N):
        w = WB()
        w.ncg, w.nkt, w.K, w.N = N // 512, (K + 2047) // 2048, K, N
        w.t = dscr("wb_" + name, [w.ncg, w.nkt, 128, 16, 512], BF16)
        w.name, w.src, w.done = name, src, False
        return w

    def wconvert(w):
        if w.done or "conv" in skip:
            return
        w.done = True
        for cg in range(w.ncg):
            for kt in range(w.nkt):
                nk = min(16, (w.K - kt * 2048) // 128)
                s = w.src[kt * 2048:kt * 2048 + nk * 128, cg * 512:(cg + 1) * 512].rearrange("(kc p) e -> p kc e", p=128)
                o = w.t[cg, kt, :, 0:nk, :]
                tk = wtok[wtokc[0] % len(wtok)]
                wtokc[0] += 1
                dma("pool", o, s, wconv_slot, [], [db("wb_" + w.name, (cg, kt)), tk])

    def wload(w, cg, kt):
        i = wctr[0] % NW
        wctr[0] += 1
        t, b = wbufs[i]
        nk = min(16, (w.K - kt * 2048) // 128)
        dma("sp", t[:, 0:nk, :], w.t[cg, kt, :, 0:nk, :], b, [db("wb_" + w.name, (cg, kt))], [b])
        return t, b, nk

    W = {}
    for l in range(1 if lite else 4):
        if lite:
            W[("in", l)] = wprep("in%d" % l, w_in_even[l // 2], D, 9216)
            continue
        if l % 2 == 0:
            W[("in", l)] = wprep("in%d" % l, w_in_even[l // 2], D, 9216)
            W[("out", l)] = wprep("out%d" % l, w_out_even[l // 2], D, D)
        else:
            W[("in", l)] = wprep("in%d" % l, w_in_odd[l // 2], D, 2 * D)
            W[("out", l)] = wprep("out%d" % l, w_out_odd[l // 2], D, D)
        W[("g", l)] = wprep("g%d" % l, w_g[l], D, DFF)
        W[("u", l)] = wprep("u%d" % l, w_u[l], D, DFF)
        W[("d", l)] = wprep("d%d" % l, w_d[l], DFF, D)

    if lite:
        P.op("dve", lambda e: e.memset(modT[:], 0.05), [], [b_modT])
        P.op("dve", lambda e: e.memset(A1[:], 1.05), [], [b_A1])
        P.op("dve", lambda e: e.memset(A2[:], 1.05), [], [b_A2])
    for l in range(0 if lite else 4):
        pt, pb = newps()
        for cg in range(24):
            i = wctr[0] % NW
            wctr[0] += 1
            t, b = wbufs[i]
            s = w_mod[l][:, cg * 512:(cg + 1) * 512].rearrange("(kc p) e -> p kc e", p=128)
            dma("pool", t[:], s, b, [], [b])
            for j in range(4):
                ch = cg * 4 + j
                mmk(pt[:, ch * 2:ch * 2 + 2], t, slice(j * 128, (j + 1) * 128), scond, 16, [b, b_scond], [pb])
        tt("dve", modT[:, l].rearrange("p c j -> p j c"), pt[:, 0:192].rearrange("p (j c) -> p j c", c=2),
           bmodT[:, l, :].unsqueeze(2).to_broadcast([128, 96, 2]), ALU.add, [pb, b_bmodT], [b_modT])
        for c in range(2):
            stt(A1[:, l, c, :], modT[:, l, c, 16:32], 1.0, nmix[:, l, :], ALU.add, ALU.mult, [b_modT, b_nmix], [b_A1])
            stt(A2[:, l, c, :], modT[:, l, c, 64:80], 1.0, nffn[:, l, :], ALU.add, ALU.mult, [b_modT, b_nffn], [b_A2])

    def cond_of_tile(ti):
        return 0 if ti < NTS else 1

    def rmsnorm_mod(xt, b_xt, hT, b_hT, Acol, Bcol, tmp):
        (sq, b_sq), (rs, b_rs), (tm, b_tm) = tmp
        pt, pb = newps()
        for c in range(NCH):
            s_, bs_ = sq[c % 2], b_sq[c % 2]
            act(s_, xt[:, c, :], AF.Square, [b_xt], [bs_])
            mm(pt[:, :], ones[:], s_, [b_ones, bs_], [pb], start=(c == 0), stop=(c == NCH - 1))
        act(rs, pt[:, :], AF.Sqrt, [pb, b_epsc], [b_rs], scale=1.0 / D, bias=epsc[:, 0:1])
        P.op("dve", lambda e: e.reciprocal(out=rs, in_=rs), [b_rs], [b_rs])
        for c in range(NCH):
            t_, bt_ = tm[c % 2], b_tm[c % 2]
            a_, b_ = Acol(c), Bcol(c)
            stt(t_, xt[:, c, :], a_[0], rs, ALU.mult, ALU.mult, [b_xt, b_rs, a_[1]], [bt_])
            if b_ is None:
                cp("act", hT[:, c, :], t_, [bt_], [b_hT])
            else:
                act(hT[:, c, :], t_, AF.Identity, [bt_, b_[1]], [b_hT], bias=b_[0])

    def stream_T(x_src, ti):
        t0 = ti * TT
        return x_src[:, :, t0:t0 + TT]

    def phaseA_even(l, first):
        le = l // 2
        areset()
        xt, b_xt = aal("xt", [128, NCH, TT], F32)
        hT, b_hT = aal("hT", [128, NCH, TT], BF16)
        sq = [aal("sq%d" % i, [128, TT], BF16) for i in range(2)]
        rs, b_rs = aal("rs", [128, TT], F32)
        tm = [aal("tm%d" % i, [128, TT], F32) for i in range(2)]
        tmp = (([s[0] for s in sq], [s[1] for s in sq]), (rs, b_rs), ([s[0] for s in tm], [s[1] for s in tm]))
        q4, b_q4 = aal("q4", [128, 4, TT], F32)
        k4, b_k4 = aal("k4", [128, 4, TT], F32)
        R = 2
        sig = [aal("sig%d" % i, [128, TT], F32) for i in range(R)]
        lf = [aal("lf%d" % i, [128, TT], F32) for i in range(R)]
        ka = [aal("ka%d" % i, [128, TT], F32) for i in range(R)]
        bc = [aal("bc%d" % i, [128, TT], F32) for i in range(R)]
        eb = [aal("eb%d" % i, [128, TT], F32) for i in range(R)]
        enb = [aal("enb%d" % i, [128, TT], F32) for i in range(R)]
        qst = [aal("qst%d" % i, [128, TT], BF16) for i in range(R)]
        kst = [aal("kst%d" % i, [128, TT], BF16) for i in range(R)]
        kss = [aal("kss%d" % i, [128, TT], BF16) for i in range(R)]
        sst = [aal("sst%d" % i, [128, 4, 128], BF16) for i in range(R)]
        vst = [aal("vst%d" % i, [128, TT], BF16) for i in range(R)]
        gst = [aal("gst0", [128, 4, TT], BF16)] * 2
        decst, b_decst = aal("decst", [128, 16, 2, 16], F32)
        cosT, b_cosT = aal("cosT", [128, TT], F32)
        sinT, b_sinT = aal("sinT", [128, TT], F32)
        rb, r1, r2 = vst, sig, lf
        rot = [0]
        w = W[("in", l)]
        x_src = xT_in if first else XT

        def emit_hd(ti, hh, d, qsrc, bq, ksrc, bk, ebv, benb_r, enbv, decv, decr):
            if "hd" in skip:
                return
            i = rot[0] % R
            rot[0] += 1
            blk0 = ti * 4
            tgtQ, tgtK, tgtS = (QF, KF, SF) if d == 0 else (QB, KB, SB)
            q_, bq_ = qst[i]
            k_, bk_ = kst[i]
            s_, bs_ = kss[i]
            st_, bst_ = sst[i]
            v3 = lambda a: a.rearrange("p (c t) -> p c t", t=32)
            stt(v3(q_), v3(qsrc), KS, ebv, ALU.mult, ALU.mult, [bq] + benb_r, [bq_])
            tt("dve", v3(k_), v3(ksrc), enbv, ALU.mult, [bk] + benb_r, [bk_])
            tt("dve", s_.rearrange("p (c t) -> p c t", t=32), k_.rearrange("p (c t) -> p c t", t=32),
               decv, ALU.mult, [bk_] + decr, [bs_])
            dma("sp", tgtQ[blk0:blk0 + 4, :, hh, :].rearrange("b p t -> p b t"), q_.rearrange("p (b t) -> p b t", t=128),
                bq_, [bq_], [db(tgtQ.tensor.name, ti)])
            dma("sp", tgtK[blk0:blk0 + 4, :, hh, :].rearrange("b p t -> p b t"), k_.rearrange("p (b t) -> p b t", t=128),
                bk_, [bk_], [db(tgtK.tensor.name, ti)])
            for j in range(4):
                P.op("pe", (lambda j=j, s_=s_: (lambda e: e.transpose(psT[:, j * 128:(j + 1) * 128], s_[:, j * 128:(j + 1) * 128], ident[:]))),
                     [bs_, b_ident], [b_psT])
            cp("act", st_.rearrange("p b k -> p (b k)"), psT[:, 0:512], [b_psT], [bst_])
            dma("sp", tgtS[blk0:blk0 + 4, :, hh, :].rearrange("b p k -> p b k"), st_, bst_, [bst_], [db(tgtS.tensor.name, ti)])

        for ti in range(NT):
            cnd = cond_of_tile(ti)
            sample = ti < NTS
            dma("sp", xt, stream_T(x_src, ti), b_xt, [db(x_src.tensor.name, ti)], [b_xt])
            if first and "xst" not in skip:
                dma("sp", stream_T(XT, ti), xt, b_xt, [b_xt], [db("XT", ti)])
            if "norm" not in skip:
                rmsnorm_mod(xt, b_xt, hT, b_hT, lambda c: (A1[:, l, cnd, c:c + 1], b_A1),
                            lambda c: (modT[:, l, cnd, c:c + 1], b_modT), tmp)
            if sample and "cs" not in skip:
                dma("sp", cosT, c_cos[:, ti * TT:(ti + 1) * TT], b_cosT, [], [b_cosT])
                dma("sp", sinT, c_sin[:, ti * TT:(ti + 1) * TT], b_sinT, [], [b_sinT])

            def proj4(cg):
                wt, wb_, nk = wload(w, cg, 0)
                outs = []
                for j in range(4):
                    pt, pb = newps()
                    mmk(pt[:, :], wt, slice(j * 128, (j + 1) * 128), hT, 16, [wb_, b_hT], [pb])
                    outs.append((pt, pb))
                return outs

            def projV(cg, hbase):
                if "V" in skip:
                    return
                wt, wb_, nk = wload(w, cg, 0)
                for tb in range(4):
                    pt, pb = newps()

                    def fn(e, pt=pt, wt=wt, tb=tb):
                        r = None
                        for kc in range(16):
                            r = e.matmul(pt[:, :], lhsT=hT[:, kc, tb * 128:(tb + 1) * 128], rhs=wt[:, kc, :],
                                         start=(kc == 0), stop=(kc == 15))
                        return r
                    P.op("pe", fn, [wb_, b_hT], [pb])
                    i = rot[0] % R
                    rot[0] += 1
                    v_, bv_ = vst[i]
                    cp("act", v_, pt[:, :], [pb], [bv_])
                    dma("sp", VV[ti * 4 + tb, :, hbase:hbase + 4, :], v_.rearrange("p (h v) -> p h v", v=128), bv_, [bv_],
                        [db("VV", ti)])

            def projG(cg, chbase):
                if "G" in skip:
                    return
                outs = proj4(cg)
                i = rot[0] % R
                rot[0] += 1
                g_, bg_ = gst[i]
                for j, (pt, pb) in enumerate(outs):
                    act(g_[:, j, :], pt[:, :], AF.Silu, [pb], [bg_])
                dma("sp", GG[:, chbase:chbase + 4, ti * TT:(ti + 1) * TT], g_, bg_, [bg_], [db("GG", ti)])

            for hq in range(0 if "A" in skip else 2):
                outs = proj4(0 + hq)
                for j, (pt, pb) in enumerate(outs):
                    act(q4[:, j, :], pt[:, :], AF.Silu, [pb], [b_q4])
                for d in range(2):
                    outs = proj4(2 + 2 * d + hq)
                    for j, (pt, pb) in enumerate(outs):
                        h = hq * 4 + j
                        i = rot[0] % R
                        sg, bsg = sig[i]
                        l_, bl_ = lf[i]
                        a_, ba_ = ka[i]
                        c_, bc_ = bc[i]
                        e1, be1 = eb[i]
                        e2, be2 = enb[i]
                        act(sg, pt[:, :], AF.Sigmoid, [pb], [bsg])
                        act(l_, sg, AF.Ln, [bsg, b_OML, b_LB], [bl_], scale=OML[:, le, d, h:h + 1], bias=LB[:, le, d, h:h + 1])
                        ts("dve", a_, sg, LBM[:, le, d, h:h + 1], OML[:, le, d, h:h + 1], ALU.mult, ALU.add,
                           [bsg, b_LBM, b_OML], [ba_])
                        P.op("dve", (lambda c_=c_, l_=l_: (lambda e: e.tensor_tensor_scan(out=c_, data0=resetm[:], data1=l_, initial=0.0,
                                                                                          op0=ALU.mult, op1=ALU.add))),
                             [bl_, b_resetm], [bc_])
                        c3 = c_.rearrange("p (c t) -> p c t", t=32)
                        if d == 1:
                            tt("dve", l_, l_, c_, ALU.subtract, [bl_, bc_], [bl_])
                            tt("dve", c3, l_.rearrange("p (c t) -> p c t", t=32), c3[:, :, 31:32].to_broadcast([128, 16, 32]),
                               ALU.add, [bl_, bc_], [bc_])
                        ts("dve", c_, c_, -80.0, None, ALU.max, None, [bc_], [bc_])
                        act(e1, c_, AF.Exp, [bc_], [be1])
                        act(e2, c_, AF.Exp, [bc_], [be2], scale=-1.0)
                        e13 = e1.rearrange("p (c t) -> p c t", t=32)
                        dcol = e13[:, :, 31:32] if d == 0 else e13[:, :, 0:1]
                        cp("dve", decst[:, h, d, :].unsqueeze(2), dcol, [be1], [b_decst])
                        emit_hd(ti, h, d, q4[:, j, :], b_q4, a_, ba_, e13, [be1, be2],
                                e2.rearrange("p (c t) -> p c t", t=32), dcol.to_broadcast([128, 16, 32]), [be1])
                projV(6 + hq, hq * 4)
                projG(8 + hq, hq * 4)
            for hq in range(0 if "B" in skip else 2):
                for (cgb, dst, bdst) in ((10, q4, b_q4), (12, k4, b_k4)):
                    outs = proj4(cgb + hq)
                    for j, (pt, pb) in enumerate(outs):
                        if not sample or "rope" in skip:
                            cp("act", dst[:, j, :], pt[:, :], [pb], [bdst])
                            continue
                        i = rot[0] % R
                        rot[0] += 1
                        rb_, brb = rb[i]
                        a1, ba1 = r1[i]
                        a2, ba2 = r2[i]
                        cp("act", a1, pt[:, :], [pb], [ba1])
                        cp("dve", rb_, a1, [ba1], [brb])
                        p2, pb2 = newps()
                        mm(p2[:, :], perm[:], rb_, [b_perm, brb], [pb2])
                        tt("dve", a2, p2[:, :], sinT, ALU.mult, [pb2, b_sinT], [ba2])
                        tt("dve", a1, a1, cosT, ALU.mult, [ba1, b_cosT], [ba1])
                        tt("dve", dst[:, j, :], a1, a2, ALU.add, [ba1, ba2], [bdst])
                for j in range(4):
                    hb = hq * 4 + j
                    for d in range(2):
                        ix = (le * 2 + d) * 8 + hb
                        ebv = EB32[:, ix, :].unsqueeze(1).to_broadcast([128, 16, 32])
                        enbv = ENB32[:, ix, :].unsqueeze(1).to_broadcast([128, 16, 32])
                        dcol = EB32[:, ix, 31:32] if d == 0 else EB32[:, ix, 0:1]
                        if "decb" not in skip:
                            P.op("dve", (lambda hb=hb, d=d, dcol=dcol: (lambda e: e.tensor_copy(out=decst[:, 8 + hb, d, :],
                                                                                                 in_=dcol.to_broadcast([128, 16])))),
                                 [b_EB32], [b_decst])
                        emit_hd(ti, 8 + hb, d, q4[:, j, :], b_q4, k4[:, j, :], b_k4, ebv, [b_EB32, b_ENB32], enbv,
                                dcol.unsqueeze(2).to_broadcast([128, 16, 32]), [b_EB32])
                projV(14 + hq, 8 + hq * 4)
                projG(16 + hq, 8 + hq * 4)
            if "dec" not in skip:
                dma("sp", DEC[ti], decst, b_decst, [b_decst], [db("DEC", ti)])


    def phaseS_even(l):
        le = l // 2
        areset()
        S32, b_S = aal("S32", [128, 32, 128], F32)
        Sb, b_Sb = aal("Sb", [128, 32, 128], BF16)
        NBF = 2
        lds = [[aal("ld%d_%d" % (i, k), [128, 16, 128], BF16) for k in range(8)] for i in range(NBF)]
        decs = [aal("decs%d" % i, [128, 16, 2, 16], F32) for i in range(2)]
        ost = [[aal("ost_%d" % d, [128, 16, 128], F32) for d in range(2)]] * 2
        pm = [aal("pm%d" % i, [128, 128], BF16) for i in range(4)]
        rot = [0]
        bS_h = [P.buf("S_%d" % si, "arena", b_S.lo + si * 512, b_S.lo + (si + 1) * 512) for si in range(32)]
        bSb_h = [P.buf("Sb_%d" % si, "arena", b_Sb.lo + si * 256, b_Sb.lo + (si + 1) * 256) for si in range(32)]
        bo_h = [[P.buf("o_%d_%d" % (d, h), "arena", ost[0][d][1].lo + h * 512, ost[0][d][1].lo + (h + 1) * 512) for h in range(16)]
                for d in range(2)]
        for (t0, L, pi) in seqs:
            nblk = L // 128
            b0 = t0 // 128
            if pi is None:
                for d in range(2):
                    dma("sp", S32[:, d * 16:d * 16 + 8, :], st_h_in[le, d].rearrange("h k v -> k h v"), b_S, [], [b_S])
                    dma("sp", S32[:, d * 16 + 8:d * 16 + 16, :], st_r_in[le, d].rearrange("h k v -> k h v"), b_S, [], [b_S])
            else:
                P.op("dve", lambda e: e.memset(S32[:], 0.0), [], [b_S])
            cp("act", Sb[:], S32[:], [b_S], [b_Sb])
            for i in range(nblk):
                fb, bb = b0 + i, b0 + nblk - 1 - i
                L_ = lds[i % NBF]
                srcs = [(QF, fb), (KF, fb), (SF, fb), (VV, fb), (QB, bb), (KB, bb), (SB, bb), (VV, bb)]
                for k, (src, blk) in enumerate(srcs):
                    dma("sp", L_[k][0], src[blk], L_[k][1], [db(src.tensor.name, blk // 4)], [L_[k][1]])
                dcs = []
                for d, blk in ((0, fb), (1, bb)):
                    dc_, bdc_ = decs[d]
                    if i == 0 or (blk % 4 == (0 if d == 0 else 3)):
                        dma("sp", dc_, DEC[blk // 4], bdc_, [db("DEC", blk // 4)], [bdc_])
                    dcs.append((dc_, bdc_))
                for d in range(2):
                    q_, bq_ = L_[4 * d + 0]
                    k_, bk_ = L_[4 * d + 1]
                    s_, bs_ = L_[4 * d + 2]
                    v_, bv_ = L_[4 * d + 3]
                    blk = fb if d == 0 else bb
                    o_, bo_ = ost[i % 2][d]
                    msk, bmsk = (maskf, b_maskf) if d == 0 else (maskb, b_maskb)
                    dc_, bdc_ = dcs[d]

                    def hbody(h, slot, d=d, q_=q_, bq_=bq_, k_=k_, bk_=bk_, s_=s_, bs_=bs_, v_=v_, bv_=bv_, blk=blk, o_=o_,
                              msk=msk, bmsk=bmsk, dc_=dc_, bdc_=bdc_):
                        sp_, bsp_ = psums[slot * 3 + 0]
                        op_, bop_ = psums[slot * 3 + 1]
                        p_, bp_ = pm[rot[0] % 4]
                        rot[0] += 1
                        si = d * 16 + h
                        bSi, bSbi, boh = bS_h[si], bSb_h[si], bo_h[d][h]
                        mm(sp_[:, 0:128], k_[:, h, :], q_[:, h, :], [bk_, bq_], [bsp_])
                        yield
                        tt("dve", p_, sp_[:, 0:128], msk[:], ALU.mult, [bsp_, bmsk], [bp_])
                        yield
                        mm(op_[:, 0:128], v_[:, h, :], p_, [bv_, bp_], [bop_], start=True, stop=False)
                        corder = range(4) if d == 0 else range(3, -1, -1)
                        for n_, c in enumerate(corder):
                            mm(op_[:, c * 32:(c + 1) * 32], Sb[:, si, :], q_[:, h, c * 32:(c + 1) * 32], [bSbi, bq_], [bop_],
                               start=False, stop=(n_ == 3))
                            ds_, bds_ = psums[slot * 3 + 2]
                            tp = (96, 0) if c == 3 else None
                            mm(ds_[:, 0:128], s_[c * 32:(c + 1) * 32, h, :], v_[c * 32:(c + 1) * 32, h, :], [bs_, bv_], [bds_], tp=tp)
                            yield
                            cidx = (blk % 4) * 4 + c
                            stt(S32[:, si, :], S32[:, si, :], dc_[:, h, d, cidx:cidx + 1], ds_[:, 0:128], ALU.mult, ALU.add,
                                [bSi, bdc_, bds_], [bSi])
                            yield
                            cp("act", Sb[:, si, :], S32[:, si, :], [bSi], [bSbi])
                            yield
                        cp("act", o_[:, h, :], op_[:, 0:128], [bop_], [boh])
                        yield
                    lockstep(((lambda slot, h=h: hbody(h, slot)) for h in range(16)), 2)
                    tgt = OF if d == 0 else OB
                    dma("sp", tgt[:, :, blk * 128:(blk + 1) * 128], o_, ost[i % 2][d][1], bo_h[d], [db(tgt.tensor.name, blk // 4)])
            if pi is not None:
                for d in range(2):
                    dma("sp", nsh_out[pi, le, d].rearrange("h k v -> k h v"), S32[:, d * 16:d * 16 + 8, :], b_S, [b_S],
                        [db("nsh", (pi, le, d))])
                    dma("sp", nsr_out[pi, le, d].rearrange("h k v -> k h v"), S32[:, d * 16 + 8:d * 16 + 16, :], b_S, [b_S],
                        [db("nsr", (pi, le, d))])

    def phaseB(l, last):
        even = (l % 2 == 0)
        areset()
        xt, b_xt = aal("xt", [128, NCH, TT], F32)
        hT, b_hT = aal("hT", [128, NCH, TT], BF16)
        actT, b_actT = aal("actT", [128, NFC, TT], BF16)
        sq = [aal("sq%d" % i, [128, TT], BF16) for i in range(2)]
        rs, b_rs = aal("rs", [128, TT], F32)
        tm = [aal("tm%d" % i, [128, TT], F32) for i in range(2)]
        tmp = (([s[0] for s in sq], [s[1] for s in sq]), (rs, b_rs), ([s[0] for s in tm], [s[1] for s in tm]))
        of_ = [aal("of%d" % i, [128, TT], F32) for i in range(2)]
        ob_ = [aal("ob%d" % i, [128, TT], F32) for i in range(2)]
        gg_ = [aal("gg%d" % i, [128, TT], BF16) for i in range(2)]
        oo_ = [aal("oo%d" % i, [128, TT], F32) for i in range(2)]
        r2_ = [aal("r2%d" % i, [128, TT], F32) for i in range(2)]
        sg_ = r2_
        wo, wg_, wu_, wd_ = W[("out", l)], W[("g", l)], W[("u", l)], W[("d", l)]
        for ti in range(NT):
            cnd = cond_of_tile(ti)
            tsl = slice(ti * TT, (ti + 1) * TT)
            dma("sp", xt, XT[:, :, tsl], b_xt, [db("XT", ti)], [b_xt])
            if even:
                for c in range(NCH):
                    i = c % 2
                    dma("sp", of_[i][0], OF[:, c, tsl], of_[i][1], [db("OF", ti)], [of_[i][1]])
                    dma("sp", ob_[i][0], OB[:, c, tsl], ob_[i][1], [db("OB", ti)], [ob_[i][1]])
                    dma("sp", gg_[i][0], GG[:, c, tsl], gg_[i][1], [db("GG", ti)], [gg_[i][1]])
                    o, bo = oo_[i]
                    tt("dve", o, of_[i][0], ob_[i][0], ALU.add, [of_[i][1], ob_[i][1]], [bo])
                    act(sq[i][0], o, AF.Square, [bo], [sq[i][1]])
                    pt, pb = newps()
                    mm(pt[:, :], ones[:], sq[i][0], [b_ones, sq[i][1]], [pb])
                    r_, br_ = r2_[i]
                    act(r_, pt[:, :], AF.Sqrt, [pb, b_epsc], [br_], scale=1.0 / 128, bias=epsc[:, 0:1])
                    P.op("dve", (lambda r_=r_: (lambda e: e.reciprocal(out=r_, in_=r_))), [br_], [br_])
                    tt("dve", o, o, r_, ALU.mult, [bo, br_], [bo])
                    tt("dve", hT[:, c, :], o, gg_[i][0], ALU.mult, [bo, gg_[i][1]], [b_hT])
            else:
                dma("sp", hT, YO[:, :, tsl], b_hT, [db("YO", ti)], [b_hT])
            for cg in range(4):
                wt, wb_, nk = wload(wo, cg, 0)
                for j in range(4):
                    ch = cg * 4 + j
                    pt, pb = newps()
                    mmk(pt[:, :], wt, slice(j * 128, (j + 1) * 128), hT, 16, [wb_, b_hT], [pb])
                    stt(xt[:, ch, :], pt[:, :], modT[:, l, cnd, 32 + ch:33 + ch], xt[:, ch, :], ALU.mult, ALU.add,
                        [pb, b_modT, b_xt], [b_xt])
            rmsnorm_mod(xt, b_xt, hT, b_hT, lambda c: (A2[:, l, cnd, c:c + 1], b_A2),
                        lambda c: (modT[:, l, cnd, 48 + c:49 + c], b_modT), tmp)
            for jg in range(11):
                wtg, wbg, _ = wload(wg_, jg, 0)
                wtu, wbu, _ = wload(wu_, jg, 0)
                for j in range(4):
                    fc = jg * 4 + j
                    pg, pbg = newps()
                    pu, pbu = newps()
                    mmk(pg[:, :], wtg, slice(j * 128, (j + 1) * 128), hT, 16, [wbg, b_hT], [pbg])
                    mmk(pu[:, :], wtu, slice(j * 128, (j + 1) * 128), hT, 16, [wbu, b_hT], [pbu])
                    s_, bs_ = sg_[fc % 2]
                    act(s_, pg[:, :], AF.Silu, [pbg], [bs_])
                    tt("dve", actT[:, fc, :], pu[:, :], s_, ALU.mult, [pbu, bs_], [b_actT])
            for cg in range(4):
                pts = [newps() for _ in range(4)]
                for kt in range(3):
                    wt, wb_, nk = wload(wd_, cg, kt)
                    for j in range(4):
                        mmk(pts[j][0][:, :], wt, slice(j * 128, (j + 1) * 128), actT, nk, [wb_, b_actT], [pts[j][1]],
                            start=(kt == 0), stop=(kt == 2), k0=kt * 16)
                for j in range(4):
                    ch = cg * 4 + j
                    stt(xt[:, ch, :], pts[j][0][:, :], modT[:, l, cnd, 80 + ch:81 + ch], xt[:, ch, :], ALU.mult, ALU.add,
                        [pts[j][1], b_modT, b_xt], [b_xt])
            if not last:
                dma("sp", XT[:, :, tsl], xt, b_xt, [b_xt], [db("XT", ti)])
            else:
                yt, b_yt = actT[:, 0:32, :].bitcast(F32) if False else (None, None)
                (sqv, b_sqv), (rsv, b_rsv), (tmv, b_tmv) = tmp
                pt, pb = newps()
                for c in range(NCH):
                    act(sqv[c % 2], xt[:, c, :], AF.Square, [b_xt], [b_sqv[c % 2]])
                    mm(pt[:, :], ones[:], sqv[c % 2], [b_ones, b_sqv[c % 2]], [pb], start=(c == 0), stop=(c == NCH - 1))
                act(rsv, pt[:, :], AF.Sqrt, [pb, b_epsc], [b_rsv], scale=1.0 / D, bias=epsc[:, 0:1])
                P.op("dve", lambda e: e.reciprocal(out=rsv, in_=rsv), [b_rsv], [b_rsv])
                for c in range(NCH):
                    stt(xt[:, c, :], xt[:, c, :], fnorm[:, c:c + 1], rsv, ALU.mult, ALU.mult, [b_xt, b_fnorm, b_rsv], [b_xt])
                dma("sp", yT_out[:, :, tsl], xt, b_xt, [b_xt], [db("yT", ti)])

    def phaseA_odd(l):
        lo = l // 2
        areset()
        xt, b_xt = aal("xt", [128, NCH, TT], F32)
        hT, b_hT = aal("hT", [128, NCH, TT], BF16)
        sq = [aal("sq%d" % i, [128, TT], BF16) for i in range(2)]
        rs, b_rs = aal("rs", [128, TT], F32)
        tm = [aal("tm%d" % i, [128, TT], F32) for i in range(2)]
        tmp = (([s[0] for s in sq], [s[1] for s in sq]), (rs, b_rs), ([s[0] for s in tm], [s[1] for s in tm]))
        gx = [aal("gx%d" % i, [128, TT], F32) for i in range(2)]
        g2 = [aal("g2%d" % i, [128, TT], F32) for i in range(2)]
        gs = [aal("gs%d" % i, [128, 4, TT], BF16) for i in range(2)]
        xs = [aal("xs%d" % i, [128, 4, TT], F32) for i in range(2)]
        w = W[("in", l)]
        for ti in range(NT):
            cnd = cond_of_tile(ti)
            tsl = slice(ti * TT, (ti + 1) * TT)
            dma("sp", xt, XT[:, :, tsl], b_xt, [db("XT", ti)], [b_xt])
            rmsnorm_mod(xt, b_xt, hT, b_hT, lambda c: (A1[:, l, cnd, c:c + 1], b_A1),
                        lambda c: (modT[:, l, cnd, c:c + 1], b_modT), tmp)
            for cg in range(8):
                wt, wb_, nk = wload(w, cg, 0)
                st_, bst_ = (gs if cg < 4 else xs)[cg % 2]
                for j in range(4):
                    pt, pb = newps()
                    mmk(pt[:, :], wt, slice(j * 128, (j + 1) * 128), hT, 16, [wb_, b_hT], [pb])
                    if cg < 4:
                        x_, bx_ = gx[j % 2]
                        y_, by_ = g2[j % 2]
                        cp("act", x_, pt[:, :], [pb], [bx_])
                        act(y_, pt[:, :], AF.Square, [pb], [by_])
                        ts("dve", y_, y_, 0.044715, 1.0, ALU.mult, ALU.add, [by_], [by_])
                        tt("dve", y_, y_, x_, ALU.mult, [by_, bx_], [by_])
                        act(y_, y_, AF.Sigmoid, [by_], [by_], scale=1.5957691216057308)
                        tt("dve", st_[:, j, :], y_, x_, ALU.mult, [by_, bx_], [bst_])
                    else:
                        cp("act", st_[:, j, :], pt[:, :], [pb], [bst_])
                if cg < 4:
                    dma("sp", GG[:, cg * 4:cg * 4 + 4, tsl], st_, bst_, [bst_], [db("GG", ti)])
                else:
                    dma("sp", XBR[:, (cg - 4) * 4:(cg - 4) * 4 + 4, tsl], st_, bst_, [bst_], [db("XBR", ti)])

    def phaseS_odd(l):
        lo = l // 2
        areset()
        rw = [[aal("rw%d_%d" % (d, g), [128, 16, 128], BF16) for g in range(2)] for d in range(2)]
        for d in range(2):
            for g, src in enumerate((rg_w_a, rg_w_i)):
                dma("pool", rw[d][g][0], src[lo, d].rearrange("n i j -> i n j"), rw[d][g][1], [], [rw[d][g][1]])
        xh = [aal("xh%d" % i, [128, TT + 4], F32) for i in range(3)]
        xc = [aal("xc%d" % i, [128, TT], F32) for i in range(3)]
        xcb = [aal("xcb%d" % i, [128, TT], BF16) for i in range(3)]
        rr = [aal("rr%d" % i, [128, TT], F32) for i in range(3)]
        gi = [aal("gi%d" % i, [128, TT], F32) for i in range(3)]
        aa = [aal("aa%d" % i, [128, TT], F32) for i in range(3)]
        a2 = [aal("a2%d" % i, [128, TT], F32) for i in range(3)]
        uu = [aal("uu%d" % i, [128, TT], F32) for i in range(3)]
        ar = [aal("ar%d" % i, [128, TT], F32) for i in range(3)]
        ur = [aal("ur%d" % i, [128, TT], F32) for i in range(3)]
        hh = [aal("hh%d" % i, [128, TT], F32) for i in range(3)]
        hf = [aal("hf%d" % i, [128, TT], F32) for i in range(3)]
        gg = [aal("gg%d" % i, [128, TT], BF16) for i in range(3)]
        yo = [aal("yo%d" % i, [128, TT], BF16) for i in range(3)]
        rot = [0]
        b_hcar_c = [P.buf("hcar_%d" % c) for c in range(NCH)]
        for d in range(2):
            for (t0, L, pi) in seqs:
                SEG = min(TT, L)
                nseg = L // SEG
                order = range(nseg) if d == 0 else range(nseg - 1, -1, -1)
                for n_, sgi in enumerate(order):
                    s0 = t0 + sgi * SEG
                    ti = s0 // TT
                    def cbody(c, i, d=d, t0=t0, L=L, pi=pi, SEG=SEG, nseg=nseg, n_=n_, sgi=sgi, s0=s0, ti=ti):
                        b_hcar = b_hcar_c[c]
                        xh_, bxh_ = xh[i]
                        lh = 2 if sgi > 0 else 0
                        rh = 1 if sgi < nseg - 1 else 0
                        P.op("dve", (lambda xh_=xh_: (lambda e: e.memset(xh_, 0.0))), [], [bxh_])
                        dma("sp", xh_[:, 2 - lh:2 + SEG + rh], XBR[:, c, s0 - lh:s0 + SEG + rh], bxh_,
                            [db("XBR", ti), db("XBR", max(ti - 1, 0)), db("XBR", min(ti + 1, NT - 1))], [bxh_])
                        xc_, bxc_ = xc[i]
                        xc_ = xc_[:, 0:SEG]
                        ts("dve", xc_, xh_[:, 0:SEG], convw[:, lo, 0, c:c + 1], convb[:, lo, c:c + 1], ALU.mult, ALU.add,
                           [bxh_, b_convw, b_convb], [bxc_])
                        for j in range(1, 4):
                            stt(xc_, xh_[:, j:j + SEG], convw[:, lo, j, c:c + 1], xc_, ALU.mult, ALU.add, [bxh_, b_convw, bxc_], [bxc_])
                        yield
                        xb_, bxb_ = xcb[i]
                        xb_ = xb_[:, 0:SEG]
                        cp("act", xb_, xc_, [bxc_], [bxb_])
                        pr, pbr = newps()
                        pg, pbg = newps()
                        mm(pr[:, 0:SEG], rw[d][0][0][:, c, :], xb_, [rw[d][0][1], bxb_], [pbr])
                        mm(pg[:, 0:SEG], rw[d][1][0][:, c, :], xb_, [rw[d][1][1], bxb_], [pbg])
                        yield
                        r_, br_ = rr[i]
                        g_, bg_ = gi[i]
                        a_, ba_ = aa[i]
                        q_, bq_ = a2[i]
                        u_, bu_ = uu[i]
                        r_, g_, a_, q_, u_ = [z[:, 0:SEG] for z in (r_, g_, a_, q_, u_)]
                        act(r_, pr[:, 0:SEG], AF.Sigmoid, [pbr, b_rgba], [br_], bias=rgba[:, lo, d, c:c + 1])
                        act(g_, pg[:, 0:SEG], AF.Sigmoid, [pbg, b_rgbi], [bg_], bias=rgbi[:, lo, d, c:c + 1])
                        act(a_, r_, AF.Exp, [br_, b_cA], [ba_], scale=cA[:, lo, d, c:c + 1])
                        act(q_, r_, AF.Exp, [br_, b_cA2], [bq_], scale=cA2[:, lo, d, c:c + 1])
                        yield
                        ts("dve", q_, q_, -1.0, 1.0, ALU.mult, ALU.add, [bq_], [bq_])
                        ts("dve", q_, q_, 0.0, None, ALU.max, None, [bq_], [bq_])
                        act(q_, q_, AF.Sqrt, [bq_], [bq_])
                        yield
                        tt("dve", u_, g_, xc_, ALU.mult, [bg_, bxc_], [bu_])
                        tt("dve", u_, u_, q_, ALU.mult, [bu_, bq_], [bu_])
                        yield
                        h_, bh_ = hh[i]
                        h_ = h_[:, 0:SEG]
                        if n_ == 0:
                            if pi is None:
                                init, binit = h0T[:, lo, d, c:c + 1], b_h0T
                            else:
                                init, binit = 0.0, None
                        else:
                            init, binit = hcar[:, c:c + 1], b_hcar
                        rds = [ba_, bu_] + ([binit] if binit is not None else [])
                        if d == 0:
                            P.op("dve", (lambda h_=h_, a_=a_, u_=u_, init=init: (lambda e: e.tensor_tensor_scan(
                                out=h_, data0=a_, data1=u_, initial=init, op0=ALU.mult, op1=ALU.add))), rds, [bh_])
                            cp("dve", hcar[:, c:c + 1], h_[:, SEG - 1:SEG], [bh_], [b_hcar])
                            dma("sp", HF[:, c, s0:s0 + SEG], h_, bh_, [bh_], [db("HF", ti)])
                            if pi is not None and n_ == nseg - 1:
                                cp("dve", nsl[:, pi, lo, 0, c:c + 1], h_[:, SEG - 1:SEG], [bh_], [b_nsl])
                        else:
                            ar_, bar_ = ar[i]
                            ur_, bur_ = ur[i]
                            ar_, ur_ = ar_[:, 0:SEG], ur_[:, 0:SEG]
                            cp("dve", ar_, a_[:, ::-1], [ba_], [bar_])
                            cp("dve", ur_, u_[:, ::-1], [bu_], [bur_])
                            rds = [bar_, bur_] + ([binit] if binit is not None else [])
                            P.op("dve", (lambda h_=h_, ar_=ar_, ur_=ur_, init=init: (lambda e: e.tensor_tensor_scan(
                                out=h_, data0=ar_, data1=ur_, initial=init, op0=ALU.mult, op1=ALU.add))), rds, [bh_])
                            cp("dve", hcar[:, c:c + 1], h_[:, SEG - 1:SEG], [bh_], [b_hcar])
                            if pi is not None and n_ == nseg - 1:
                                cp("dve", nsl[:, pi, lo, 1, c:c + 1], h_[:, SEG - 1:SEG], [bh_], [b_nsl])
                            f_, bf_ = hf[i]
                            f_ = f_[:, 0:SEG]
                            gg_, bgg_ = gg[i]
                            gg_ = gg_[:, 0:SEG]
                            y_, by_ = yo[i]
                            y_ = y_[:, 0:SEG]
                            dma("sp", f_, HF[:, c, s0:s0 + SEG], bf_, [db("HF", ti)], [bf_])
                            dma("sp", gg_, GG[:, c, s0:s0 + SEG], bgg_, [db("GG", ti)], [bgg_])
                            tt("dve", f_, f_, h_[:, ::-1], ALU.add, [bf_, bh_], [bf_])
                            tt("dve", y_, f_, gg_, ALU.mult, [bf_, bgg_], [by_])
                            dma("sp", YO[:, c, s0:s0 + SEG], y_, by_, [by_], [db("YO", ti)])
                        yield
                    lockstep(((lambda slot, c=c: cbody(c, slot)) for c in range(NCH)), 3)

    class Stop(Exception):
        pass

    def chk(tag):
        if stop == tag:
            raise Stop()
    try:
        chk("pre")
        for l in range(4):
            for k in (("in",) if lite else ("in", "out", "g", "u", "d")):
                wconvert(W[(k, l)])
            if l % 2 == 0:
                phaseA_even(l, l == 0)
                chk("A%d" % l)
                phaseS_even(l)
            else:
                phaseA_odd(l)
                chk("A%d" % l)
                phaseS_odd(l)
            chk("S%d" % l)
            phaseB(l, l == 3)
            chk("B%d" % l)
    except Stop:
        pass
    if "modT" in dump:
        dmod = dout("modT_o", [128, 4, 2, 96])
        dma("sp", dmod, modT[:], b_modT, [b_modT], [db("yT", "modT")])
    dma("sp", nsl_out, nsl[:], b_nsl, [b_nsl], [db("nsl_out", 0)])
    outs = [b for (k, b) in dbufs.items() if k[0] in ("yT", "nsh", "nsr", "nsl_out")]
    P.op("sp", lambda e: e.nop(), outs, [])

    P.resolve()
    slots = P.dma_slots()
    sems = {}
    for k in ("pe", "act", "dve", "pool", "sp"):
        sems[("e", k)] = es.enter_context(nc.semaphore("s_" + k))
    for n_, sid in enumerate(slots):
        sems[("d", sid)] = es.enter_context(nc.semaphore("d%s%d" % sid))
    with nc.Block() as block:
        @block.sync
        def _(e):
            P.emit("sp", e, sems)

        @block.tensor
        def _(e):
            P.emit("pe", e, sems)

        @block.scalar
        def _(e):
            P.emit("act", e, sems)

        @block.vector
        def _(e):
            P.emit("dve", e, sems)

        @block.gpsimd
        def _(e):
            P.emit("pool", e, sems)
    es.close()
    return nc


def _fm(v):
    v = np.asarray(v, np.float32)
    lead = v.shape[:-1]
    v = v.reshape(lead + (16, 128))
    return np.ascontiguousarray(np.moveaxis(v, -1, 0))


def _consts(LS):
    bf = ml_dtypes.bfloat16
    c = {}
    c["c_ident"] = np.eye(128, dtype=np.float32).astype(bf)
    c["c_ones"] = np.ones((128, 128), np.float32).astype(bf)
    i = np.arange(128)
    partner = np.where((i % 64) < 32, i + 32, i - 32)
    pm = np.zeros((128, 128), np.float32)
    pm[partner, i] = 1.0
    c["c_perm"] = pm.astype(bf)
    s, t = np.meshgrid(i, i, indexing="ij")
    same = (s // 32) == (t // 32)
    c["c_maskf"] = (same & (s <= t)).astype(np.float32).astype(bf)
    c["c_maskb"] = (same & (s >= t)).astype(np.float32).astype(bf)
    r = np.ones((128, TT), np.float32)
    r[:, 0::32] = 0.0
    c["c_reset"] = r
    pos = np.zeros((128, 2, 32), np.float32)
    pos[:, 0, :] = np.arange(1, 33)
    pos[:, 1, :] = 32 - np.arange(32)
    c["c_pos"] = pos
    tt_ = np.arange(LS)
    rows = (tt_ // 64).astype(np.float32)
    cols = (tt_ % 64).astype(np.float32)
    inv = (10000.0 ** (-np.arange(32, dtype=np.float32) / 32)).astype(np.float32)
    cosT = np.zeros((128, LS), np.float32)
    sinT = np.zeros((128, LS), np.float32)
    for f in range(128):
        p = rows if f < 64 else cols
        ang = (p * inv[(f % 64) % 32]).astype(np.float32)
        cosT[f] = np.cos(ang)
        sinT[f] = np.sin(ang) * (-1.0 if (f % 64) < 32 else 1.0)
    c["c_cos"] = cosT
    c["c_sin"] = sinT
    return c


_NC_CACHE = {}


def kernel(**inp):
    nc, in_maps, post = _prep(inp)
    res = run_bass_kernel_spmd(nc, in_maps, core_ids=list(range(len(in_maps))))
    return post(res)


def _prep(inp, **bkw):
    x_prompt = np.asarray(inp["x_prompt"], np.float32)
    x_sample = np.asarray(inp["x_sample"], np.float32)
    NS, LS = x_sample.shape[0], x_sample.shape[1]
    NPT, LP = x_prompt.shape[0], x_prompt.shape[1]
    n = NS
    NP = NPT // n
    key = (LS, NP, LP)
    if bkw:
        nc = build(LS, NP, LP, **bkw)
    else:
        if key not in _NC_CACHE:
            _NC_CACHE[key] = build(LS, NP, LP)
        nc = _NC_CACHE[key]
    consts = _consts(LS)
    shared = {}
    for k in ("w_mod", "w_in_even", "w_out_even", "w_in_odd", "w_out_odd", "w_ffn_gate", "w_ffn_up", "w_ffn_down",
              "rg_w_a", "rg_w_i"):
        shared[k] = np.ascontiguousarray(np.asarray(inp[k], np.float32))
    shared["bmodT"] = np.ascontiguousarray(np.asarray(inp["b_mod"], np.float32).reshape(4, 96, 128).transpose(2, 0, 1))
    shared["nmixT"] = _fm(inp["norm_mix"])
    shared["nffnT"] = _fm(inp["norm_ffn"])
    shared["fnormT"] = _fm(inp["final_norm"])
    shared["lblT"] = np.ascontiguousarray(np.asarray(inp["hgrn_lb_logits"], np.float32).reshape(2, 2, 8, 128).transpose(3, 0, 1, 2))
    shared["decl"] = np.ascontiguousarray(np.asarray(inp["ret_decay_logit"], np.float32).reshape(1, 32))
    shared["convwT"] = _fm(inp["conv_w"])
    shared["convbT"] = _fm(inp["conv_b"])
    shared["rgbaT"] = _fm(np.asarray(inp["rg_b_a"], np.float32).reshape(2, 2, 2048))
    shared["rgbiT"] = _fm(np.asarray(inp["rg_b_i"], np.float32).reshape(2, 2, 2048))
    shared["aparT"] = _fm(inp["rg_a_param"])
    shared.update(consts)
    c = np.asarray(inp["c"], np.float32)
    c_ctx = np.asarray(inp["c_ctx"], np.float32)
    in_maps = []
    for b in range(n):
        X = np.concatenate([x_sample[b]] + [x_prompt[b * NP + i] for i in range(NP)], axis=0)
        m = dict(shared)
        m["xT"] = np.ascontiguousarray(X.reshape(-1, 16, 128).transpose(2, 1, 0))
        m["condT"] = _fm(np.stack([c[b], c_ctx], axis=0)).transpose(0, 2, 1).copy()
        m["st_h"] = np.ascontiguousarray(np.asarray(inp["state_hgrn"], np.float32)[b])
        m["st_r"] = np.ascontiguousarray(np.asarray(inp["state_ret"], np.float32)[b])
        m["h0T"] = _fm(np.asarray(inp["state_rglru"], np.float32)[b])
        in_maps.append(m)
    def post(res):
        return _post(res, n, NP, NPT, NS, LS, LP)
    return nc, in_maps, post


def _post(res, n, NP, NPT, NS, LS, LP):
    y_prompt = np.zeros((NPT, LP, D), np.float32)
    y_sample = np.zeros((NS, LS, D), np.float32)
    nsh = np.zeros((NPT, 2, 2, 8, 128, 128), np.float32)
    nsr = np.zeros((NPT, 2, 2, 8, 128, 128), np.float32)
    nsl = np.zeros((NPT, 2, 2, D), np.float32)
    for b in range(n):
        r = res.results[b]
        Y = np.asarray(r["yT"]).transpose(2, 1, 0).reshape(-1, D)
        y_sample[b] = Y[:LS]
        for i in range(NP):
            y_prompt[b * NP + i] = Y[LS + i * LP:LS + (i + 1) * LP]
        nsh[b * NP:(b + 1) * NP] = np.asarray(r["nsh"])
        nsr[b * NP:(b + 1) * NP] = np.asarray(r["nsr"])
        nsl[b * NP:(b + 1) * NP] = np.asarray(r["nsl"]).transpose(1, 2, 3, 4, 0).reshape(NP, 2, 2, D)
    return (y_prompt, y_sample, nsh, nsr, nsl)
```

```python
import numpy as np
import ml_dtypes
from contextlib import ExitStack
import concourse.bass as bass
import concourse.mybir as mybir
from concourse.bass_utils import run_bass_kernel_spmd

F32 = mybir.dt.float32
BF16 = mybir.dt.bfloat16
AF = mybir.ActivationFunctionType
ALU = mybir.AluOpType
D = 2048
NCH = 16
DFF = 5632
NFC = 44
TT = 512
EPS = 1e-6
KS = 128 ** -0.5


class Buf:
    def __init__(self, name, arena=None, lo=0, hi=0, const=False):
        self.name, self.arena, self.lo, self.hi, self.const = name, arena, lo, hi, const
        self.lw = None
        self.rd = {}
        self.ov = [self]


class Op:
    __slots__ = ("eng", "fn", "reads", "writes", "dma", "ndma", "sig", "sigval", "waits", "dval")


class Prog:
    def __init__(self):
        self.ops = []
        self.arena_bufs = {}

    def buf(self, name, arena=None, lo=0, hi=0, const=False):
        b = Buf(name, arena, lo, hi, const)
        if arena is not None:
            lst = self.arena_bufs.setdefault(arena, [])
            for o in lst:
                if o.lo < hi and lo < o.hi:
                    o.ov.append(b)
                    b.ov.append(o)
            lst.append(b)
        return b

    def op(self, eng, fn, reads=(), writes=(), dma=None, ndma=1):
        if fn.__defaults__ is not None and fn.__code__.co_argcount == len(fn.__defaults__):
            fn = fn()
        o = Op()
        o.eng, o.fn, o.reads, o.writes, o.dma, o.ndma = eng, fn, list(reads), list(writes), dma, ndma
        o.sig, o.sigval, o.waits, o.dval = False, 0, [], 0
        self.ops.append(o)
        return o

    NSD = 96

    def gid(self, slot, eng):
        d = self.gids.setdefault(eng, {})
        n = 40 if eng == "sp" else 8
        g = d.setdefault(id(slot), len(d) % n)
        return (eng, g)

    def resolve(self):
        self.gids = {}
        for o in self.ops:
            if o.dma is not None:
                o.dma = self.gid(o.dma, o.eng)
        dcnt = {}
        for o in self.ops:
            deps = {}
            for b in o.reads:
                for ob in b.ov:
                    if ob.lw is not None:
                        deps[id(ob.lw)] = ob.lw
            for b in o.writes:
                for ob in b.ov:
                    if ob.lw is not None:
                        deps[id(ob.lw)] = ob.lw
                    for r in ob.rd.values():
                        deps[id(r)] = r
            deps.pop(id(o), None)
            for p in deps.values():
                if p.dma is not None:
                    o.waits.append((("d", p.dma), dcnt[p.dma] * 16))
                else:
                    if p.eng == o.eng == "pe" and o.dma is None:
                        continue
                    p.sig = True
                    o.waits.append((("e", p.eng), p))
            if o.dma is not None:
                if dcnt.get(o.dma, 0) > 0:
                    o.waits.append((("d", o.dma), dcnt[o.dma] * 16))
                dcnt[o.dma] = dcnt.get(o.dma, 0) + o.ndma
            key = o.eng if o.dma is None else ("d", o.dma)
            for b in o.reads:
                if not b.const:
                    b.rd[key] = o
            for b in o.writes:
                for ob in b.ov:
                    ob.lw = o
                    ob.rd = {}
        cnt = {}
        for o in self.ops:
            if o.dma is None and o.sig:
                cnt[o.eng] = cnt.get(o.eng, 0) + 1
                o.sigval = cnt[o.eng]
        for o in self.ops:
            o.waits = [(k, (v.sigval if isinstance(v, Op) else v)) for k, v in o.waits]

    def dma_slots(self):
        return sorted(set(o.dma for o in self.ops if o.dma is not None))

    def emit(self, eng_name, e, sems):
        waited = {}
        for o in self.ops:
            if o.eng != eng_name:
                continue
            for k, v in o.waits:
                if waited.get(k, 0) < v:
                    e.wait_ge(sems[k], v)
                    waited[k] = v
            r = o.fn(e)
            if o.dma is not None:
                rl = r if isinstance(r, (list, tuple)) else [r]
                assert len(rl) == o.ndma, (len(rl), o.ndma)
                for ins in rl:
                    ins.then_inc(sems[("d", o.dma)], 16)
            elif o.sig:
                last = r[-1] if isinstance(r, (list, tuple)) else r
                last.then_inc(sems[("e", o.eng)], 1)


def lockstep(makers, K):
    active, free, it, done = [], list(range(K)), iter(makers), False
    while True:
        while free and not done:
            try:
                mk = next(it)
            except StopIteration:
                done = True
                break
            sl = free.pop(0)
            active.append((sl, mk(sl)))
        if not active:
            break
        for item in list(active):
            try:
                next(item[1])
            except StopIteration:
                active.remove(item)
                free.append(item[0])


def build(LS, NP, LP=256, stop=None, dump=(), lite=False, skip=()):
    T = LS + NP * LP
    NT = T // TT
    NB = T // 128
    NTS = LS // TT
    seqs = [(0, LS, None)] + [(LS + i * LP, LP, i) for i in range(NP)]
    nc = bass.Bass("TRN2", target_bir_lowering=False)
    P = Prog()
    es = ExitStack()

    def din(name, shape, dt=F32):
        return nc.dram_tensor(name, list(shape), dt, kind="ExternalInput").ap()

    def dout(name, shape, dt=F32):
        return nc.dram_tensor(name, list(shape), dt, kind="ExternalOutput").ap()

    def dscr(name, shape, dt):
        kind = "ExternalOutput" if name in dump else "Internal"
        return nc.dram_tensor(name, list(shape), dt, kind=kind).ap()

    xT_in = din("xT", [128, NCH, T])
    condT_in = din("condT", [128, NCH, 2])
    st_h_in = din("st_h", [2, 2, 8, 128, 128])
    st_r_in = din("st_r", [2, 2, 8, 128, 128])
    h0T_in = din("h0T", [128, 2, 2, NCH])
    _din = din

    def din(name, shape, dt=F32):
        if lite and name.startswith("w_") and name != "w_in_even":
            return None
        return _din(name, shape, dt)
    w_mod = din("w_mod", [4, D, 6 * D])
    w_in_even = din("w_in_even", [2, D, 9216])
    w_out_even = din("w_out_even", [2, D, D])
    w_in_odd = din("w_in_odd", [2, D, 2 * D])
    w_out_odd = din("w_out_odd", [2, D, D])
    w_g = din("w_ffn_gate", [4, D, DFF])
    w_u = din("w_ffn_up", [4, D, DFF])
    w_d = din("w_ffn_down", [4, DFF, D])
    din = _din
    rg_w_a = din("rg_w_a", [2, 2, 16, 128, 128])
    rg_w_i = din("rg_w_i", [2, 2, 16, 128, 128])
    bmodT_in = din("bmodT", [128, 4, 96])
    nmixT_in = din("nmixT", [128, 4, NCH])
    nffnT_in = din("nffnT", [128, 4, NCH])
    fnormT_in = din("fnormT", [128, NCH])
    lblT_in = din("lblT", [128, 2, 2, 8])
    decl_in = din("decl", [1, 32])
    convwT_in = din("convwT", [128, 2, 4, NCH])
    convbT_in = din("convbT", [128, 2, NCH])
    rgbaT_in = din("rgbaT", [128, 2, 2, NCH])
    rgbiT_in = din("rgbiT", [128, 2, 2, NCH])
    aparT_in = din("aparT", [128, 2, 2, NCH])
    c_ident = din("c_ident", [128, 128], BF16)
    c_ones = din("c_ones", [128, 128], BF16)
    c_perm = din("c_perm", [128, 128], BF16)
    c_maskf = din("c_maskf", [128, 128], BF16)
    c_maskb = din("c_maskb", [128, 128], BF16)
    c_reset = din("c_reset", [128, TT])
    c_pos = din("c_pos", [128, 2, 32])
    c_cos = din("c_cos", [128, LS])
    c_sin = din("c_sin", [128, LS])
    yT_out = dout("yT", [128, NCH, T])
    nsh_out = dout("nsh", [NP, 2, 2, 8, 128, 128])
    nsr_out = dout("nsr", [NP, 2, 2, 8, 128, 128])
    nsl_out = dout("nsl", [128, NP, 2, 2, NCH])
    XT = dscr("XT", [128, NCH, T], F32)
    QF = dscr("QF", [NB, 128, 16, 128], BF16)
    QB = dscr("QB", [NB, 128, 16, 128], BF16)
    KF = dscr("KF", [NB, 128, 16, 128], BF16)
    KB = dscr("KB", [NB, 128, 16, 128], BF16)
    SF = dscr("SF", [NB, 128, 16, 128], BF16)
    SB = dscr("SB", [NB, 128, 16, 128], BF16)
    VV = dscr("VV", [NB, 128, 16, 128], BF16)
    DEC = dscr("DEC", [NT, 128, 16, 2, 16], F32)
    GG = dscr("GG", [128, NCH, T], BF16)
    OF = dscr("OF", [128, NCH, T], F32)
    OB = dscr("OB", [128, NCH, T], F32)
    XBR = dscr("XBR", [128, NCH, T], F32)
    HF = dscr("HF", [128, NCH, T], F32)
    YO = dscr("YO", [128, NCH, T], BF16)

    dbufs = {}

    def db(name, idx=0):
        k = (name, idx)
        if k not in dbufs:
            dbufs[k] = P.buf("%s_%s" % (name, idx))
        return dbufs[k]

    def sb(name, shape, dt, const=False):
        t = es.enter_context(nc.sbuf_tensor(name, list(shape), dt))
        return t, P.buf(name, const=const)

    ident, b_ident = sb("ident", [128, 128], BF16, True)
    ones, b_ones = sb("ones", [128, 128], BF16, True)
    perm, b_perm = sb("perm", [128, 128], BF16, True)
    maskf, b_maskf = sb("maskf", [128, 128], BF16, True)
    maskb, b_maskb = sb("maskb", [128, 128], BF16, True)
    resetm, b_resetm = sb("resetm", [128, TT], F32, True)
    posc, b_posc = sb("posc", [128, 2, 32], F32, True)
    epsc, b_epsc = sb("epsc", [128, 4], F32, True)
    condT, b_condT = sb("condT_s", [128, NCH, 2], F32)
    scond, b_scond = sb("scond", [128, NCH, 2], BF16)
    modT, b_modT = sb("modT", [128, 4, 2, 96], F32)
    bmodT, b_bmodT = sb("bmodT_s", [128, 4, 96], F32)
    nmix, b_nmix = sb("nmix", [128, 4, NCH], F32)
    nffn, b_nffn = sb("nffn", [128, 4, NCH], F32)
    fnorm, b_fnorm = sb("fnorm", [128, NCH], F32)
    A1, b_A1 = sb("A1", [128, 4, 2, NCH], F32)
    A2, b_A2 = sb("A2", [128, 4, 2, NCH], F32)
    lbl, b_lbl = sb("lbl", [128, 2, 2, 8], F32)
    LB, b_LB = sb("LB", [128, 2, 2, 8], F32)
    OML, b_OML = sb("OML", [128, 2, 2, 8], F32)
    LBM, b_LBM = sb("LBM", [128, 2, 2, 8], F32)
    lng, b_lng = sb("lng", [128, 32], F32)
    nlng, b_nlng = sb("nlng", [128, 32], F32)
    EB32, b_EB32 = sb("EB32", [128, 32, 32], F32)
    ENB32, b_ENB32 = sb("ENB32", [128, 32, 32], F32)
    convw, b_convw = sb("convw", [128, 2, 4, NCH], F32)
    convb, b_convb = sb("convb", [128, 2, NCH], F32)
    rgba, b_rgba = sb("rgba", [128, 2, 2, NCH], F32)
    rgbi, b_rgbi = sb("rgbi", [128, 2, 2, NCH], F32)
    apar, b_apar = sb("apar", [128, 2, 2, NCH], F32)
    cA, b_cA = sb("cA", [128, 2, 2, NCH], F32)
    cA2, b_cA2 = sb("cA2", [128, 2, 2, NCH], F32)
    h0T, b_h0T = sb("h0T_s", [128, 2, 2, NCH], F32)
    nsl, b_nsl = sb("nsl_s", [128, NP, 2, 2, NCH], F32)
    hcar, b_hcar = sb("hcar", [128, NCH], F32)
    NW = 4
    wbufs = [sb("wbuf%d" % i, [128, 16, 512], BF16) for i in range(NW)]
    wctr = [0]
    psums = []
    for i in range(7):
        t = es.enter_context(nc.psum_tensor("ps%d" % i, [128, 512], F32))
        psums.append((t, P.buf("ps%d" % i)))
    psT = es.enter_context(nc.psum_tensor("psT", [128, 1024], BF16))
    b_psT = P.buf("psT")
    pctr = [0]

    def newps():
        r = psums[pctr[0] % 7]
        pctr[0] += 1
        return r

    ARENA = 120 * 1024
    arena = es.enter_context(nc.sbuf_tensor("arena", [128, ARENA // 4], F32))
    acur = [0]

    def areset():
        acur[0] = 0

    def aal(name, shape, dt):
        n = 1
        for s_ in shape[1:]:
            n *= s_
        nb = n * (4 if dt == F32 else 2)
        nb = (nb + 63) // 64 * 64
        lo = acur[0]
        acur[0] += nb
        assert acur[0] <= ARENA, ("arena overflow", name, acur[0])
        v = arena[:, lo // 4:(lo + nb) // 4]
        if dt != F32:
            v = v.bitcast(dt)
        v = v[:, 0:n]
        if len(shape) == 3:
            v = v.rearrange("p (a b) -> p a b", b=shape[2])
        elif len(shape) == 4:
            v = v.rearrange("p (a b c) -> p a b c", b=shape[2], c=shape[3])
        return v, P.buf(name, "arena", lo, lo + nb)

    def dma(eng, out, in_, slot, reads, writes):
        P.op(eng, lambda e: e.dma_start(out=out, in_=in_), reads, writes, dma=slot)

    def act(out, in_, func, reads, writes, scale=None, bias=None):
        kw = {}
        if scale is not None:
            kw["scale"] = scale
        if bias is not None:
            kw["bias"] = bias
        P.op("act", lambda e: e.activation(out=out, in_=in_, func=func, **kw), reads, writes)

    def tt(eng, out, in0, in1, op, reads, writes):
        P.op(eng, lambda e: e.tensor_tensor(out=out, in0=in0, in1=in1, op=op), reads, writes)

    def ts(eng, out, in0, s1, s2, op0, op1, reads, writes):
        if op1 is None:
            P.op(eng, lambda e: e.tensor_scalar(out=out, in0=in0, scalar1=s1, scalar2=None, op0=op0), reads, writes)
        else:
            P.op(eng, lambda e: e.tensor_scalar(out=out, in0=in0, scalar1=s1, scalar2=s2, op0=op0, op1=op1), reads, writes)

    def stt(out, in0, scalar, in1, op0, op1, reads, writes):
        P.op("dve", lambda e: e.scalar_tensor_tensor(out=out, in0=in0, scalar=scalar, in1=in1, op0=op0, op1=op1), reads, writes)

    def cp(eng, out, in_, reads, writes):
        if eng == "act":
            P.op("act", lambda e: e.copy(out=out, in_=in_), reads, writes)
        else:
            P.op(eng, lambda e: e.tensor_copy(out=out, in_=in_), reads, writes)

    def mm(out, lhsT, rhs, reads, writes, start=True, stop=True, tp=None):
        if tp is None:
            P.op("pe", lambda e: e.matmul(out, lhsT=lhsT, rhs=rhs, start=start, stop=stop), reads, writes)
        else:
            P.op("pe", lambda e: e.matmul(out, lhsT=lhsT, rhs=rhs, start=start, stop=stop, tile_position=tp), reads, writes)

    def mmk(out, wt, wcols, rhs3, nk, reads, writes, start=True, stop=True, k0=0):
        def fn(e):
            r = None
            for kc in range(nk):
                r = e.matmul(out, lhsT=wt[:, kc, wcols], rhs=rhs3[:, k0 + kc, :],
                             start=(start and kc == 0), stop=(stop and kc == nk - 1))
            return r
        P.op("pe", fn, reads, writes)

    for (t, b, src) in [(ident, b_ident, c_ident), (ones, b_ones, c_ones), (perm, b_perm, c_perm),
                        (maskf, b_maskf, c_maskf), (maskb, b_maskb, c_maskb), (resetm, b_resetm, c_reset),
                        (posc, b_posc, c_pos), (condT, b_condT, condT_in), (bmodT, b_bmodT, bmodT_in),
                        (nmix, b_nmix, nmixT_in), (nffn, b_nffn, nffnT_in), (fnorm, b_fnorm, fnormT_in),
                        (lbl, b_lbl, lblT_in), (convw, b_convw, convwT_in), (convb, b_convb, convbT_in),
                        (rgba, b_rgba, rgbaT_in), (rgbi, b_rgbi, rgbiT_in), (apar, b_apar, aparT_in),
                        (h0T, b_h0T, h0T_in)]:
        dma("sp", t[:], src, b, [], [b])
    dma("sp", lng[:], decl_in.partition_broadcast(128) if False else decl_in[0:1, :].to_broadcast([128, 32]), b_lng, [], [b_lng])
    P.op("dve", lambda e: e.memset(epsc[:], EPS), [], [b_epsc])
    P.op("dve", lambda e: e.memset(nsl[:], 0.0), [], [b_nsl])
    act(scond[:], condT[:], AF.Silu, [b_condT], [b_scond])
    P.op("dve", lambda e: e.memset(LB[:, 0], 0.0), [], [b_LB])
    tt("dve", lbl[:, 1], lbl[:, 1], lbl[:, 0], ALU.subtract, [b_lbl], [b_lbl])
    act(LB[:, 1], lbl[:, 1], AF.Sigmoid, [b_lbl, b_LB], [b_LB])
    ts("dve", OML[:], LB[:], -1.0, 1.0, ALU.mult, ALU.add, [b_LB], [b_OML])
    ts("dve", LBM[:], LB[:], -1.0, None, ALU.add, None, [b_LB], [b_LBM])
    act(lng[:], lng[:], AF.Sigmoid, [b_lng], [b_lng])
    act(lng[:], lng[:], AF.Ln, [b_lng], [b_lng])
    ts("dve", nlng[:], lng[:], -1.0, None, ALU.mult, None, [b_lng], [b_nlng])
    for i in range(32):
        d = (i // 8) % 2
        act(EB32[:, i, :], posc[:, d, :], AF.Exp, [b_posc, b_lng], [b_EB32], scale=lng[:, i:i + 1])
        act(ENB32[:, i, :], posc[:, d, :], AF.Exp, [b_posc, b_nlng], [b_ENB32], scale=nlng[:, i:i + 1])
    act(cA[:], apar[:], AF.Exp, [b_apar], [b_cA], scale=-1.0)
    ts("dve", cA[:], cA[:], 1.0, None, ALU.add, None, [b_cA], [b_cA])
    act(cA[:], cA[:], AF.Ln, [b_cA], [b_cA])
    ts("dve", cA2[:], cA[:], -16.0, None, ALU.mult, None, [b_cA], [b_cA2])
    ts("dve", cA[:], cA[:], -8.0, None, ALU.mult, None, [b_cA, b_cA2], [b_cA])

    wconv_slot = P.buf("wconv_slot")
    wtok = [P.buf("wtok%d" % i) for i in range(3)]
    wtokc = [0]

    class WB:
        pass

    def wprep(name, src, K, N):
        w = WB()
        w.ncg, w.nkt, w.K, w.N = N // 512, (K + 2047) // 2048, K, N
        w.t = dscr("wb_" + name, [w.ncg, w.nkt, 128, 16, 512], BF16)
        w.name, w.src, w.done = name, src, False
        return w

    def wconvert(w):
        if w.done or "conv" in skip:
            return
        w.done = True
        for cg in range(w.ncg):
            for kt in range(w.nkt):
                nk = min(16, (w.K - kt * 2048) // 128)
                s = w.src[kt * 2048:kt * 2048 + nk * 128, cg * 512:(cg + 1) * 512].rearrange("(kc p) e -> p kc e", p=128)
                o = w.t[cg, kt, :, 0:nk, :]
                tk = wtok[wtokc[0] % len(wtok)]
                wtokc[0] += 1
                dma("pool", o, s, wconv_slot, [], [db("wb_" + w.name, (cg, kt)), tk])

    def wload(w, cg, kt):
        i = wctr[0] % NW
        wctr[0] += 1
        t, b = wbufs[i]
        nk = min(16, (w.K - kt * 2048) // 128)
        dma("sp", t[:, 0:nk, :], w.t[cg, kt, :, 0:nk, :], b, [db("wb_" + w.name, (cg, kt))], [b])
        return t, b, nk

    W = {}
    for l in range(1 if lite else 4):
        if lite:
            W[("in", l)] = wprep("in%d" % l, w_in_even[l // 2], D, 9216)
            continue
        if l % 2 == 0:
            W[("in", l)] = wprep("in%d" % l, w_in_even[l // 2], D, 9216)
            W[("out", l)] = wprep("out%d" % l, w_out_even[l // 2], D, D)
        else:
            W[("in", l)] = wprep("in%d" % l, w_in_odd[l // 2], D, 2 * D)
            W[("out", l)] = wprep("out%d" % l, w_out_odd[l // 2], D, D)
        W[("g", l)] = wprep("g%d" % l, w_g[l], D, DFF)
        W[("u", l)] = wprep("u%d" % l, w_u[l], D, DFF)
        W[("d", l)] = wprep("d%d" % l, w_d[l], DFF, D)

    if lite:
        P.op("dve", lambda e: e.memset(modT[:], 0.05), [], [b_modT])
        P.op("dve", lambda e: e.memset(A1[:], 1.05), [], [b_A1])
        P.op("dve", lambda e: e.memset(A2[:], 1.05), [], [b_A2])
    for l in range(0 if lite else 4):
        pt, pb = newps()
        for cg in range(24):
            i = wctr[0] % NW
            wctr[0] += 1
            t, b = wbufs[i]
            s = w_mod[l][:, cg * 512:(cg + 1) * 512].rearrange("(kc p) e -> p kc e", p=128)
            dma("pool", t[:], s, b, [], [b])
            for j in range(4):
                ch = cg * 4 + j
                mmk(pt[:, ch * 2:ch * 2 + 2], t, slice(j * 128, (j + 1) * 128), scond, 16, [b, b_scond], [pb])
        tt("dve", modT[:, l].rearrange("p c j -> p j c"), pt[:, 0:192].rearrange("p (j c) -> p j c", c=2),
           bmodT[:, l, :].unsqueeze(2).to_broadcast([128, 96, 2]), ALU.add, [pb, b_bmodT], [b_modT])
        for c in range(2):
            stt(A1[:, l, c, :], modT[:, l, c, 16:32], 1.0, nmix[:, l, :], ALU.add, ALU.mult, [b_modT, b_nmix], [b_A1])
            stt(A2[:, l, c, :], modT[:, l, c, 64:80], 1.0, nffn[:, l, :], ALU.add, ALU.mult, [b_modT, b_nffn], [b_A2])

    def cond_of_tile(ti):
        return 0 if ti < NTS else 1

    def rmsnorm_mod(xt, b_xt, hT, b_hT, Acol, Bcol, tmp):
        (sq, b_sq), (rs, b_rs), (tm, b_tm) = tmp
        pt, pb = newps()
        for c in range(NCH):
            s_, bs_ = sq[c % 2], b_sq[c % 2]
            act(s_, xt[:, c, :], AF.Square, [b_xt], [bs_])
            mm(pt[:, :], ones[:], s_, [b_ones, bs_], [pb], start=(c == 0), stop=(c == NCH - 1))
        act(rs, pt[:, :], AF.Sqrt, [pb, b_epsc], [b_rs], scale=1.0 / D, bias=epsc[:, 0:1])
        P.op("dve", lambda e: e.reciprocal(out=rs, in_=rs), [b_rs], [b_rs])
        for c in range(NCH):
            t_, bt_ = tm[c % 2], b_tm[c % 2]
            a_, b_ = Acol(c), Bcol(c)
            stt(t_, xt[:, c, :], a_[0], rs, ALU.mult, ALU.mult, [b_xt, b_rs, a_[1]], [bt_])
            if b_ is None:
                cp("act", hT[:, c, :], t_, [bt_], [b_hT])
            else:
                act(hT[:, c, :], t_, AF.Identity, [bt_, b_[1]], [b_hT], bias=b_[0])

    def stream_T(x_src, ti):
        t0 = ti * TT
        return x_src[:, :, t0:t0 + TT]

    def phaseA_even(l, first):
        le = l // 2
        areset()
        xt, b_xt = aal("xt", [128, NCH, TT], F32)
        hT, b_hT = aal("hT", [128, NCH, TT], BF16)
        sq = [aal("sq%d" % i, [128, TT], BF16) for i in range(2)]
        rs, b_rs = aal("rs", [128, TT], F32)
        tm = [aal("tm%d" % i, [128, TT], F32) for i in range(2)]
        tmp = (([s[0] for s in sq], [s[1] for s in sq]), (rs, b_rs), ([s[0] for s in tm], [s[1] for s in tm]))
        q4, b_q4 = aal("q4", [128, 4, TT], F32)
        k4, b_k4 = aal("k4", [128, 4, TT], F32)
        R = 2
        sig = [aal("sig%d" % i, [128, TT], F32) for i in range(R)]
        lf = [aal("lf%d" % i, [128, TT], F32) for i in range(R)]
        ka = [aal("ka%d" % i, [128, TT], F32) for i in range(R)]
        bc = [aal("bc%d" % i, [128, TT], F32) for i in range(R)]
        eb = [aal("eb%d" % i, [128, TT], F32) for i in range(R)]
        enb = [aal("enb%d" % i, [128, TT], F32) for i in range(R)]
        qst = [aal("qst%d" % i, [128, TT], BF16) for i in range(R)]
        kst = [aal("kst%d" % i, [128, TT], BF16) for i in range(R)]
        kss = [aal("kss%d" % i, [128, TT], BF16) for i in range(R)]
        sst = [aal("sst%d" % i, [128, 4, 128], BF16) for i in range(R)]
        vst = [aal("vst%d" % i, [128, TT], BF16) for i in range(R)]
        gst = [aal("gst0", [128, 4, TT], BF16)] * 2
        decst, b_decst = aal("decst", [128, 16, 2, 16], F32)
        cosT, b_cosT = aal("cosT", [128, TT], F32)
        sinT, b_sinT = aal("sinT", [128, TT], F32)
        rb, r1, r2 = vst, sig, lf
        rot = [0]
        w = W[("in", l)]
        x_src = xT_in if first else XT

        def emit_hd(ti, hh, d, qsrc, bq, ksrc, bk, ebv, benb_r, enbv, decv, decr):
            if "hd" in skip:
                return
            i = rot[0] % R
            rot[0] += 1
            blk0 = ti * 4
            tgtQ, tgtK, tgtS = (QF, KF, SF) if d == 0 else (QB, KB, SB)
            q_, bq_ = qst[i]
            k_, bk_ = kst[i]
            s_, bs_ = kss[i]
            st_, bst_ = sst[i]
            v3 = lambda a: a.rearrange("p (c t) -> p c t", t=32)
            stt(v3(q_), v3(qsrc), KS, ebv, ALU.mult, ALU.mult, [bq] + benb_r, [bq_])
            tt("dve", v3(k_), v3(ksrc), enbv, ALU.mult, [bk] + benb_r, [bk_])
            tt("dve", s_.rearrange("p (c t) -> p c t", t=32), k_.rearrange("p (c t) -> p c t", t=32),
               decv, ALU.mult, [bk_] + decr, [bs_])
            dma("sp", tgtQ[blk0:blk0 + 4, :, hh, :].rearrange("b p t -> p b t"), q_.rearrange("p (b t) -> p b t", t=128),
                bq_, [bq_], [db(tgtQ.tensor.name, ti)])
            dma("sp", tgtK[blk0:blk0 + 4, :, hh, :].rearrange("b p t -> p b t"), k_.rearrange("p (b t) -> p b t", t=128),
                bk_, [bk_], [db(tgtK.tensor.name, ti)])
            for j in range(4):
                P.op("pe", (lambda j=j, s_=s_: (lambda e: e.transpose(psT[:, j * 128:(j + 1) * 128], s_[:, j * 128:(j + 1) * 128], ident[:]))),
                     [bs_, b_ident], [b_psT])
            cp("act", st_.rearrange("p b k -> p (b k)"), psT[:, 0:512], [b_psT], [bst_])
            dma("sp", tgtS[blk0:blk0 + 4, :, hh, :].rearrange("b p k -> p b k"), st_, bst_, [bst_], [db(tgtS.tensor.name, ti)])

        for ti in range(NT):
            cnd = cond_of_tile(ti)
            sample = ti < NTS
            dma("sp", xt, stream_T(x_src, ti), b_xt, [db(x_src.tensor.name, ti)], [b_xt])
            if first and "xst" not in skip:
                dma("sp", stream_T(XT, ti), xt, b_xt, [b_xt], [db("XT", ti)])
            if "norm" not in skip:
                rmsnorm_mod(xt, b_xt, hT, b_hT, lambda c: (A1[:, l, cnd, c:c + 1], b_A1),
                            lambda c: (modT[:, l, cnd, c:c + 1], b_modT), tmp)
            if sample and "cs" not in skip:
                dma("sp", cosT, c_cos[:, ti * TT:(ti + 1) * TT], b_cosT, [], [b_cosT])
                dma("sp", sinT, c_sin[:, ti * TT:(ti + 1) * TT], b_sinT, [], [b_sinT])

            def proj4(cg):
                wt, wb_, nk = wload(w, cg, 0)
                outs = []
                for j in range(4):
                    pt, pb = newps()
                    mmk(pt[:, :], wt, slice(j * 128, (j + 1) * 128), hT, 16, [wb_, b_hT], [pb])
                    outs.append((pt, pb))
                return outs

            def projV(cg, hbase):
                if "V" in skip:
                    return
                wt, wb_, nk = wload(w, cg, 0)
                for tb in range(4):
                    pt, pb = newps()

                    def fn(e, pt=pt, wt=wt, tb=tb):
                        r = None
                        for kc in range(16):
                            r = e.matmul(pt[:, :], lhsT=hT[:, kc, tb * 128:(tb + 1) * 128], rhs=wt[:, kc, :],
                                         start=(kc == 0), stop=(kc == 15))
                        return r
                    P.op("pe", fn, [wb_, b_hT], [pb])
                    i = rot[0] % R
                    rot[0] += 1
                    v_, bv_ = vst[i]
                    cp("act", v_, pt[:, :], [pb], [bv_])
                    dma("sp", VV[ti * 4 + tb, :, hbase:hbase + 4, :], v_.rearrange("p (h v) -> p h v", v=128), bv_, [bv_],
                        [db("VV", ti)])

            def projG(cg, chbase):
                if "G" in skip:
                    return
                outs = proj4(cg)
                i = rot[0] % R
                rot[0] += 1
                g_, bg_ = gst[i]
                for j, (pt, pb) in enumerate(outs):
                    act(g_[:, j, :], pt[:, :], AF.Silu, [pb], [bg_])
                dma("sp", GG[:, chbase:chbase + 4, ti * TT:(ti + 1) * TT], g_, bg_, [bg_], [db("GG", ti)])

            for hq in range(0 if "A" in skip else 2):
                outs = proj4(0 + hq)
                for j, (pt, pb) in enumerate(outs):
                    act(q4[:, j, :], pt[:, :], AF.Silu, [pb], [b_q4])
                for d in range(2):
                    outs = proj4(2 + 2 * d + hq)
                    for j, (pt, pb) in enumerate(outs):
                        h = hq * 4 + j
                        i = rot[0] % R
                        sg, bsg = sig[i]
                        l_, bl_ = lf[i]
                        a_, ba_ = ka[i]
                        c_, bc_ = bc[i]
                        e1, be1 = eb[i]
                        e2, be2 = enb[i]
                        act(sg, pt[:, :], AF.Sigmoid, [pb], [bsg])
                        act(l_, sg, AF.Ln, [bsg, b_OML, b_LB], [bl_], scale=OML[:, le, d, h:h + 1], bias=LB[:, le, d, h:h + 1])
                        ts("dve", a_, sg, LBM[:, le, d, h:h + 1], OML[:, le, d, h:h + 1], ALU.mult, ALU.add,
                           [bsg, b_LBM, b_OML], [ba_])
                        P.op("dve", (lambda c_=c_, l_=l_: (lambda e: e.tensor_tensor_scan(out=c_, data0=resetm[:], data1=l_, initial=0.0,
                                                                                          op0=ALU.mult, op1=ALU.add))),
                             [bl_, b_resetm], [bc_])
                        c3 = c_.rearrange("p (c t) -> p c t", t=32)
                        if d == 1:
                            tt("dve", l_, l_, c_, ALU.subtract, [bl_, bc_], [bl_])
                            tt("dve", c3, l_.rearrange("p (c t) -> p c t", t=32), c3[:, :, 31:32].to_broadcast([128, 16, 32]),
                               ALU.add, [bl_, bc_], [bc_])
                        ts("dve", c_, c_, -80.0, None, ALU.max, None, [bc_], [bc_])
                        act(e1, c_, AF.Exp, [bc_], [be1])
                        act(e2, c_, AF.Exp, [bc_], [be2], scale=-1.0)
                        e13 = e1.rearrange("p (c t) -> p c t", t=32)
                        dcol = e13[:, :, 31:32] if d == 0 else e13[:, :, 0:1]
                        cp("dve", decst[:, h, d, :].unsqueeze(2), dcol, [be1], [b_decst])
                        emit_hd(ti, h, d, q4[:, j, :], b_q4, a_, ba_, e13, [be1, be2],
                                e2.rearrange("p (c t) -> p c t", t=32), dcol.to_broadcast([128, 16, 32]), [be1])
                projV(6 + hq, hq * 4)
                projG(8 + hq, hq * 4)
            for hq in range(0 if "B" in skip else 2):
                for (cgb, dst, bdst) in ((10, q4, b_q4), (12, k4, b_k4)):
                    outs = proj4(cgb + hq)
                    for j, (pt, pb) in enumerate(outs):
                        if not sample or "rope" in skip:
                            cp("act", dst[:, j, :], pt[:, :], [pb], [bdst])
                            continue
                        i = rot[0] % R
                        rot[0] += 1
                        rb_, brb = rb[i]
                        a1, ba1 = r1[i]
                        a2, ba2 = r2[i]
                        cp("act", a1, pt[:, :], [pb], [ba1])
                        cp("dve", rb_, a1, [ba1], [brb])
                        p2, pb2 = newps()
                        mm(p2[:, :], perm[:], rb_, [b_perm, brb], [pb2])
                        tt("dve", a2, p2[:, :], sinT, ALU.mult, [pb2, b_sinT], [ba2])
                        tt("dve", a1, a1, cosT, ALU.mult, [ba1, b_cosT], [ba1])
                        tt("dve", dst[:, j, :], a1, a2, ALU.add, [ba1, ba2], [bdst])
                for j in range(4):
                    hb = hq * 4 + j
                    for d in range(2):
                        ix = (le * 2 + d) * 8 + hb
                        ebv = EB32[:, ix, :].unsqueeze(1).to_broadcast([128, 16, 32])
                        enbv = ENB32[:, ix, :].unsqueeze(1).to_broadcast([128, 16, 32])
                        dcol = EB32[:, ix, 31:32] if d == 0 else EB32[:, ix, 0:1]
                        if "decb" not in skip:
                            P.op("dve", (lambda hb=hb, d=d, dcol=dcol: (lambda e: e.tensor_copy(out=decst[:, 8 + hb, d, :],
                                                                                                 in_=dcol.to_broadcast([128, 16])))),
                                 [b_EB32], [b_decst])
                        emit_hd(ti, 8 + hb, d, q4[:, j, :], b_q4, k4[:, j, :], b_k4, ebv, [b_EB32, b_ENB32], enbv,
                                dcol.unsqueeze(2).to_broadcast([128, 16, 32]), [b_EB32])
                projV(14 + hq, 8 + hq * 4)
                projG(16 + hq, 8 + hq * 4)
            if "dec" not in skip:
                dma("sp", DEC[ti], decst, b_decst, [b_decst], [db("DEC", ti)])


    def phaseS_even(l):
        le = l // 2
        areset()
        S32, b_S = aal("S32", [128, 32, 128], F32)
        Sb, b_Sb = aal("Sb", [128, 32, 128], BF16)
        NBF = 2
        lds = [[aal("ld%d_%d" % (i, k), [128, 16, 128], BF16) for k in range(8)] for i in range(NBF)]
        decs = [aal("decs%d" % i, [128, 16, 2, 16], F32) for i in range(2)]
        ost = [[aal("ost_%d" % d, [128, 16, 128], F32) for d in range(2)]] * 2
        pm = [aal("pm%d" % i, [128, 128], BF16) for i in range(4)]
        rot = [0]
        bS_h = [P.buf("S_%d" % si, "arena", b_S.lo + si * 512, b_S.lo + (si + 1) * 512) for si in range(32)]
        bSb_h = [P.buf("Sb_%d" % si, "arena", b_Sb.lo + si * 256, b_Sb.lo + (si + 1) * 256) for si in range(32)]
        bo_h = [[P.buf("o_%d_%d" % (d, h), "arena", ost[0][d][1].lo + h * 512, ost[0][d][1].lo + (h + 1) * 512) for h in range(16)]
                for d in range(2)]
        for (t0, L, pi) in seqs:
            nblk = L // 128
            b0 = t0 // 128
            if pi is None:
                for d in range(2):
                    dma("sp", S32[:, d * 16:d * 16 + 8, :], st_h_in[le, d].rearrange("h k v -> k h v"), b_S, [], [b_S])
                    dma("sp", S32[:, d * 16 + 8:d * 16 + 16, :], st_r_in[le, d].rearrange("h k v -> k h v"), b_S, [], [b_S])
            else:
                P.op("dve", lambda e: e.memset(S32[:], 0.0), [], [b_S])
            cp("act", Sb[:], S32[:], [b_S], [b_Sb])
            for i in range(nblk):
                fb, bb = b0 + i, b0 + nblk - 1 - i
                L_ = lds[i % NBF]
                srcs = [(QF, fb), (KF, fb), (SF, fb), (VV, fb), (QB, bb), (KB, bb), (SB, bb), (VV, bb)]
                for k, (src, blk) in enumerate(srcs):
                    dma("sp", L_[k][0], src[blk], L_[k][1], [db(src.tensor.name, blk // 4)], [L_[k][1]])
                dcs = []
                for d, blk in ((0, fb), (1, bb)):
                    dc_, bdc_ = decs[d]
                    if i == 0 or (blk % 4 == (0 if d == 0 else 3)):
                        dma("sp", dc_, DEC[blk // 4], bdc_, [db("DEC", blk // 4)], [bdc_])
                    dcs.append((dc_, bdc_))
                for d in range(2):
                    q_, bq_ = L_[4 * d + 0]
                    k_, bk_ = L_[4 * d + 1]
                    s_, bs_ = L_[4 * d + 2]
                    v_, bv_ = L_[4 * d + 3]
                    blk = fb if d == 0 else bb
                    o_, bo_ = ost[i % 2][d]
                    msk, bmsk = (maskf, b_maskf) if d == 0 else (maskb, b_maskb)
                    dc_, bdc_ = dcs[d]

                    def hbody(h, slot, d=d, q_=q_, bq_=bq_, k_=k_, bk_=bk_, s_=s_, bs_=bs_, v_=v_, bv_=bv_, blk=blk, o_=o_,
                              msk=msk, bmsk=bmsk, dc_=dc_, bdc_=bdc_):
                        sp_, bsp_ = psums[slot * 3 + 0]
                        op_, bop_ = psums[slot * 3 + 1]
                        p_, bp_ = pm[rot[0] % 4]
                        rot[0] += 1
                        si = d * 16 + h
                        bSi, bSbi, boh = bS_h[si], bSb_h[si], bo_h[d][h]
                        mm(sp_[:, 0:128], k_[:, h, :], q_[:, h, :], [bk_, bq_], [bsp_])
                        yield
                        tt("dve", p_, sp_[:, 0:128], msk[:], ALU.mult, [bsp_, bmsk], [bp_])
                        yield
                        mm(op_[:, 0:128], v_[:, h, :], p_, [bv_, bp_], [bop_], start=True, stop=False)
                        corder = range(4) if d == 0 else range(3, -1, -1)
                        for n_, c in enumerate(corder):
                            mm(op_[:, c * 32:(c + 1) * 32], Sb[:, si, :], q_[:, h, c * 32:(c + 1) * 32], [bSbi, bq_], [bop_],
                               start=False, stop=(n_ == 3))
                            ds_, bds_ = psums[slot * 3 + 2]
                            tp = (96, 0) if c == 3 else None
                            mm(ds_[:, 0:128], s_[c * 32:(c + 1) * 32, h, :], v_[c * 32:(c + 1) * 32, h, :], [bs_, bv_], [bds_], tp=tp)
                            yield
                            cidx = (blk % 4) * 4 + c
                            stt(S32[:, si, :], S32[:, si, :], dc_[:, h, d, cidx:cidx + 1], ds_[:, 0:128], ALU.mult, ALU.add,
                                [bSi, bdc_, bds_], [bSi])
                            yield
                            cp("act", Sb[:, si, :], S32[:, si, :], [bSi], [bSbi])
                            yield
                        cp("act", o_[:, h, :], op_[:, 0:128], [bop_], [boh])
                        yield
                    lockstep(((lambda slot, h=h: hbody(h, slot)) for h in range(16)), 2)
                    tgt = OF if d == 0 else OB
                    dma("sp", tgt[:, :, blk * 128:(blk + 1) * 128], o_, ost[i % 2][d][1], bo_h[d], [db(tgt.tensor.name, blk // 4)])
            if pi is not None:
                for d in range(2):
                    dma("sp", nsh_out[pi, le, d].rearrange("h k v -> k h v"), S32[:, d * 16:d * 16 + 8, :], b_S, [b_S],
                        [db("nsh", (pi, le, d))])
                    dma("sp", nsr_out[pi, le, d].rearrange("h k v -> k h v"), S32[:, d * 16 + 8:d * 16 + 16, :], b_S, [b_S],
                        [db("nsr", (pi, le, d))])

    def phaseB(l, last):
        even = (l % 2 == 0)
        areset()
        xt, b_xt = aal("xt", [128, NCH, TT], F32)
        hT, b_hT = aal("hT", [128, NCH, TT], BF16)
        actT, b_actT = aal("actT", [128, NFC, TT], BF16)
        sq = [aal("sq%d" % i, [128, TT], BF16) for i in range(2)]
        rs, b_rs = aal("rs", [128, TT], F32)
        tm = [aal("tm%d" % i, [128, TT], F32) for i in range(2)]
        tmp = (([s[0] for s in sq], [s[1] for s in sq]), (rs, b_rs), ([s[0] for s in tm], [s[1] for s in tm]))
        of_ = [aal("of%d" % i, [128, TT], F32) for i in range(2)]
        ob_ = [aal("ob%d" % i, [128, TT], F32) for i in range(2)]
        gg_ = [aal("gg%d" % i, [128, TT], BF16) for i in range(2)]
        oo_ = [aal("oo%d" % i, [128, TT], F32) for i in range(2)]
        r2_ = [aal("r2%d" % i, [128, TT], F32) for i in range(2)]
        sg_ = r2_
        wo, wg_, wu_, wd_ = W[("out", l)], W[("g", l)], W[("u", l)], W[("d", l)]
        for ti in range(NT):
            cnd = cond_of_tile(ti)
            tsl = slice(ti * TT, (ti + 1) * TT)
            dma("sp", xt, XT[:, :, tsl], b_xt, [db("XT", ti)], [b_xt])
            if even:
                for c in range(NCH):
                    i = c % 2
                    dma("sp", of_[i][0], OF[:, c, tsl], of_[i][1], [db("OF", ti)], [of_[i][1]])
                    dma("sp", ob_[i][0], OB[:, c, tsl], ob_[i][1], [db("OB", ti)], [ob_[i][1]])
                    dma("sp", gg_[i][0], GG[:, c, tsl], gg_[i][1], [db("GG", ti)], [gg_[i][1]])
                    o, bo = oo_[i]
                    tt("dve", o, of_[i][0], ob_[i][0], ALU.add, [of_[i][1], ob_[i][1]], [bo])
                    act(sq[i][0], o, AF.Square, [bo], [sq[i][1]])
                    pt, pb = newps()
                    mm(pt[:, :], ones[:], sq[i][0], [b_ones, sq[i][1]], [pb])
                    r_, br_ = r2_[i]
                    act(r_, pt[:, :], AF.Sqrt, [pb, b_epsc], [br_], scale=1.0 / 128, bias=epsc[:, 0:1])
                    P.op("dve", (lambda r_=r_: (lambda e: e.reciprocal(out=r_, in_=r_))), [br_], [br_])
                    tt("dve", o, o, r_, ALU.mult, [bo, br_], [bo])
                    tt("dve", hT[:, c, :], o, gg_[i][0], ALU.mult, [bo, gg_[i][1]], [b_hT])
            else:
                dma("sp", hT, YO[:, :, tsl], b_hT, [db("YO", ti)], [b_hT])
            for cg in range(4):
                wt, wb_, nk = wload(wo, cg, 0)
                for j in range(4):
                    ch = cg * 4 + j
                    pt, pb = newps()
                    mmk(pt[:, :], wt, slice(j * 128, (j + 1) * 128), hT, 16, [wb_, b_hT], [pb])
                    stt(xt[:, ch, :], pt[:, :], modT[:, l, cnd, 32 + ch:33 + ch], xt[:, ch, :], ALU.mult, ALU.add,
                        [pb, b_modT, b_xt], [b_xt])
            rmsnorm_mod(xt, b_xt, hT, b_hT, lambda c: (A2[:, l, cnd, c:c + 1], b_A2),
                        lambda c: (modT[:, l, cnd, 48 + c:49 + c], b_modT), tmp)
            for jg in range(11):
                wtg, wbg, _ = wload(wg_, jg, 0)
                wtu, wbu, _ = wload(wu_, jg, 0)
                for j in range(4):
                    fc = jg * 4 + j
                    pg, pbg = newps()
                    pu, pbu = newps()
                    mmk(pg[:, :], wtg, slice(j * 128, (j + 1) * 128), hT, 16, [wbg, b_hT], [pbg])
                    mmk(pu[:, :], wtu, slice(j * 128, (j + 1) * 128), hT, 16, [wbu, b_hT], [pbu])
                    s_, bs_ = sg_[fc % 2]
                    act(s_, pg[:, :], AF.Silu, [pbg], [bs_])
                    tt("dve", actT[:, fc, :], pu[:, :], s_, ALU.mult, [pbu, bs_], [b_actT])
            for cg in range(4):
                pts = [newps() for _ in range(4)]
                for kt in range(3):
                    wt, wb_, nk = wload(wd_, cg, kt)
                    for j in range(4):
                        mmk(pts[j][0][:, :], wt, slice(j * 128, (j + 1) * 128), actT, nk, [wb_, b_actT], [pts[j][1]],
                            start=(kt == 0), stop=(kt == 2), k0=kt * 16)
                for j in range(4):
                    ch = cg * 4 + j
                    stt(xt[:, ch, :], pts[j][0][:, :], modT[:, l, cnd, 80 + ch:81 + ch], xt[:, ch, :], ALU.mult, ALU.add,
                        [pts[j][1], b_modT, b_xt], [b_xt])
            if not last:
                dma("sp", XT[:, :, tsl], xt, b_xt, [b_xt], [db("XT", ti)])
            else:
                yt, b_yt = actT[:, 0:32, :].bitcast(F32) if False else (None, None)
                (sqv, b_sqv), (rsv, b_rsv), (tmv, b_tmv) = tmp
                pt, pb = newps()
                for c in range(NCH):
                    act(sqv[c % 2], xt[:, c, :], AF.Square, [b_xt], [b_sqv[c % 2]])
                    mm(pt[:, :], ones[:], sqv[c % 2], [b_ones, b_sqv[c % 2]], [pb], start=(c == 0), stop=(c == NCH - 1))
                act(rsv, pt[:, :], AF.Sqrt, [pb, b_epsc], [b_rsv], scale=1.0 / D, bias=epsc[:, 0:1])
                P.op("dve", lambda e: e.reciprocal(out=rsv, in_=rsv), [b_rsv], [b_rsv])
                for c in range(NCH):
                    stt(xt[:, c, :], xt[:, c, :], fnorm[:, c:c + 1], rsv, ALU.mult, ALU.mult, [b_xt, b_fnorm, b_rsv], [b_xt])
                dma("sp", yT_out[:, :, tsl], xt, b_xt, [b_xt], [db("yT", ti)])

    def phaseA_odd(l):
        lo = l // 2
        areset()
        xt, b_xt = aal("xt", [128, NCH, TT], F32)
        hT, b_hT = aal("hT", [128, NCH, TT], BF16)
        sq = [aal("sq%d" % i, [128, TT], BF16) for i in range(2)]
        rs, b_rs = aal("rs", [128, TT], F32)
        tm = [aal("tm%d" % i, [128, TT], F32) for i in range(2)]
        tmp = (([s[0] for s in sq], [s[1] for s in sq]), (rs, b_rs), ([s[0] for s in tm], [s[1] for s in tm]))
        gx = [aal("gx%d" % i, [128, TT], F32) for i in range(2)]
        g2 = [aal("g2%d" % i, [128, TT], F32) for i in range(2)]
        gs = [aal("gs%d" % i, [128, 4, TT], BF16) for i in range(2)]
        xs = [aal("xs%d" % i, [128, 4, TT], F32) for i in range(2)]
        w = W[("in", l)]
        for ti in range(NT):
            cnd = cond_of_tile(ti)
            tsl = slice(ti * TT, (ti + 1) * TT)
            dma("sp", xt, XT[:, :, tsl], b_xt, [db("XT", ti)], [b_xt])
            rmsnorm_mod(xt, b_xt, hT, b_hT, lambda c: (A1[:, l, cnd, c:c + 1], b_A1),
                        lambda c: (modT[:, l, cnd, c:c + 1], b_modT), tmp)
            for cg in range(8):
                wt, wb_, nk = wload(w, cg, 0)
                st_, bst_ = (gs if cg < 4 else xs)[cg % 2]
                for j in range(4):
                    pt, pb = newps()
                    mmk(pt[:, :], wt, slice(j * 128, (j + 1) * 128), hT, 16, [wb_, b_hT], [pb])
                    if cg < 4:
                        x_, bx_ = gx[j % 2]
                        y_, by_ = g2[j % 2]
                        cp("act", x_, pt[:, :], [pb], [bx_])
                        act(y_, pt[:, :], AF.Square, [pb], [by_])
                        ts("dve", y_, y_, 0.044715, 1.0, ALU.mult, ALU.add, [by_], [by_])
                        tt("dve", y_, y_, x_, ALU.mult, [by_, bx_], [by_])
                        act(y_, y_, AF.Sigmoid, [by_], [by_], scale=1.5957691216057308)
                        tt("dve", st_[:, j, :], y_, x_, ALU.mult, [by_, bx_], [bst_])
                    else:
                        cp("act", st_[:, j, :], pt[:, :], [pb], [bst_])
                if cg < 4:
                    dma("sp", GG[:, cg * 4:cg * 4 + 4, tsl], st_, bst_, [bst_], [db("GG", ti)])
                else:
                    dma("sp", XBR[:, (cg - 4) * 4:(cg - 4) * 4 + 4, tsl], st_, bst_, [bst_], [db("XBR", ti)])

    def phaseS_odd(l):
        lo = l // 2
        areset()
        rw = [[aal("rw%d_%d" % (d, g), [128, 16, 128], BF16) for g in range(2)] for d in range(2)]
        for d in range(2):
            for g, src in enumerate((rg_w_a, rg_w_i)):
                dma("pool", rw[d][g][0], src[lo, d].rearrange("n i j -> i n j"), rw[d][g][1], [], [rw[d][g][1]])
        xh = [aal("xh%d" % i, [128, TT + 4], F32) for i in range(3)]
        xc = [aal("xc%d" % i, [128, TT], F32) for i in range(3)]
        xcb = [aal("xcb%d" % i, [128, TT], BF16) for i in range(3)]
        rr = [aal("rr%d" % i, [128, TT], F32) for i in range(3)]
        gi = [aal("gi%d" % i, [128, TT], F32) for i in range(3)]
        aa = [aal("aa%d" % i, [128, TT], F32) for i in range(3)]
        a2 = [aal("a2%d" % i, [128, TT], F32) for i in range(3)]
        uu = [aal("uu%d" % i, [128, TT], F32) for i in range(3)]
        ar = [aal("ar%d" % i, [128, TT], F32) for i in range(3)]
        ur = [aal("ur%d" % i, [128, TT], F32) for i in range(3)]
        hh = [aal("hh%d" % i, [128, TT], F32) for i in range(3)]
        hf = [aal("hf%d" % i, [128, TT], F32) for i in range(3)]
        gg = [aal("gg%d" % i, [128, TT], BF16) for i in range(3)]
        yo = [aal("yo%d" % i, [128, TT], BF16) for i in range(3)]
        rot = [0]
        b_hcar_c = [P.buf("hcar_%d" % c) for c in range(NCH)]
        for d in range(2):
            for (t0, L, pi) in seqs:
                SEG = min(TT, L)
                nseg = L // SEG
                order = range(nseg) if d == 0 else range(nseg - 1, -1, -1)
                for n_, sgi in enumerate(order):
                    s0 = t0 + sgi * SEG
                    ti = s0 // TT
                    def cbody(c, i, d=d, t0=t0, L=L, pi=pi, SEG=SEG, nseg=nseg, n_=n_, sgi=sgi, s0=s0, ti=ti):
                        b_hcar = b_hcar_c[c]
                        xh_, bxh_ = xh[i]
                        lh = 2 if sgi > 0 else 0
                        rh = 1 if sgi < nseg - 1 else 0
                        P.op("dve", (lambda xh_=xh_: (lambda e: e.memset(xh_, 0.0))), [], [bxh_])
                        dma("sp", xh_[:, 2 - lh:2 + SEG + rh], XBR[:, c, s0 - lh:s0 + SEG + rh], bxh_,
                            [db("XBR", ti), db("XBR", max(ti - 1, 0)), db("XBR", min(ti + 1, NT - 1))], [bxh_])
                        xc_, bxc_ = xc[i]
                        xc_ = xc_[:, 0:SEG]
                        ts("dve", xc_, xh_[:, 0:SEG], convw[:, lo, 0, c:c + 1], convb[:, lo, c:c + 1], ALU.mult, ALU.add,
                           [bxh_, b_convw, b_convb], [bxc_])
                        for j in range(1, 4):
                            stt(xc_, xh_[:, j:j + SEG], convw[:, lo, j, c:c + 1], xc_, ALU.mult, ALU.add, [bxh_, b_convw, bxc_], [bxc_])
                        yield
                        xb_, bxb_ = xcb[i]
                        xb_ = xb_[:, 0:SEG]
                        cp("act", xb_, xc_, [bxc_], [bxb_])
                        pr, pbr = newps()
                        pg, pbg = newps()
                        mm(pr[:, 0:SEG], rw[d][0][0][:, c, :], xb_, [rw[d][0][1], bxb_], [pbr])
                        mm(pg[:, 0:SEG], rw[d][1][0][:, c, :], xb_, [rw[d][1][1], bxb_], [pbg])
                        yield
                        r_, br_ = rr[i]
                        g_, bg_ = gi[i]
                        a_, ba_ = aa[i]
                        q_, bq_ = a2[i]
                        u_, bu_ = uu[i]
                        r_, g_, a_, q_, u_ = [z[:, 0:SEG] for z in (r_, g_, a_, q_, u_)]
                        act(r_, pr[:, 0:SEG], AF.Sigmoid, [pbr, b_rgba], [br_], bias=rgba[:, lo, d, c:c + 1])
                        act(g_, pg[:, 0:SEG], AF.Sigmoid, [pbg, b_rgbi], [bg_], bias=rgbi[:, lo, d, c:c + 1])
                        act(a_, r_, AF.Exp, [br_, b_cA], [ba_], scale=cA[:, lo, d, c:c + 1])
                        act(q_, r_, AF.Exp, [br_, b_cA2], [bq_], scale=cA2[:, lo, d, c:c + 1])
                        yield
                        ts("dve", q_, q_, -1.0, 1.0, ALU.mult, ALU.add, [bq_], [bq_])
                        ts("dve", q_, q_, 0.0, None, ALU.max, None, [bq_], [bq_])
                        act(q_, q_, AF.Sqrt, [bq_], [bq_])
                        yield
                        tt("dve", u_, g_, xc_, ALU.mult, [bg_, bxc_], [bu_])
                        tt("dve", u_, u_, q_, ALU.mult, [bu_, bq_], [bu_])
                        yield
                        h_, bh_ = hh[i]
                        h_ = h_[:, 0:SEG]
                        if n_ == 0:
                            if pi is None:
                                init, binit = h0T[:, lo, d, c:c + 1], b_h0T
                            else:
                                init, binit = 0.0, None
                        else:
                            init, binit = hcar[:, c:c + 1], b_hcar
                        rds = [ba_, bu_] + ([binit] if binit is not None else [])
                        if d == 0:
                            P.op("dve", (lambda h_=h_, a_=a_, u_=u_, init=init: (lambda e: e.tensor_tensor_scan(
                                out=h_, data0=a_, data1=u_, initial=init, op0=ALU.mult, op1=ALU.add))), rds, [bh_])
                            cp("dve", hcar[:, c:c + 1], h_[:, SEG - 1:SEG], [bh_], [b_hcar])
                            dma("sp", HF[:, c, s0:s0 + SEG], h_, bh_, [bh_], [db("HF", ti)])
                            if pi is not None and n_ == nseg - 1:
                                cp("dve", nsl[:, pi, lo, 0, c:c + 1], h_[:, SEG - 1:SEG], [bh_], [b_nsl])
                        else:
                            ar_, bar_ = ar[i]
                            ur_, bur_ = ur[i]
                            ar_, ur_ = ar_[:, 0:SEG], ur_[:, 0:SEG]
                            cp("dve", ar_, a_[:, ::-1], [ba_], [bar_])
                            cp("dve", ur_, u_[:, ::-1], [bu_], [bur_])
                            rds = [bar_, bur_] + ([binit] if binit is not None else [])
                            P.op("dve", (lambda h_=h_, ar_=ar_, ur_=ur_, init=init: (lambda e: e.tensor_tensor_scan(
                                out=h_, data0=ar_, data1=ur_, initial=init, op0=ALU.mult, op1=ALU.add))), rds, [bh_])
                            cp("dve", hcar[:, c:c + 1], h_[:, SEG - 1:SEG], [bh_], [b_hcar])
                            if pi is not None and n_ == nseg - 1:
                                cp("dve", nsl[:, pi, lo, 1, c:c + 1], h_[:, SEG - 1:SEG], [bh_], [b_nsl])
                            f_, bf_ = hf[i]
                            f_ = f_[:, 0:SEG]
                            gg_, bgg_ = gg[i]
                            gg_ = gg_[:, 0:SEG]
                            y_, by_ = yo[i]
                            y_ = y_[:, 0:SEG]
                            dma("sp", f_, HF[:, c, s0:s0 + SEG], bf_, [db("HF", ti)], [bf_])
                            dma("sp", gg_, GG[:, c, s0:s0 + SEG], bgg_, [db("GG", ti)], [bgg_])
                            tt("dve", f_, f_, h_[:, ::-1], ALU.add, [bf_, bh_], [bf_])
                            tt("dve", y_, f_, gg_, ALU.mult, [bf_, bgg_], [by_])
                            dma("sp", YO[:, c, s0:s0 + SEG], y_, by_, [by_], [db("YO", ti)])
                        yield
                    lockstep(((lambda slot, c=c: cbody(c, slot)) for c in range(NCH)), 3)

    class Stop(Exception):
        pass

    def chk(tag):
        if stop == tag:
            raise Stop()
    try:
        chk("pre")
        for l in range(4):
            for k in (("in",) if lite else ("in", "out", "g", "u", "d")):
                wconvert(W[(k, l)])
            if l % 2 == 0:
                phaseA_even(l, l == 0)
                chk("A%d" % l)
                phaseS_even(l)
            else:
                phaseA_odd(l)
                chk("A%d" % l)
                phaseS_odd(l)
            chk("S%d" % l)
            phaseB(l, l == 3)
            chk("B%d" % l)
    except Stop:
        pass
    if "modT" in dump:
        dmod = dout("modT_o", [128, 4, 2, 96])
        dma("sp", dmod, modT[:], b_modT, [b_modT], [db("yT", "modT")])
    dma("sp", nsl_out, nsl[:], b_nsl, [b_nsl], [db("nsl_out", 0)])
    outs = [b for (k, b) in dbufs.items() if k[0] in ("yT", "nsh", "nsr", "nsl_out")]
    P.op("sp", lambda e: e.nop(), outs, [])

    P.resolve()
    slots = P.dma_slots()
    sems = {}
    for k in ("pe", "act", "dve", "pool", "sp"):
        sems[("e", k)] = es.enter_context(nc.semaphore("s_" + k))
    for n_, sid in enumerate(slots):
        sems[("d", sid)] = es.enter_context(nc.semaphore("d%s%d" % sid))
    with nc.Block() as block:
        @block.sync
        def _(e):
            P.emit("sp", e, sems)

        @block.tensor
        def _(e):
            P.emit("pe", e, sems)

        @block.scalar
        def _(e):
            P.emit("act", e, sems)

        @block.vector
        def _(e):
            P.emit("dve", e, sems)

        @block.gpsimd
        def _(e):
            P.emit("pool", e, sems)
    es.close()
    return nc


def _fm(v):
    v = np.asarray(v, np.float32)
    lead = v.shape[:-1]
    v = v.reshape(lead + (16, 128))
    return np.ascontiguousarray(np.moveaxis(v, -1, 0))


def _consts(LS):
    bf = ml_dtypes.bfloat16
    c = {}
    c["c_ident"] = np.eye(128, dtype=np.float32).astype(bf)
    c["c_ones"] = np.ones((128, 128), np.float32).astype(bf)
    i = np.arange(128)
    partner = np.where((i % 64) < 32, i + 32, i - 32)
    pm = np.zeros((128, 128), np.float32)
    pm[partner, i] = 1.0
    c["c_perm"] = pm.astype(bf)
    s, t = np.meshgrid(i, i, indexing="ij")
    same = (s // 32) == (t // 32)
    c["c_maskf"] = (same & (s <= t)).astype(np.float32).astype(bf)
    c["c_maskb"] = (same & (s >= t)).astype(np.float32).astype(bf)
    r = np.ones((128, TT), np.float32)
    r[:, 0::32] = 0.0
    c["c_reset"] = r
    pos = np.zeros((128, 2, 32), np.float32)
    pos[:, 0, :] = np.arange(1, 33)
    pos[:, 1, :] = 32 - np.arange(32)
    c["c_pos"] = pos
    tt_ = np.arange(LS)
    rows = (tt_ // 64).astype(np.float32)
    cols = (tt_ % 64).astype(np.float32)
    inv = (10000.0 ** (-np.arange(32, dtype=np.float32) / 32)).astype(np.float32)
    cosT = np.zeros((128, LS), np.float32)
    sinT = np.zeros((128, LS), np.float32)
    for f in range(128):
        p = rows if f < 64 else cols
        ang = (p * inv[(f % 64) % 32]).astype(np.float32)
        cosT[f] = np.cos(ang)
        sinT[f] = np.sin(ang) * (-1.0 if (f % 64) < 32 else 1.0)
    c["c_cos"] = cosT
    c["c_sin"] = sinT
    return c


_NC_CACHE = {}


def kernel(**inp):
    nc, in_maps, post = _prep(inp)
    res = run_bass_kernel_spmd(nc, in_maps, core_ids=list(range(len(in_maps))))
    return post(res)


def _prep(inp, **bkw):
    x_prompt = np.asarray(inp["x_prompt"], np.float32)
    x_sample = np.asarray(inp["x_sample"], np.float32)
    NS, LS = x_sample.shape[0], x_sample.shape[1]
    NPT, LP = x_prompt.shape[0], x_prompt.shape[1]
    n = NS
    NP = NPT // n
    key = (LS, NP, LP)
    if bkw:
        nc = build(LS, NP, LP, **bkw)
    else:
        if key not in _NC_CACHE:
            _NC_CACHE[key] = build(LS, NP, LP)
        nc = _NC_CACHE[key]
    consts = _consts(LS)
    shared = {}
    for k in ("w_mod", "w_in_even", "w_out_even", "w_in_odd", "w_out_odd", "w_ffn_gate", "w_ffn_up", "w_ffn_down",
              "rg_w_a", "rg_w_i"):
        shared[k] = np.ascontiguousarray(np.asarray(inp[k], np.float32))
    shared["bmodT"] = np.ascontiguousarray(np.asarray(inp["b_mod"], np.float32).reshape(4, 96, 128).transpose(2, 0, 1))
    shared["nmixT"] = _fm(inp["norm_mix"])
    shared["nffnT"] = _fm(inp["norm_ffn"])
    shared["fnormT"] = _fm(inp["final_norm"])
    shared["lblT"] = np.ascontiguousarray(np.asarray(inp["hgrn_lb_logits"], np.float32).reshape(2, 2, 8, 128).transpose(3, 0, 1, 2))
    shared["decl"] = np.ascontiguousarray(np.asarray(inp["ret_decay_logit"], np.float32).reshape(1, 32))
    shared["convwT"] = _fm(inp["conv_w"])
    shared["convbT"] = _fm(inp["conv_b"])
    shared["rgbaT"] = _fm(np.asarray(inp["rg_b_a"], np.float32).reshape(2, 2, 2048))
    shared["rgbiT"] = _fm(np.asarray(inp["rg_b_i"], np.float32).reshape(2, 2, 2048))
    shared["aparT"] = _fm(inp["rg_a_param"])
    shared.update(consts)
    c = np.asarray(inp["c"], np.float32)
    c_ctx = np.asarray(inp["c_ctx"], np.float32)
    in_maps = []
    for b in range(n):
        X = np.concatenate([x_sample[b]] + [x_prompt[b * NP + i] for i in range(NP)], axis=0)
        m = dict(shared)
        m["xT"] = np.ascontiguousarray(X.reshape(-1, 16, 128).transpose(2, 1, 0))
        m["condT"] = _fm(np.stack([c[b], c_ctx], axis=0)).transpose(0, 2, 1).copy()
        m["st_h"] = np.ascontiguousarray(np.asarray(inp["state_hgrn"], np.float32)[b])
        m["st_r"] = np.ascontiguousarray(np.asarray(inp["state_ret"], np.float32)[b])
        m["h0T"] = _fm(np.asarray(inp["state_rglru"], np.float32)[b])
        in_maps.append(m)
    def post(res):
        return _post(res, n, NP, NPT, NS, LS, LP)
    return nc, in_maps, post


def _post(res, n, NP, NPT, NS, LS, LP):
    y_prompt = np.zeros((NPT, LP, D), np.float32)
    y_sample = np.zeros((NS, LS, D), np.float32)
    nsh = np.zeros((NPT, 2, 2, 8, 128, 128), np.float32)
    nsr = np.zeros((NPT, 2, 2, 8, 128, 128), np.float32)
    nsl = np.zeros((NPT, 2, 2, D), np.float32)
    for b in range(n):
        r = res.results[b]
        Y = np.asarray(r["yT"]).transpose(2, 1, 0).reshape(-1, D)
        y_sample[b] = Y[:LS]
        for i in range(NP):
            y_prompt[b * NP + i] = Y[LS + i * LP:LS + (i + 1) * LP]
        nsh[b * NP:(b + 1) * NP] = np.asarray(r["nsh"])
        nsr[b * NP:(b + 1) * NP] = np.asarray(r["nsr"])
        nsl[b * NP:(b + 1) * NP] = np.asarray(r["nsl"]).transpose(1, 2, 3, 4, 0).reshape(NP, 2, 2, D)
    return (y_prompt, y_sample, nsh, nsr, nsl)
```
